# Optimizing a Trainium2 kernel written in Bass

```python
import math
import jax
import jax.numpy as jnp
from jax import lax
import numpy as np

D_MODEL = 1024
BATCH = 16
SEQ = 2048
DEPTH = 4
DEC_BATCH = 8
DEC_SEQ = 32
PAST_LEN = 2048

CHUNK = 64
Q_BLOCK = 128
DK_A = 128
DV_A = 128
H_A = D_MODEL // DV_A
CONV_W = 4
CONV_CH = 2 * H_A * DK_A + H_A * DV_A
DH_B = 64
DHV_B = 2 * DH_B
H_B = D_MODEL // DHV_B
N_BUCKETS = 32
MAX_DIST = 128
D_FF = -(-8 * D_MODEL // (3 * 256)) * 256
ALPHA = (2 * DEPTH) ** 0.25
BETA_INIT = (8 * DEPTH) ** -0.25
LN_EPS = 1e-5
IN_SIZES = (CONV_CH, H_A * DV_A, H_A, H_A, H_B * 2 * DH_B, H_B * 2 * DH_B, H_B * DHV_B, 2 * D_MODEL)
D_IN = sum(IN_SIZES)
IN_SPLITS = tuple(np.cumsum(IN_SIZES)[:-1].tolist())

kernel_name = 'hybrid_deltanet_diffattn_stream_step'

F32 = jnp.float32


def layer_norm(x, g, b):
    xf = x.astype(F32)
    mu = jnp.mean(xf, axis=-1, keepdims=True)
    var = jnp.mean(jnp.square(xf - mu), axis=-1, keepdims=True)
    return ((xf - mu) * lax.rsqrt(var + LN_EPS) * g.astype(F32) + b.astype(F32)).astype(x.dtype)


def rms_norm(x, g):
    xf = x.astype(F32)
    out = xf * lax.rsqrt(jnp.mean(jnp.square(xf), axis=-1, keepdims=True) + LN_EPS) * g.astype(F32)
    return out.astype(x.dtype)


def l2norm(x):
    return x * lax.rsqrt(jnp.sum(jnp.square(x), axis=-1, keepdims=True) + 1e-6)


def t5_bucket(rel):
    nb = N_BUCKETS // 2
    max_exact = nb // 2
    ret = jnp.where(rel > 0, nb, 0)
    n = jnp.abs(rel)
    nf = jnp.maximum(n, 1).astype(F32)
    large = max_exact + (jnp.log(nf / max_exact) / math.log(MAX_DIST / max_exact) * (nb - max_exact)).astype(jnp.int32)
    large = jnp.minimum(large, nb - 1)
    return ret + jnp.where(n < max_exact, n, large)


def causal_conv(u, buf, w):
    T = u.shape[1]
    cat = jnp.concatenate([buf.astype(u.dtype), u], axis=1)
    out = cat[:, 0:T] * w[0]
    for i in range(1, CONV_W):
        out = out + cat[:, i:i + T] * w[i]
    return out, cat[:, cat.shape[1] - (CONV_W - 1):]


def gated_delta_rule(q, k, v, beta, g, s0, chunk):
    B, T, H, DK = q.shape
    DV = v.shape[-1]
    N = T // chunk

    def to_blocks(a):
        a = a.reshape((B, N, chunk) + a.shape[2:])
        return jnp.moveaxis(a, (1, 2), (0, 3))

    qc, kc, vc = to_blocks(q), to_blocks(k), to_blocks(v)
    bc, gc = to_blocks(beta), to_blocks(g)
    G = jnp.cumsum(gc, axis=-1)
    idx = jnp.arange(chunk)
    causal = idx[:, None] >= idx[None, :]
    strict = idx[:, None] > idx[None, :]
    decay = jnp.exp(jnp.where(causal, G[..., :, None] - G[..., None, :], -jnp.inf))
    kb = kc * bc[..., None]
    a_mat = jnp.where(strict, jnp.einsum('nbhid,nbhjd->nbhij', kb, kc) * decay, 0.0)
    lhs = jnp.eye(chunk, dtype=F32) + a_mat
    u_v = lax.linalg.triangular_solve(lhs, vc * bc[..., None], left_side=True, lower=True, unit_diagonal=True)
    w_k = lax.linalg.triangular_solve(lhs, kb * jnp.exp(G)[..., None], left_side=True, lower=True, unit_diagonal=True)
    qk = jnp.einsum('nbhid,nbhjd->nbhij', qc, kc) * decay
    q_g = qc * jnp.exp(G)[..., None]
    k_tail = kc * jnp.exp(G[..., -1:] - G)[..., None]
    g_last = jnp.exp(G[..., -1])

    def step(S, xs):
        u_v_n, w_n, qk_n, q_n, kt_n, gl_n = xs
        u = u_v_n - jnp.einsum('bhlk,bhkv->bhlv', w_n, S)
        o = jnp.einsum('bhlk,bhkv->bhlv', q_n, S) + jnp.einsum('bhij,bhjv->bhiv', qk_n, u)
        S = gl_n[..., None, None] * S + jnp.einsum('bhlk,bhlv->bhkv', kt_n, u)
        return S, o

    s_final, o = lax.scan(step, s0, (u_v, w_k, qk, q_g, k_tail, g_last))
    o = jnp.moveaxis(o, (0, 3), (1, 2)).reshape(B, T, H, DV)
    return o, s_final


def diff_attend(q, k, v, qpos, kpos, lam_val, rel_bias):
    bias = jnp.take(rel_bias, t5_bucket(kpos[None, :] - qpos[:, None]), axis=0)
    bias = jnp.transpose(bias, (2, 0, 1)).astype(F32)
    mask = (kpos[None, :] // CHUNK) <= (qpos[:, None] // CHUNK)
    s = jnp.einsum('bqhmd,bkhmd->bhmqk', q, k).astype(F32) * (DH_B ** -0.5) + bias[None, :, None]
    s = jnp.where(mask, s, -1e30)
    p = jax.nn.softmax(s, axis=-1)
    w = (p[:, :, 0] - lam_val * p[:, :, 1]).astype(v.dtype)
    return jnp.einsum('bhqk,bkhe->bqhe', w, v)


def prompt_diff_attention(q, k, v, lam_val, rel_bias):
    B, T = q.shape[0], q.shape[1]
    nb = T // Q_BLOCK
    qb = jnp.swapaxes(q.reshape(B, nb, Q_BLOCK, H_B, 2, DH_B), 0, 1)
    kpos = jnp.arange(T)

    def one_block(args):
        q_blk, i = args
        qpos = i * Q_BLOCK + jnp.arange(Q_BLOCK)
        return diff_attend(q_blk, k, v, qpos, kpos, lam_val, rel_bias)

    o = lax.map(one_block, (qb, jnp.arange(nb)))
    return jnp.swapaxes(o, 0, 1).reshape(B, T, H_B, DHV_B)


def trunk_layer(x, c, l, conv_buf, s0, past_k, past_v, P):
    Bn, T, _ = x.shape
    mod = jnp.einsum('bd,de->be', jax.nn.silu(c), P['w_ada'][l]) + P['b_ada'][l]
    sh1, sc1, gt1, sh2, sc2, gt2 = [m[:, None, :] for m in jnp.split(mod, 6, axis=-1)]
    h = x * (1.0 + sc1) + sh1
    proj = jnp.einsum('btd,de->bte', h, P['w_in'][l])
    qkv_a, z_a, b_a, a_a, q_b, k_b, v_b, gates = jnp.split(proj, IN_SPLITS, axis=-1)
    conv_out, new_buf = causal_conv(qkv_a, conv_buf, P['conv_w'][l])
    conv_out = jax.nn.silu(conv_out.astype(F32))
    q_a, k_a, v_a = jnp.split(conv_out, [H_A * DK_A, 2 * H_A * DK_A], axis=-1)
    q_a = l2norm(q_a.reshape(Bn, T, H_A, DK_A)) * (DK_A ** -0.5)
    k_a = l2norm(k_a.reshape(Bn, T, H_A, DK_A))
    v_a = v_a.reshape(Bn, T, H_A, DV_A)
    beta = jax.nn.sigmoid(b_a.astype(F32))
    g = -jnp.exp(P['a_log'][l].astype(F32)) * jax.nn.softplus(a_a.astype(F32) + P['dt_bias'][l].astype(F32))
    chunk = CHUNK if past_k is None else T
    o_a, s_new = gated_delta_rule(q_a, k_a, v_a, beta, g, s0.astype(F32), chunk)
    o_a = rms_norm(o_a, P['norm_a'][l]) * jax.nn.silu(z_a.astype(F32).reshape(Bn, T, H_A, DV_A))
    o_a = o_a.reshape(Bn, T, H_A * DV_A).astype(x.dtype)
    lam_init = 0.8 - 0.6 * math.exp(-0.3 * l)
    lp = P['lam'][l].astype(F32)
    lam_val = jnp.exp(jnp.sum(lp[0] * lp[1])) - jnp.exp(jnp.sum(lp[2] * lp[3])) + lam_init
    q_b = q_b.reshape(Bn, T, H_B, 2, DH_B)
    k_b = k_b.reshape(Bn, T, H_B, 2, DH_B)
    v_b = v_b.reshape(Bn, T, H_B, DHV_B)
    if past_k is None:
        o_b = prompt_diff_attention(q_b, k_b, v_b, lam_val, P['rel_bias'])
    else:
        past = past_k.shape[1]
        k_all = jnp.concatenate([past_k.reshape(Bn, past, H_B, 2, DH_B).astype(x.dtype), k_b], axis=1)
        v_all = jnp.concatenate([past_v.astype(x.dtype), v_b], axis=1)
        o_b = diff_attend(q_b, k_all, v_all, past + jnp.arange(T), jnp.arange(past + T), lam_val, P['rel_bias'])
    o_b = rms_norm(o_b, P['subln_g'][l]) * (1.0 - lam_init)
    o_b = o_b.reshape(Bn, T, H_B * DHV_B).astype(x.dtype)
    g_a, g_b = jnp.split(gates, 2, axis=-1)
    merged = jax.nn.sigmoid(g_a) * o_a + jax.nn.sigmoid(g_b) * o_b
    y = jnp.einsum('btd,de->bte', merged, P['w_o'][l])
    x = layer_norm(ALPHA * x + gt1 * y, P['ln1_g'][l], P['ln1_b'][l])
    h = x * (1.0 + sc2) + sh2
    u, v = jnp.split(jnp.einsum('btd,df->btf', h, P['w_ff_in'][l]), 2, axis=-1)
    y = jnp.einsum('btf,fd->btd', jax.nn.silu(u) * v, P['w_ff_out'][l])
    x = layer_norm(ALPHA * x + gt2 * y, P['ln2_g'][l], P['ln2_b'][l])
    return x, new_buf, s_new.astype(x.dtype), k_b.reshape(Bn, T, H_B, 2 * DH_B), v_b


def setup_inputs(seed: int = 0) -> dict:
    key = jax.random.key(seed)
    ks = jax.random.split(key, 32)

    def nrm(k, shape, s):
        return s * jax.random.normal(k, shape, F32)

    D = D_MODEL
    b_ada = nrm(ks[12], (DEPTH, 6 * D), 0.02)
    b_ada = b_ada.at[:, 2 * D:3 * D].add(1.0).at[:, 5 * D:6 * D].add(1.0)
    dt = jnp.exp(jax.random.uniform(ks[16], (DEPTH, H_A), F32, math.log(1e-3), math.log(1e-1)))
    return {
        'x_prompt': nrm(ks[0], (BATCH, SEQ, D), 1.0),
        'x_sample': nrm(ks[1], (DEC_BATCH, DEC_SEQ, D), 1.0),
        'cache_k': nrm(ks[2], (DEPTH, DEC_BATCH, PAST_LEN, H_B, 2 * DH_B), 1.0),
        'cache_v': nrm(ks[3], (DEPTH, DEC_BATCH, PAST_LEN, H_B, DHV_B), 1.0),
        'state_conv': nrm(ks[4], (DEPTH, DEC_BATCH, CONV_W - 1, CONV_CH), 1.0),
        'state_delta': nrm(ks[5], (DEPTH, DEC_BATCH, H_A, DK_A, DV_A), 0.05),
        'c_prompt': nrm(ks[6], (BATCH, D), 1.0),
        'c_sample': nrm(ks[7], (DEC_BATCH, D), 1.0),
        'ln_in_g': 1.0 + nrm(ks[8], (D,), 0.02),
        'ln_in_b': nrm(ks[9], (D,), 0.02),
        'rel_bias': nrm(ks[10], (N_BUCKETS, H_B), 0.5),
        'w_ada': nrm(ks[11], (DEPTH, D, 6 * D), 0.5 * D ** -0.5),
        'b_ada': b_ada,
        'w_in': nrm(ks[13], (DEPTH, D, D_IN), D ** -0.5),
        'conv_w': nrm(ks[14], (DEPTH, CONV_W, CONV_CH), CONV_W ** -0.5),
        'a_log': jnp.log(jax.random.uniform(ks[15], (DEPTH, H_A), F32, 1.0, 16.0)),
        'dt_bias': dt + jnp.log(-jnp.expm1(-dt)),
        'norm_a': 1.0 + nrm(ks[17], (DEPTH, DV_A), 0.02),
        'lam': nrm(ks[18], (DEPTH, 4, DH_B), 0.1),
        'subln_g': 1.0 + nrm(ks[19], (DEPTH, DHV_B), 0.02),
        'w_o': nrm(ks[20], (DEPTH, D, D), D ** -0.5 * BETA_INIT),
        'ln1_g': 1.0 + nrm(ks[21], (DEPTH, D), 0.02),
        'ln1_b': nrm(ks[22], (DEPTH, D), 0.02),
        'w_ff_in': nrm(ks[23], (DEPTH, D, 2 * D_FF), D ** -0.5),
        'w_ff_out': nrm(ks[24], (DEPTH, D_FF, D), D_FF ** -0.5 * BETA_INIT),
        'ln2_g': 1.0 + nrm(ks[25], (DEPTH, D), 0.02),
        'ln2_b': nrm(ks[26], (DEPTH, D), 0.02),
    }


def reference(x_prompt, x_sample, cache_k, cache_v, state_conv, state_delta, c_prompt, c_sample,
              ln_in_g, ln_in_b, rel_bias, w_ada, b_ada, w_in, conv_w, a_log, dt_bias, norm_a,
              lam, subln_g, w_o, ln1_g, ln1_b, w_ff_in, w_ff_out, ln2_g, ln2_b):
    P = {'rel_bias': rel_bias, 'w_ada': w_ada, 'b_ada': b_ada, 'w_in': w_in, 'conv_w': conv_w,
         'a_log': a_log, 'dt_bias': dt_bias, 'norm_a': norm_a, 'lam': lam, 'subln_g': subln_g,
         'w_o': w_o, 'ln1_g': ln1_g, 'ln1_b': ln1_b, 'w_ff_in': w_ff_in, 'w_ff_out': w_ff_out,
         'ln2_g': ln2_g, 'ln2_b': ln2_b}
    xp = layer_norm(x_prompt, ln_in_g, ln_in_b)
    xs = layer_norm(x_sample, ln_in_g, ln_in_b)
    bp = xp.shape[0]
    conv0 = jnp.zeros((bp, CONV_W - 1, CONV_CH), xp.dtype)
    s0 = jnp.zeros((bp, H_A, DK_A, DV_A), F32)
    kp_l, vp_l, cp_l, sp_l = [], [], [], []
    ks_l, vs_l, cs_l, ss_l = [], [], [], []
    for l in range(DEPTH):
        xp, cbp, sp, kp, vp = trunk_layer(xp, c_prompt, l, conv0, s0, None, None, P)
        xs, cbs, ss, kk, vv = trunk_layer(xs, c_sample, l, state_conv[l], state_delta[l], cache_k[l], cache_v[l], P)
        kp_l.append(kp); vp_l.append(vp); cp_l.append(cbp); sp_l.append(sp)
        ks_l.append(kk); vs_l.append(vv); cs_l.append(cbs); ss_l.append(ss)
    return (xp, xs,
            jnp.stack(kp_l), jnp.stack(vp_l), jnp.stack(cp_l), jnp.stack(sp_l),
            jnp.stack(ks_l), jnp.stack(vs_l), jnp.stack(cs_l), jnp.stack(ss_l))
```

```python
import math
from contextlib import ExitStack
import numpy as np
import ml_dtypes
import concourse.bass as bass
import concourse.mybir as mybir
from concourse.bass_utils import run_bass_kernel_spmd

F32 = mybir.dt.float32
BF16 = mybir.dt.bfloat16
AF = mybir.ActivationFunctionType
ALU = mybir.AluOpType
AX = mybir.AxisListType

D = 1024
H = 8
DFF = 2816
DIN = 9232
NBK = 32
ALPHA = 8.0 ** 0.25
LN_EPS = 1e-5
NEG = -30000.0

NDMA = 8


class Res:
    __slots__ = ("name", "lw", "readers")

    def __init__(self, name=""):
        self.name = name
        self.lw = None
        self.readers = []


class Op:
    __slots__ = ("eng", "fn", "deps", "signal", "count", "is_dma", "dma_i", "idx")

    def __init__(self, eng, fn, is_dma):
        self.eng = eng
        self.fn = fn
        self.deps = []
        self.signal = False
        self.count = 0
        self.is_dma = is_dma
        self.dma_i = -1
        self.idx = -1


class TB:
    def __init__(self, t, name=""):
        self.t = t
        self.r = Res(name)

    def __getitem__(self, k):
        return self.t[k]


def _res(x):
    return x.r if isinstance(x, TB) else x


class Sched:
    ENGS = ("pe", "act", "dve", "pool", "sp")

    def __init__(self, nc):
        self.nc = nc
        self.streams = {e: [] for e in self.ENGS}
        self.ndma = {e: 0 for e in self.ENGS}
        self.all_dma = []
        self.pending_dma = []
        self.nops = 0

    def add(self, eng, fn, reads=(), writes=(), dma=False, extra=()):
        op = Op(eng, fn, dma)
        deps = list(extra)
        for r in reads:
            lw = _res(r).lw
            if lw is not None and not (lw.eng == "pe" and eng == "pe" and not lw.is_dma and not dma):
                deps.append(lw)
        for w in writes:
            w = _res(w)
            for rd in w.readers:
                if rd.is_dma or rd.eng != eng or dma:
                    deps.append(rd)
            lw = w.lw
            if lw is not None and (lw.is_dma or lw.eng != eng or dma):
                deps.append(lw)
        seen = set()
        for d in deps:
            if id(d) not in seen:
                seen.add(id(d))
                op.deps.append(d)
                d.signal = True
        for r in reads:
            _res(r).readers.append(op)
        for w in writes:
            w = _res(w)
            w.lw = op
            w.readers = []
        op.idx = len(self.streams[eng])
        self.streams[eng].append(op)
        if dma:
            op.dma_i = self.ndma[eng]
            self.ndma[eng] += 1
            op.signal = True
            self.all_dma.append(op)
            self.pending_dma.append(op)
        self.nops += 1
        return op

    def barrier(self):
        lasts = []
        for e in self.ENGS:
            for op in reversed(self.streams[e]):
                if op.fn is not None and not op.is_dma:
                    lasts.append(op)
                    break
        extra = lasts + self.pending_dma
        self.pending_dma = []
        for e in self.ENGS:
            self.add(e, None, extra=extra)

    def emit(self, ctx):
        nc = self.nc
        esem = {e: ctx.enter_context(nc.semaphore("s_" + e)) for e in self.ENGS}
        dsem = {}
        for e in self.ENGS:
            if self.ndma[e]:
                dsem[e] = [ctx.enter_context(nc.semaphore("d_%s%d" % (e, i))) for i in range(NDMA)]
        for e in self.ENGS:
            c = 0
            for op in self.streams[e]:
                if op.signal and not op.is_dma and op.fn is not None:
                    c += 1
                op.count = c

        def ev(d):
            if d.is_dma:
                return dsem[d.eng][d.dma_i % NDMA], 16 * (d.dma_i // NDMA + 1)
            return esem[d.eng], d.count

        streams = self.streams
        all_dma = self.all_dma

        def run(e, h):
            known = {}
            for op in streams[e]:
                need = {}
                for d in op.deps:
                    if d.fn is None:
                        continue
                    s, v = ev(d)
                    k = id(s)
                    if k not in need or need[k][1] < v:
                        need[k] = (s, v)
                if op.is_dma and op.dma_i >= NDMA:
                    s = dsem[e][op.dma_i % NDMA]
                    v = 16 * (op.dma_i // NDMA)
                    k = id(s)
                    if k not in need or need[k][1] < v:
                        need[k] = (s, v)
                for k, (s, v) in need.items():
                    if known.get(k, 0) < v:
                        h.wait_ge(s, v)
                        known[k] = v
                if op.fn is None:
                    continue
                ins = op.fn(h)
                if op.is_dma:
                    ins.then_inc(dsem[e][op.dma_i % NDMA], 16)
                elif op.signal:
                    ins.then_inc(esem[e], 1)
            if e == "sp":
                last = {}
                for d in all_dma:
                    s, v = ev(d)
                    k = id(s)
                    if k not in last or last[k][1] < v:
                        last[k] = (s, v)
                for k, (s, v) in last.items():
                    if known.get(k, 0) < v:
                        h.wait_ge(s, v)

        with nc.Block() as block:
            @block.sync
            def _(h):
                run("sp", h)

            @block.tensor
            def _(h):
                run("pe", h)

            @block.scalar
            def _(h):
                run("act", h)

            @block.vector
            def _(h):
                run("dve", h)

            @block.gpsimd
            def _(h):
                run("pool", h)


class Cfg:
    def __init__(self, depth=4, n_prompt=2, T=2048, with_sample=True, TS=32, past=2048,
                 do_attn=True, do_delta=True):
        self.depth = depth
        self.n_prompt = n_prompt
        self.T = T
        self.with_sample = with_sample
        self.TS = TS
        self.past = past
        self.do_attn = do_attn
        self.do_delta = do_delta


def t5_bucket_np(rel):
    import jax
    import jax.numpy as jnp
    with jax.default_device(jax.devices("cpu")[0]):
        return _t5_bucket_cpu(jnp, rel)


def _t5_bucket_cpu(jnp, rel):
    rel = jnp.asarray(rel, jnp.int32)
    nb = NBK // 2
    max_exact = nb // 2
    ret = jnp.where(rel > 0, nb, 0)
    n = jnp.abs(rel)
    nf = jnp.maximum(n, 1).astype(jnp.float32)
    large = max_exact + (jnp.log(nf / max_exact) / math.log(128 / max_exact) * (nb - max_exact)).astype(jnp.int32)
    large = jnp.minimum(large, nb - 1)
    return np.asarray(ret + jnp.where(n < max_exact, n, large))


def make_consts():
    c = {}
    c["identf"] = np.eye(128, dtype=np.float32)
    p = np.arange(128)[:, None]
    f = np.arange(128)[None, :]
    c["utri"] = (p <= f).astype(np.float32)
    c["onesf"] = np.ones((128, 128), np.float32)
    c["mincl"] = np.where(p >= f, 0.0, NEG).astype(np.float32)
    c["sstrict"] = (p > f).astype(np.float32)
    c["bmask"] = np.stack([((p // s2) == (f // s2)).astype(np.float32) for s2 in (2, 4, 8, 16, 32, 64)])
    kl = np.arange(128)[:, None]
    ql = np.arange(128)[None, :]
    bd = t5_bucket_np(kl - ql)
    bp = t5_bucket_np(kl - ql - 128)
    oh = np.zeros((2, NBK, 128, 128), np.float32)
    for b in range(NBK):
        oh[0, b] = (bd == b)
        oh[1, b] = (bp == b)
    c["bk_oh"] = oh
    vis = ((kl // 64) <= (ql // 64))
    c["amask"] = np.where(vis, 0.0, 8.0 * NEG).astype(np.float32)
    used = [sorted(set(np.unique(bd).tolist())), sorted(set(np.unique(bp).tolist()))]
    return c, used


def build_program(cfg):
    nc = bass.Bass("TRN2", target_bir_lowering=False)
    consts, used_buckets = make_consts()
    NP, T, L = cfg.n_prompt, cfg.T, cfg.depth
    NS = NP + (1 if cfg.with_sample else 0)
    TS, PAST = cfg.TS, cfg.past

    def din(name, shape, dt=F32):
        return nc.dram_tensor(name, list(shape), dt, kind="ExternalInput").ap()

    def dout(name, shape, dt=F32):
        return nc.dram_tensor(name, list(shape), dt, kind="ExternalOutput").ap()

    def dscr(name, shape, dt=F32):
        return nc.dram_tensor(name, list(shape), dt, kind="Internal").ap()

    I = {}
    I["xp"] = din("xp", [NP, T, D])
    I["cvec"] = din("cvec", [NS, D])
    if cfg.with_sample:
        I["xs"] = din("xs", [1, TS, D])
        I["ck"] = din("ck", [L, PAST, D])
        I["cv"] = din("cv", [L, PAST, D])
        I["sconv"] = din("sconv", [L, 3, 3072])
        I["sdelta"] = din("sdelta", [L, H, 128, 128])
    for nm, shp in [("ln_in_g", [D]), ("ln_in_b", [D]), ("rel_bias", [NBK, H]), ("w_ada", [L, D, 6 * D]),
                    ("b_ada", [L, 6 * D]), ("w_in", [L, D, DIN]), ("conv_w", [L, 4, 3072]), ("a_log", [L, H]),
                    ("dt_bias", [L, H]), ("norm_a", [L, 128]), ("lam", [L, 4, 64]), ("subln_g", [L, 128]),
                    ("w_o", [L, D, D]), ("ln1_g", [L, D]), ("ln1_b", [L, D]), ("w_ff_in", [L, D, 2 * DFF]),
                    ("w_ff_out", [L, DFF, D]), ("ln2_g", [L, D]), ("ln2_b", [L, D])]:
        I[nm] = din(nm, shp)
    for nm, arr in consts.items():
        I["c_" + nm] = din("c_" + nm, arr.shape)

    O = {}
    O["yp"] = dout("yp", [NP, T, D])
    O["nkp"] = dout("nkp", [L, NP, T, D])
    O["nvp"] = dout("nvp", [L, NP, T, D])
    O["ncp"] = dout("ncp", [L, NP, 3, 3072])
    O["ndp"] = dout("ndp", [L, NP, H, 128, 128])
    if cfg.with_sample:
        O["ys"] = dout("ys", [1, TS, D])
        O["nks"] = dout("nks", [L, 1, TS, D])
        O["nvs"] = dout("nvs", [L, 1, TS, D])
        O["ncs"] = dout("ncs", [L, 1, 3, 3072])
        O["nds"] = dout("nds", [L, 1, H, 128, 128])

    DBG = getattr(cfg, "dbg", False)
    if DBG:
        NCd = T // 128
        O["dbg_tk"] = dout("dbg_tk", [128, NCd * 16])
        O["dbg_raw"] = dout("dbg_raw", [3, 128, T], BF16)
        O["dbg_Q"] = dout("dbg_Q", [128, T], BF16)
        O["dbg_qkT"] = dout("dbg_qkT", [128, T], BF16)
        O["dbg_wkT"] = dout("dbg_wkT", [128, T], BF16)
        O["dbg_oa"] = dout("dbg_oa", [128, NCd * 128])
        O["dbg_ob"] = dout("dbg_ob", [128, NCd * 128])
        O["dbg_E"] = dout("dbg_E", [128, 512])
        O["dbg_X0"] = dout("dbg_X0", [128, 512], BF16)
    TMAX = max(T, TS)
    xsc = dscr("xsc", [TMAX, D])
    modd = dscr("modd", [L, NS, 6 * D])
    mTd = dscr("mTd", [D, TMAX], BF16)

    ctx = ExitStack()
    S = Sched(nc)

    AW = 53000
    arena_t = ctx.enter_context(nc.sbuf_tensor("arena", [128, AW], F32))
    aoff = [0]
    amax = [0]

    def sb(shape, dt, name):
        n = 1
        for v in shape[1:]:
            n *= v
        words = n if dt == F32 else (n + 1) // 2
        o = aoff[0]
        assert o + words <= AW, "arena overflow %s %d" % (name, o + words)
        aoff[0] = o + words
        amax[0] = max(amax[0], aoff[0])
        ap = arena_t[:, o:o + words]
        if dt != F32:
            ap = ap.bitcast(dt)[:, 0:n]
        if len(shape) == 3:
            ap = ap.rearrange("p (a b) -> p a b", b=shape[2])
        elif len(shape) == 4:
            ap = ap.rearrange("p (a b c) -> p a b c", b=shape[2], c=shape[3])
        return TB(ap, name)

    def mark():
        return aoff[0]

    def release(m):
        aoff[0] = m

    PB = [TB(ctx.enter_context(nc.psum_tensor("pb%d" % i, [128, 512], F32)), "pb%d" % i) for i in range(8)]

    def pbf(i):
        return PB[i].t[:].bitcast(BF16)

    def DMA(out, in_, reads=(), writes=(), q="sp"):
        return S.add(q, lambda h: h.dma_start(out=out, in_=in_), reads, writes, dma=True)

    def DMAS(out, in_, reads=(), writes=(), q="sp"):
        return S.add(q, lambda h: h.dma_start(out=out, in_=in_, allow_slow_non_contiguous=True), reads, writes, dma=True)

    def MM(out, lhsT, rhs, start, stop, reads, writes):
        return S.add("pe", lambda h: h.matmul(out, lhsT=lhsT, rhs=rhs, start=start, stop=stop), reads, writes)

    def TR(out, in_, ident, reads, writes):
        return S.add("pe", lambda h: h.transpose(out, in_, ident), reads, writes)

    def ACT(out, in_, func, reads, writes, bias=None, scale=None, accum=None):
        kw = {}
        if bias is not None:
            kw["bias"] = bias
        if scale is not None:
            kw["scale"] = scale
        if accum is not None:
            kw["accum_out"] = accum
        return S.add("act", lambda h: h.activation(out=out, in_=in_, func=func, **kw), reads, writes)

    def TS_(eng, out, in0, s1, s2, op0, op1, reads, writes):
        if s2 is None:
            return S.add(eng, lambda h: h.tensor_scalar(out=out, in0=in0, scalar1=s1, scalar2=None, op0=op0), reads, writes)
        return S.add(eng, lambda h: h.tensor_scalar(out=out, in0=in0, scalar1=s1, scalar2=s2, op0=op0, op1=op1), reads, writes)

    def TT(eng, out, in0, in1, op, reads, writes):
        return S.add(eng, lambda h: h.tensor_tensor(out=out, in0=in0, in1=in1, op=op), reads, writes)

    def STT(eng, out, in0, scalar, in1, op0, op1, reads, writes):
        return S.add(eng, lambda h: h.scalar_tensor_tensor(out=out, in0=in0, scalar=scalar, in1=in1, op0=op0, op1=op1), reads, writes)

    def CP(eng, out, in_, reads, writes):
        if eng == "act":
            return S.add("act", lambda h: h.activation(out=out, in_=in_, func=AF.Copy), reads, writes)
        return S.add(eng, lambda h: h.tensor_copy(out=out, in_=in_), reads, writes)

    def MEMSET(eng, ap, val, writes):
        return S.add(eng, lambda h: h.memset(ap, val), (), writes)

    def RECIP(out, in_, reads, writes):
        return S.add("dve", lambda h: h.reciprocal(out=out, in_=in_), reads, writes)

    def bcast_row(dst, row_ap):
        DMA(dst[:], row_ap.partition_broadcast(128), writes=[dst])

    identf = sb([128, 128], F32, "identf")
    identb = sb([128, 128], BF16, "identb")
    utri = sb([128, 128], F32, "utri")
    onesf = sb([128, 128], F32, "onesf")
    onesb = sb([128, 128], BF16, "onesb")
    mincl = sb([128, 512], BF16, "mincl")
    sstrict = sb([128, 512], F32, "sstrict")
    identrep = sb([128, 512], BF16, "identrep")
    cst = sb([128, 8], F32, "cst")
    cbias = sb([128, H], F32, "cbias")
    biasT = [sb([128, H, 128], BF16, "biasT%d" % t) for t in range(2)]
    bmask = sb([128, 6, 128], F32, "bmask")
    for i_ in range(6):
        DMA(bmask[:, i_, :], I["c_bmask"][i_], writes=[bmask])
    DMA(identf[:], I["c_identf"], writes=[identf])
    DMA(utri[:], I["c_utri"], writes=[utri])
    DMA(onesf[:], I["c_onesf"], writes=[onesf])
    CP("dve", identb[:], identf[:], [identf], [identb])
    CP("dve", onesb[:], onesf[:], [onesf], [onesb])
    m0 = mark()
    tmpc = sb([128, 128], F32, "tmpc")
    DMA(tmpc[:], I["c_mincl"], writes=[tmpc])
    for r4 in range(4):
        CP("dve", mincl[:, r4 * 128:(r4 + 1) * 128], tmpc[:], [tmpc], [mincl])
        CP("dve", identrep[:, r4 * 128:(r4 + 1) * 128], identf[:], [identf], [identrep])
        DMA(sstrict[:, r4 * 128:(r4 + 1) * 128], I["c_sstrict"], writes=[sstrict])
    MEMSET("dve", cst[:, 0:1], LN_EPS, [cst])
    MEMSET("dve", cst[:, 1:2], 1e-6, [cst])
    MEMSET("dve", cst[:, 2:3], 1.0, [cst])
    MEMSET("dve", cst[:, 3:4], 0.0, [cst])

    rbb = sb([128, NBK * H], F32, "rbb")
    DMA(rbb[:], I["rel_bias"].rearrange("b h -> (b h)").partition_broadcast(128), writes=[rbb])
    CP("dve", cbias[:], rbb[:, 15 * H:16 * H], [rbb], [cbias])
    rbd = sb([128, NBK * H], F32, "rbd")
    for b in range(NBK):
        TT("dve", rbd[:, b * H:(b + 1) * H], rbb[:, b * H:(b + 1) * H], cbias[:], ALU.subtract, [rbb, cbias], [rbd])
    TS_("dve", rbd[:], rbd[:], 8.0, None, ALU.mult, None, [rbd], [rbd])
    acc = sb([128, H, 128], F32, "bacc")
    ohs = [sb([128, 128], F32, "ohs%d" % i) for i in range(2)]
    cnt = 0
    for t in range(2):
        if t == 0:
            o_ = ohs[cnt % 2]
            cnt += 1
            DMA(o_[:], I["c_amask"], writes=[o_])
            for hh in range(H):
                CP("dve", acc[:, hh, :], o_[:], [o_], [acc])
        else:
            MEMSET("dve", acc[:], 0.0, [acc])
        for b in used_buckets[t]:
            o_ = ohs[cnt % 2]
            cnt += 1
            DMA(o_[:], I["c_bk_oh"][t, b], writes=[o_])
            for hh in range(H):
                STT("dve", acc[:, hh, :], o_[:], rbd[:, b * H + hh:b * H + hh + 1], acc[:, hh, :],
                    ALU.mult, ALU.add, [o_, rbd, acc], [acc])
        CP("dve", biasT[t][:], acc[:], [acc], [biasT[t]])
    S.barrier()
    release(m0)

    m0 = mark()
    cT = sb([128, 8, NS], F32, "cT")
    cTb = sb([128, 8, NS], BF16, "cTb")
    for s_i in range(NS):
        DMAS(cT[:, :, s_i], I["cvec"][s_i].rearrange("(k p) -> p k", p=128), writes=[cT])
    ACT(cTb[:], cT[:], AF.Silu, [cT], [cTb])
    wad = [sb([128, 8, 512], BF16, "wad%d" % i) for i in range(2)]
    bad = sb([128, 6 * D], F32, "bad")
    mrow = sb([128, 6 * D], F32, "mrow")
    for l in range(L):
        for s in range(NS):
            DMA(bad[s:s + 1, :], I["b_ada"][l:l + 1, :], writes=[bad])
        for ct in range(12):
            w = wad[ct % 2]
            DMA(w[:], I["w_ada"][l].rearrange("(k p) c -> p k c", p=128)[:, :, ct * 512:(ct + 1) * 512],
                writes=[w], q="pool")
            pbk = PB[ct % 2]
            for k in range(8):
                MM(pbk[0:NS, :], cTb[:, k, :], w[:, k, :], k == 0, k == 7, [cTb, w], [pbk])
            TT("dve", mrow[0:NS, ct * 512:(ct + 1) * 512], pbk[0:NS, :], bad[0:NS, ct * 512:(ct + 1) * 512], ALU.add,
               [bad], [mrow, pbk])
        DMA(modd[l], mrow[0:NS, :], reads=[mrow])
    S.barrier()
    release(m0)

    seqs = [dict(kind="p", i=i, T=T, P=128, c=i) for i in range(NP)]
    if cfg.with_sample:
        seqs.append(dict(kind="s", i=0, T=TS, P=TS, c=NP))

    def layer_norm_block(P, xin, xin_tb, xout, xout_tb, gam, bet, st, junk):
        S.add("dve", lambda h: h.reduce_sum(out=st[0:P, 0:1], in_=xin, axis=AX.X), [xin_tb], [st])
        TS_("dve", st[0:P, 1:2], st[0:P, 0:1], -1.0 / D, None, ALU.mult, None, [st], [st])
        ACT(junk[0:P, :], xin, AF.Square, [xin_tb, st], [junk, st], bias=st[0:P, 1:2], accum=st[0:P, 2:3])
        ACT(st[0:P, 3:4], st[0:P, 2:3], AF.Ln, [st, cst], [st], bias=cst[0:P, 0:1], scale=1.0 / D)
        ACT(st[0:P, 4:5], st[0:P, 3:4], AF.Exp, [st], [st], scale=-0.5)
        TS_("dve", xout, xin, st[0:P, 1:2], st[0:P, 4:5], ALU.add, ALU.mult, [xin_tb, st], [xout_tb])
        TT("pool", xout, xout, gam[0:P, :], ALU.mult, [xout_tb, gam], [xout_tb])
        TT("dve", xout, xout, bet[0:P, :], ALU.add, [xout_tb, bet], [xout_tb])

    def build_hT(P, b, x, opsc, sh, r, hbb, hT):
        TT("dve", r[0:P, :], x[0:P, :], opsc[0:P, :], ALU.mult, [x, opsc], [r])
        TT("pool", hbb[0:P, :], r[0:P, :], sh[0:P, :], ALU.add, [r, sh], [hbb])
        pb = PB[2 + (b % 2)]
        pv = pbf(2 + (b % 2)).rearrange("p (k t) -> p k t", k=8)
        for k in range(8):
            TR(pv[:, k, 0:P], hbb[0:P, k * 128:(k + 1) * 128], identb[0:P, 0:P], [hbb, identb], [pb])
        ACT(hT[:, :, b * P:(b + 1) * P], pv[:, :, 0:P], AF.Copy, [], [hT, pb])

    def mixer(sq, l, hT):
        Tq, P = sq["T"], sq["P"]
        NB = Tq // P
        isP = sq["kind"] == "p"
        si = sq["i"]
        TT_W = min(512, Tq)
        NTT = Tq // TT_W
        BPT = TT_W // P
        Lc = P
        NC = NB
        GW = min(4, NC) * Lc
        NG = (NC * Lc) // GW
        CPG = GW // Lc
        nlev = int(round(math.log2(Lc))) - 1
        lam_init = 0.8 - 0.6 * math.exp(-0.3 * l)
        KT = (PAST + TS) if not isP else Tq
        wsrc = I["w_in"][l].rearrange("(k p) c -> p k c", p=128)
        o_nk = O["nkp"][l, si] if isP else O["nks"][l, 0]
        o_nv = O["nvp"][l, si] if isP else O["nvs"][l, 0]
        o_nc = O["ncp"][l, si] if isP else O["ncs"][l, 0]
        o_nd = O["ndp"][l, si] if isP else O["nds"][l, 0]

        lamt = sb([128, 4, 64], F32, "lamt")
        DMA(lamt[:], I["lam"][l].partition_broadcast(128), writes=[lamt])
        lsc = sb([128, 8], F32, "lsc")
        ljunk = sb([128, 64], F32, "ljunk")
        TT("dve", ljunk[:], lamt[:, 0, :], lamt[:, 1, :], ALU.mult, [lamt], [ljunk])
        S.add("dve", lambda h: h.reduce_sum(out=lsc[:, 0:1], in_=ljunk[:], axis=AX.X), [ljunk], [lsc])
        TT("dve", ljunk[:], lamt[:, 2, :], lamt[:, 3, :], ALU.mult, [lamt, lsc], [ljunk])
        S.add("dve", lambda h: h.reduce_sum(out=lsc[:, 1:2], in_=ljunk[:], axis=AX.X), [ljunk], [lsc])
        ACT(lsc[:, 2:4], lsc[:, 0:2], AF.Exp, [lsc], [lsc])
        TT("dve", lsc[:, 4:5], lsc[:, 2:3], lsc[:, 3:4], ALU.subtract, [lsc], [lsc])
        TS_("dve", lsc[:, 5:6], lsc[:, 4:5], lam_init, -1.0, ALU.add, ALU.mult, [lsc], [lsc])
        neglam = lsc[:, 5:6]
        gsub = sb([128, 128], F32, "gsub")
        bcast_row(gsub, I["subln_g"][l])
        TS_("dve", gsub[:], gsub[:], 1.0 - lam_init, None, ALU.mult, None, [gsub], [gsub])
        gna = sb([128, 128], F32, "gna")
        bcast_row(gna, I["norm_a"][l])
        nea = sb([128, H], F32, "nea")
        dtb = sb([128, H], F32, "dtb")
        bcast_row(nea, I["a_log"][l])
        bcast_row(dtb, I["dt_bias"][l])
        ACT(nea[:], nea[:], AF.Exp, [nea], [nea])
        TS_("dve", nea[:], nea[:], -1.0, None, ALU.mult, None, [nea], [nea])

        beta_all = sb([128, NB, H], F32, "beta_all")
        g_all = sb([128, NB, H], F32, "g_all")
        bgw = sb([128, 8, 16], BF16, "bgw")
        DMA(bgw[:], wsrc[:, :, 4096:4112], writes=[bgw], q="pool")
        pbg = PB[0]
        pbgv = pbg.t[:, 0:NB * 16].rearrange("p (b c) -> p b c", c=16)
        for b in range(NB):
            for k in range(8):
                MM(pbg[0:P, b * 16:(b + 1) * 16], hT[:, k, b * P:(b + 1) * P], bgw[:, k, :], k == 0, k == 7, [hT, bgw], [pbg])
        ACT(beta_all[0:P, :, :], pbgv[0:P, :, 0:8], AF.Sigmoid, [], [beta_all, pbg])
        for b in range(NB):
            TT("dve", g_all[0:P, b, :], pbgv[0:P, b, 8:16], dtb[0:P, :], ALU.add, [dtb], [g_all, pbg])
        ACT(g_all[0:P, :, :], g_all[0:P, :, :], AF.Exp, [g_all], [g_all])
        ACT(g_all[0:P, :, :], g_all[0:P, :, :], AF.Ln, [g_all, cst], [g_all], bias=cst[0:P, 2:3])
        for b in range(NB):
            TT("dve", g_all[0:P, b, :], g_all[0:P, b, :], nea[0:P, :], ALU.mult, [g_all, nea], [g_all])

        NPB = PAST // 128
        if not isP:
            ckb = sb([128, NPB, D], BF16, "ckb")
            cvb = sb([128, NPB, D], BF16, "cvb")
            DMA(ckb[:], I["ck"][l].rearrange("(b p) c -> p b c", p=128), writes=[ckb], q="pool")
            DMA(cvb[:], I["cv"][l].rearrange("(b p) c -> p b c", p=128), writes=[cvb], q="pool")

        wq = [sb([128, 8, 128], BF16, "wq%d" % i) for i in range(5)]
        wtm = sb([128, 8, 512], BF16, "wtm")
        qT = sb([128, Tq], BF16, "qT")
        kT = [sb([128, KT], BF16, "kT%d" % m) for m in range(2)]
        MEMSET("pool", kT[0][64:128, :], 0.0, [kT[0]])
        MEMSET("pool", kT[1][0:64, :], 0.0, [kT[1]])
        Vh = sb([128, NB, 128], BF16, "Vh")
        zs = sb([128, NB, 128], BF16, "zs")
        sga = sb([128, NB, 128], BF16, "sga")
        sgb = sb([128, NB, 128], BF16, "sgb")
        ob = sb([128, NB, 128], F32, "ob")
        oa = sb([128, NB, 128], F32, "oa")
        vo = [sb([128, 128], F32, "vo%d" % i) for i in range(2)]
        ko = [sb([128, 128], F32, "ko%d" % i) for i in range(2)]
        NKG_ = ((NB if isP else (PAST // 128 + 1)) + 3) // 4
        PT = [sb([128, NKG_, 4 * P], BF16, "PT%d" % m) for m in range(2)]
        rz = sb([128, 16], F32, "rz")
        t1 = [sb([128, 128], F32, "t1%d" % i) for i in range(2)]
        obj = sb([128, 128], BF16, "obj")
        sst = sb([128, 8], F32, "sst")
        mg = [sb([128, 128], BF16, "mg%d" % i) for i in range(2)]
        mTh = sb([128, Tq], BF16, "mTh")
        pre = sb([128, 3 + Tq], F32, "pre")
        cacc = sb([128, Tq], F32, "cacc")
        raw = [sb([128, Tq], BF16, "raw%d" % i) for i in range(3)]
        sq_ = sb([128, Tq], BF16, "sq_")
        cw = sb([128, 4], F32, "cw")
        ncs = sb([128, 128], F32, "ncs")
        tk = sb([128, NC, 16], F32, "tk")
        Rk = sb([128, NC, 128], BF16, "Rk")
        ktl = sb([128, NC, 128], BF16, "ktl")
        vb = sb([128, NC, 128], BF16, "vb")
        Ecore = sb([128, GW], F32, "Ecore")
        Estr = sb([128, GW], F32, "Estr")
        Xb = [sb([128, GW], BF16, "Xb%d" % i) for i in range(2)]
        Yb = [sb([128, GW], BF16, "Yb%d" % i) for i in range(2)]
        Qb = [sb([128, GW], BF16, "Qb%d" % i) for i in range(2)]
        qkb = sb([128, GW], BF16, "qkb")
        Qall = sb([128, NC * Lc], BF16, "Qall")
        qkT = sb([128, NC * Lc], BF16, "qkT")
        wkT = sb([128, NC * Lc], BF16, "wkT")
        Sf = sb([128, 128], F32, "Sf")
        Sb = sb([128, 128], BF16, "Sb")
        usb = [sb([128, 128], BF16, "usb%d" % i) for i in range(2)]
        ot = [sb([128, 128], F32, "ot%d" % i) for i in range(2)]

        for h in range(H):
            cols = dict(qa=h * 128, ka=1024 + h * 128, va=2048 + h * 128, z=3072 + h * 128,
                        qb=4112 + h * 128, kb=5136 + h * 128, vb=6160 + h * 128,
                        ga=7184 + h * 128, gb=8208 + h * 128)
            for i_, nm in enumerate(["qa", "ka", "va", "qb", "kb"]):
                DMA(wq[i_][:], wsrc[:, :, cols[nm]:cols[nm] + 128], writes=[wq[i_]], q="pool")
            for i_, nm in enumerate(["z", "vb", "ga", "gb"]):
                DMA(wtm[:, :, i_ * 128:(i_ + 1) * 128], wsrc[:, :, cols[nm]:cols[nm] + 128], writes=[wtm], q="pool")

            for b in range(NB):
                bsl = slice(b * P, (b + 1) * P)
                pbk = PB[b % 2]
                for k in range(8):
                    MM(pbk[0:P, :], hT[:, k, bsl], wtm[:, k, :], k == 0, k == 7, [hT, wtm], [pbk])
                ACT(zs[0:P, b, :], pbk[0:P, 0:128], AF.Silu, [], [zs, pbk])
                v_ = vo[b % 2]
                CP("dve", v_[0:P, :], pbk[0:P, 128:256], [], [v_, pbk])
                ACT(sga[0:P, b, :], pbk[0:P, 256:384], AF.Sigmoid, [], [sga, pbk])
                ACT(sgb[0:P, b, :], pbk[0:P, 384:512], AF.Sigmoid, [], [sgb, pbk])
                DMA(o_nv[bsl, h * 128:(h + 1) * 128], v_[0:P, :], reads=[v_])
                CP("pool", Vh[0:P, b, :], v_[0:P, :], [v_], [Vh])
                pk = PB[2]
                for k in range(8):
                    MM(pk[0:P, 0:128], hT[:, k, bsl], wq[4][:, k, :], k == 0, k == 7, [hT, wq[4]], [pk])
                k_ = ko[b % 2]
                CP("dve", k_[0:P, :], pk[0:P, 0:128], [], [k_, pk])
                DMA(o_nk[bsl, h * 128:(h + 1) * 128], k_[0:P, :], reads=[k_])

            koff = 0 if isP else PAST
            for tt in range(NTT):
                tsl = slice(tt * TT_W, (tt + 1) * TT_W)
                pq = PB[4]
                for k in range(8):
                    MM(pq[:, 0:TT_W], wq[3][:, k, :], hT[:, k, tsl], k == 0, k == 7, [wq[3], hT], [pq])
                ACT(qT[:, tsl], pq[:, 0:TT_W], AF.Copy, [], [qT, pq])
                pk = PB[5]
                for k in range(8):
                    MM(pk[:, 0:TT_W], wq[4][:, k, :], hT[:, k, tsl], k == 0, k == 7, [wq[4], hT], [pk])
                ksl = slice(koff + tt * TT_W, koff + (tt + 1) * TT_W)
                ACT(kT[0][0:64, ksl], pk[0:64, 0:TT_W], AF.Copy, [], [kT[0], pk])
                CP("dve", kT[1][64:128, ksl], pk[64:128, 0:TT_W], [], [kT[1], pk])
            if not isP:
                for kb in range(NPB):
                    pb = PB[2 + (kb % 2)]
                    pv = pbf(2 + (kb % 2))
                    TR(pv[:, 0:128], ckb[:, kb, h * 128:(h + 1) * 128], identb[:, :], [ckb, identb], [pb])
                    ACT(kT[0][0:64, kb * 128:(kb + 1) * 128], pv[0:64, 0:128], AF.Copy, [], [kT[0], pb])
                    CP("dve", kT[1][64:128, kb * 128:(kb + 1) * 128], pv[64:128, 0:128], [], [kT[1], pb])

            if cfg.do_attn:
                if isP:
                    kblocks = [dict(kp=128, col=kb * 128, v=Vh[:, kb, :], vtb=Vh) for kb in range(NB)]
                else:
                    kblocks = [dict(kp=128, col=kb * 128, v=cvb[:, kb, h * 128:(h + 1) * 128], vtb=cvb) for kb in range(NPB)]
                    kblocks.append(dict(kp=TS, col=PAST, v=Vh[0:TS, 0, :], vtb=Vh))
                NKB = len(kblocks)
                QP = P
                gcount = 0
                for qb in range(NB):
                    nvis = (qb + 1) if isP else NKB
                    nkg = (nvis + 3) // 4
                    for m in range(2):
                        for kg in range(nkg):
                            sp_ = PB[4 + (gcount % 2)]
                            gcount += 1
                            kbs = list(range(kg * 4, min(nvis, kg * 4 + 4)))
                            KPg = kblocks[kbs[0]]["kp"]
                            for kb in kbs:
                                kd = kblocks[kb]
                                KP = kd["kp"]
                                assert KP == KPg
                                cl = kb - kg * 4
                                biasl = []
                                if isP:
                                    if kb == qb:
                                        biasl.append((0, 128, 128))
                                    if kb == qb - 1:
                                        biasl.append((1, 128, 128))
                                else:
                                    if kb == NPB - 1:
                                        biasl.append((1, 128, TS))
                                    if kb == NPB:
                                        biasl.append((0, TS, TS))
                                MM(sp_[0:KP, cl * QP:(cl + 1) * QP], kT[m][:, kd["col"]:kd["col"] + KP], qT[:, qb * QP:(qb + 1) * QP],
                                   True, len(biasl) == 0, [kT[m], qT], [sp_])
                                for bi, (typ, kp_, qp_) in enumerate(biasl):
                                    MM(sp_[0:kp_, cl * QP:cl * QP + qp_], identb[0:kp_, 0:kp_], biasT[typ][0:kp_, h, 0:qp_],
                                       False, bi == len(biasl) - 1, [identb, biasT[typ]], [sp_])
                            ACT(PT[m][0:KPg, kg, 0:len(kbs) * QP], sp_[0:KPg, 0:len(kbs) * QP], AF.Exp, [cbias], [PT[m], sp_],
                                bias=cbias[0:KPg, h:h + 1], scale=0.125)
                    for m in range(2):
                        for kb in range(nvis):
                            kd = kblocks[kb]
                            KP = kd["kp"]
                            MM(PB[6][0:QP, m * 128:(m + 1) * 128], PT[m][0:KP, kb // 4, (kb % 4) * QP:(kb % 4 + 1) * QP], kd["v"],
                               kb == 0, kb == nvis - 1, [PT[m], kd["vtb"]], [PB[6]])
                        for kb in range(nvis):
                            kd = kblocks[kb]
                            KP = kd["kp"]
                            MM(PB[7][0:QP, m:m + 1], PT[m][0:KP, kb // 4, (kb % 4) * QP:(kb % 4 + 1) * QP], onesb[0:KP, 0:1],
                               kb == 0, kb == nvis - 1, [PT[m], onesb], [PB[7]])
                    RECIP(rz[0:QP, 0:2], PB[7][0:QP, 0:2], [], [rz, PB[7]])
                    TS_("dve", rz[0:QP, 2:4], rz[0:QP, 0:2], neglam[0:QP, :], None, ALU.mult, None, [rz, lsc], [rz])
                    t_ = t1[qb % 2]
                    TS_("dve", t_[0:QP, :], PB[6][0:QP, 0:128], rz[0:QP, 0:1], None, ALU.mult, None, [rz], [t_, PB[6]])
                    STT("dve", t_[0:QP, :], PB[6][0:QP, 128:256], rz[0:QP, 3:4], t_[0:QP, :], ALU.mult, ALU.add, [rz, t_], [t_, PB[6]])
                    ACT(obj[0:QP, :], t_[0:QP, :], AF.Square, [t_], [obj, sst], accum=sst[0:QP, 0:1])
                    ACT(sst[0:QP, 1:2], sst[0:QP, 0:1], AF.Ln, [sst, cst], [sst], bias=cst[0:QP, 0:1], scale=1.0 / 128)
                    ACT(sst[0:QP, 2:3], sst[0:QP, 1:2], AF.Exp, [sst], [sst], scale=-0.5)
                    STT("dve", ob[0:QP, qb, :], t_[0:QP, :], sst[0:QP, 2:3], gsub[0:QP, :], ALU.mult, ALU.mult,
                        [t_, sst, gsub], [ob])
            else:
                MEMSET("dve", ob[:], 0.0, [ob])

            if cfg.do_delta:
                for gi, nm in enumerate(["qa", "ka", "va"]):
                    c0 = cols[nm]
                    DMAS(cw[:], I["conv_w"][l][:, c0:c0 + 128].rearrange("i c -> c i"), writes=[cw])
                    if isP:
                        MEMSET("pool", pre[:, 0:3], 0.0, [pre])
                    else:
                        DMAS(pre[:, 0:3], I["sconv"][l][:, c0:c0 + 128].rearrange("t c -> c t"), writes=[pre])
                    for tt in range(NTT):
                        tsl = slice(tt * TT_W, (tt + 1) * TT_W)
                        pq = PB[tt % 2]
                        for k in range(8):
                            MM(pq[:, 0:TT_W], wq[gi][:, k, :], hT[:, k, tsl], k == 0, k == 7, [wq[gi], hT], [pq])
                        ACT(pre[:, 3 + tt * TT_W:3 + (tt + 1) * TT_W], pq[:, 0:TT_W], AF.Copy, [], [pre, pq])
                    pn = PB[2]
                    for k in range(8):
                        MM(pn[0:3, 0:128], hT[:, k, Tq - 3:Tq], wq[gi][:, k, :], k == 0, k == 7, [hT, wq[gi]], [pn])
                    CP("dve", ncs[0:3, :], pn[0:3, 0:128], [], [ncs, pn])
                    DMA(o_nc[:, c0:c0 + 128], ncs[0:3, :], reads=[ncs])
                    TS_("dve", cacc[:, :], pre[:, 3:3 + Tq], cw[:, 3:4], None, ALU.mult, None, [pre, cw], [cacc])
                    for i_ in range(3):
                        STT("dve", cacc[:, :], pre[:, i_:i_ + Tq], cw[:, i_:i_ + 1], cacc[:, :],
                            ALU.mult, ALU.add, [pre, cw, cacc], [cacc])
                    ACT(raw[gi][:, :], cacc[:, :], AF.Silu, [cacc], [raw[gi]])
                    if gi < 2:
                        ACT(sq_[:, :], raw[gi][:, :], AF.Square, [raw[gi]], [sq_])
                        for c in range(NC):
                            MM(PB[3][0:Lc, 16 + gi * NC + c:16 + gi * NC + c + 1], sq_[:, c * Lc:(c + 1) * Lc], onesb[:, 0:1],
                               True, True, [sq_, onesb], [PB[3]])
                L_ = Lc
                ACT(tk[0:L_, :, 0], PB[3][0:L_, 16:16 + NC], AF.Ln, [cst], [tk, PB[3]], bias=cst[0:L_, 1:2])
                ACT(tk[0:L_, :, 1], PB[3][0:L_, 16 + NC:16 + 2 * NC], AF.Ln, [cst], [tk, PB[3]], bias=cst[0:L_, 1:2])
                ACT(tk[0:L_, :, 6], tk[0:L_, :, 0], AF.Exp, [tk], [tk], scale=-0.5)
                ACT(tk[0:L_, :, 5], tk[0:L_, :, 1], AF.Exp, [tk], [tk], scale=-0.5)
                TS_("dve", tk[0:L_, :, 6], tk[0:L_, :, 6], 128.0 ** -0.5, None, ALU.mult, None, [tk], [tk])
                CP("dve", tk[0:L_, :, 12], beta_all[0:L_, :, h], [beta_all], [tk])
                CP("dve", tk[0:L_, :, 14], g_all[0:L_, :, h], [g_all], [tk])
                pg = PB[2]
                MM(pg[0:L_, 0:NC], utri[0:L_, 0:L_], tk[0:L_, :, 14], True, True, [utri, tk], [pg])
                MM(pg[0:128, 32:32 + NC], onesf[0:L_, 0:128], tk[0:L_, :, 14], True, True, [onesf, tk], [pg])
                CP("dve", tk[0:L_, :, 2], pg[0:L_, 0:NC], [], [tk, pg])
                CP("dve", tk[:, :, 3], pg[:, 32:32 + NC], [], [tk, pg])
                STT("dve", tk[0:L_, :, 4], tk[0:L_, :, 1], -0.5, tk[0:L_, :, 2], ALU.mult, ALU.subtract, [tk], [tk])
                ACT(tk[0:L_, :, 7], tk[0:L_, :, 2], AF.Exp, [tk], [tk])
                ACT(tk[:, :, 13], tk[:, :, 3], AF.Exp, [tk], [tk])
                TT("dve", tk[0:L_, :, 8], tk[0:L_, :, 12], tk[0:L_, :, 5], ALU.mult, [tk], [tk])
                TT("dve", tk[0:L_, :, 9], tk[0:L_, :, 8], tk[0:L_, :, 7], ALU.mult, [tk], [tk])
                TT("dve", tk[0:L_, :, 10], tk[0:L_, :, 3], tk[0:L_, :, 2], ALU.subtract, [tk], [tk])
                ACT(tk[0:L_, :, 10], tk[0:L_, :, 10], AF.Exp, [tk], [tk])
                TT("dve", tk[0:L_, :, 10], tk[0:L_, :, 10], tk[0:L_, :, 5], ALU.mult, [tk], [tk])
                TT("dve", tk[0:L_, :, 11], tk[0:L_, :, 6], tk[0:L_, :, 7], ALU.mult, [tk], [tk])

                for c in range(NC):
                    csl = slice(c * L_, (c + 1) * L_)
                    pb = PB[c % 2]
                    pv = pbf(c % 2)
                    TR(pv[0:L_, 0:128], raw[1][:, csl], identb[:, :], [raw[1], identb], [pb])
                    TR(pv[0:L_, 128:256], raw[2][:, csl], identb[:, :], [raw[2], identb], [pb])
                    ACT(Rk[0:L_, c, :], pv[0:L_, 0:128], AF.Copy, [tk], [Rk, pb], scale=tk[0:L_, c, 9:10])
                    TS_("dve", ktl[0:L_, c, :], pv[0:L_, 0:128], tk[0:L_, c, 10:11], None, ALU.mult, None, [tk], [ktl, pb])
                    TS_("dve", vb[0:L_, c, :], pv[0:L_, 128:256], tk[0:L_, c, 12:13], None, ALU.mult, None, [tk], [vb, pb])

                for g in range(NG):
                    pr1, pkk, pqk = PB[4], PB[5], PB[6]
                    for cl in range(CPG):
                        c = g * CPG + cl
                        csl = slice(c * L_, (c + 1) * L_)
                        lsl = slice(cl * L_, (cl + 1) * L_)
                        MM(pr1[0:L_, lsl], tk[0:L_, c, 4:5].to_broadcast([L_, L_]), identf[0:L_, 0:L_], True, False,
                           [tk, identf], [pr1])
                        MM(pr1[0:L_, lsl], identb[0:L_, 0:L_], mincl[0:L_, 0:L_], False, True, [identb, mincl], [pr1])
                        MM(pkk[0:L_, lsl], raw[1][:, csl], raw[1][:, csl], True, True, [raw[1]], [pkk])
                        MM(pqk[0:L_, lsl], raw[0][:, csl], raw[1][:, csl], True, True, [raw[0], raw[1]], [pqk])
                    for cl in range(CPG):
                        c = g * CPG + cl
                        lsl = slice(cl * L_, (cl + 1) * L_)
                        ACT(Ecore[0:L_, lsl], pr1[0:L_, lsl], AF.Exp, [tk], [Ecore, pr1], bias=tk[0:L_, c, 2:3])
                    for cl in range(CPG):
                        lsl = slice(cl * L_, (cl + 1) * L_)
                        TT("pool", Estr[0:L_, lsl], Ecore[0:L_, lsl], sstrict[0:L_, 0:L_], ALU.mult, [Ecore, sstrict], [Estr])
                    X = Xb[0]
                    for cl in range(CPG):
                        c = g * CPG + cl
                        lsl = slice(cl * L_, (cl + 1) * L_)
                        STT("dve", X[0:L_, lsl], pkk[0:L_, lsl], tk[0:L_, c, 8:9], Estr[0:L_, lsl], ALU.mult, ALU.mult,
                            [tk, Estr], [X, pkk])
                        TT("dve", qkb[0:L_, lsl], pqk[0:L_, lsl], Ecore[0:L_, lsl], ALU.mult, [Ecore], [qkb, pqk])
                    if DBG and isP and si == 0 and l == 0 and h == 0 and g == 0:
                        DMA(O["dbg_X0"][:, 0:GW], X[:, :], reads=[X])
                    pt_ = PB[7]
                    ptv = pbf(7)
                    for cl in range(CPG):
                        lsl = slice(cl * L_, (cl + 1) * L_)
                        TR(ptv[0:L_, cl * L_:(cl + 1) * L_], X[0:L_, lsl], identb[0:L_, 0:L_], [X, identb], [pt_])
                        TR(ptv[0:L_, 512 + cl * L_:512 + (cl + 1) * L_], qkb[0:L_, lsl], identb[0:L_, 0:L_], [qkb, identb], [pt_])
                    IAt, P1sb = Yb[0], Yb[1]
                    TT("dve", IAt[0:L_, 0:GW], ptv[0:L_, 0:GW], identrep[0:L_, 0:GW], ALU.add, [identrep], [IAt, pt_])
                    CP("act", qkT[0:L_, g * GW:(g + 1) * GW], ptv[0:L_, 512:512 + GW], [], [qkT, pt_])
                    Dbuf = [Xb[1], Qb[0]]
                    Dtbuf = [Qb[1], Xb[0]]
                    Dc, Dtc = identrep, identrep
                    s_ = 1
                    li = 0
                    while s_ < L_:
                        s2 = 2 * s_
                        Dn, Dtn = Dbuf[li % 2], Dtbuf[li % 2]
                        px, py = PB[4], PB[5]
                        for cl in range(CPG):
                            lsl = slice(cl * L_, (cl + 1) * L_)
                            MM(px[0:L_, lsl], IAt[0:L_, lsl], Dc[0:L_, lsl], True, True, [IAt, Dc], [px])
                        CP("act", P1sb[0:L_, 0:GW], px[0:L_, 0:GW], [], [P1sb, px])
                        for cl in range(CPG):
                            lsl = slice(cl * L_, (cl + 1) * L_)
                            MM(py[0:L_, lsl], Dtc[0:L_, lsl], P1sb[0:L_, lsl], True, True, [Dtc, P1sb], [py])
                        if s2 < L_:
                            TT("dve", Estr[0:L_, 0:GW].rearrange("p (c f) -> p c f", f=L_),
                               py.t[0:L_, 0:GW].rearrange("p (c f) -> p c f", f=L_),
                               bmask[0:L_, li:li + 1, 0:L_].to_broadcast([L_, CPG, L_]), ALU.mult, [bmask], [Estr, py])
                            STT("dve", Dn[0:L_, 0:GW], Dc[0:L_, 0:GW], 2.0, Estr[0:L_, 0:GW], ALU.mult, ALU.subtract,
                                [Dc, Estr], [Dn])
                        else:
                            STT("dve", Dn[0:L_, 0:GW], Dc[0:L_, 0:GW], 2.0, py[0:L_, 0:GW], ALU.mult, ALU.subtract,
                                [Dc], [Dn, py])
                        pt2 = PB[6]
                        pt2v = pbf(6)
                        for cl in range(CPG):
                            lsl = slice(cl * L_, (cl + 1) * L_)
                            TR(pt2v[0:L_, lsl], Dn[0:L_, lsl], identb[0:L_, 0:L_], [Dn, identb], [pt2])
                        CP("act", Dtn[0:L_, 0:GW], pt2v[0:L_, 0:GW], [], [Dtn, pt2])
                        Dc, Dtc = Dn, Dtn
                        s_ = s2
                        li += 1
                    Qf = Dtc
                    CP("pool", Qall[0:L_, g * GW:(g + 1) * GW], Qf[0:L_, 0:GW], [Qf], [Qall])
                    pw = PB[7]
                    for cl in range(CPG):
                        c = g * CPG + cl
                        lsl = slice(cl * L_, (cl + 1) * L_)
                        MM(pw[:, lsl], Rk[0:L_, c, :], Qf[0:L_, lsl], True, True, [Rk, Qf], [pw])
                    ACT(wkT[:, g * GW:(g + 1) * GW], pw[:, 0:GW], AF.Copy, [], [wkT, pw], scale=-1.0)

                if isP:
                    MEMSET("dve", Sf[:], 0.0, [Sf])
                    MEMSET("pool", Sb[:], 0.0, [Sb])
                else:
                    DMA(Sf[:], I["sdelta"][l, h], writes=[Sf])
                    CP("act", Sb[:], Sf[:], [Sf], [Sb])
                for c in range(NC):
                    csl = slice(c * L_, (c + 1) * L_)
                    pu, po1, po2, pds = PB[0], PB[1], PB[2], PB[3]
                    MM(pu[0:L_, 0:128], Qall[0:L_, csl], vb[0:L_, c, :], True, False, [Qall, vb], [pu])
                    MM(pu[0:L_, 0:128], wkT[:, csl], Sb[:, :], False, True, [wkT, Sb], [pu])
                    MM(po1[0:L_, 0:128], raw[0][:, csl], Sb[:, :], True, True, [raw[0], Sb], [po1])
                    u_ = usb[c % 2]
                    CP("act", u_[0:L_, :], pu[0:L_, 0:128], [], [u_, pu])
                    MM(po2[0:L_, 0:128], qkT[0:L_, csl], u_[0:L_, :], True, True, [qkT, u_], [po2])
                    MM(pds[:, 0:128], ktl[0:L_, c, :], u_[0:L_, :], True, True, [ktl, u_], [pds])
                    o_ = ot[c % 2]
                    TS_("dve", o_[0:L_, :], po1[0:L_, 0:128], tk[0:L_, c, 11:12], None, ALU.mult, None, [tk], [o_, po1])
                    STT("dve", oa[0:L_, c, :], po2[0:L_, 0:128], tk[0:L_, c, 6:7], o_[0:L_, :], ALU.mult, ALU.add,
                        [tk, o_], [oa, po2])
                    STT("dve", Sf[:, :], Sf[:, :], tk[:, c, 13:14], pds[:, 0:128], ALU.mult, ALU.add, [Sf, tk], [Sf, pds])
                    CP("act", Sb[:, :], Sf[:, :], [Sf], [Sb])
                DMA(o_nd[h], Sf[:, :], reads=[Sf])
                if DBG and isP and si == 0 and l == 0 and h == 0:
                    DMA(O["dbg_tk"], tk[:, :, :].rearrange("p a b -> p (a b)"), reads=[tk])
                    for gi_ in range(3):
                        DMA(O["dbg_raw"][gi_], raw[gi_][:, :], reads=[raw[gi_]])
                    DMA(O["dbg_Q"], Qall[:, :], reads=[Qall])
                    DMA(O["dbg_qkT"], qkT[:, :], reads=[qkT])
                    DMA(O["dbg_wkT"], wkT[:, :], reads=[wkT])
                    DMA(O["dbg_oa"], oa[:, :, :].rearrange("p a b -> p (a b)"), reads=[oa])
                    DMA(O["dbg_E"][:, 0:GW], Ecore[:, :], reads=[Ecore])
                for b in range(NB):
                    ACT(obj[0:P, :], oa[0:P, b, :], AF.Square, [oa], [obj, sst], accum=sst[0:P, 4:5])
                    ACT(sst[0:P, 5:6], sst[0:P, 4:5], AF.Ln, [sst, cst], [sst], bias=cst[0:P, 0:1], scale=1.0 / 128)
                    ACT(sst[0:P, 6:7], sst[0:P, 5:6], AF.Exp, [sst], [sst], scale=-0.5)
                    STT("dve", oa[0:P, b, :], oa[0:P, b, :], sst[0:P, 6:7], gna[0:P, :], ALU.mult, ALU.mult,
                        [oa, sst, gna], [oa])
                    TT("pool", oa[0:P, b, :], oa[0:P, b, :], zs[0:P, b, :], ALU.mult, [oa, zs], [oa])
            else:
                MEMSET("dve", oa[:], 0.0, [oa])

            if DBG and isP and si == 0 and l == 0 and h == 0:
                DMA(O["dbg_ob"], ob[:, :, :].rearrange("p a b -> p (a b)"), reads=[ob])
            for b in range(NB):
                m_ = mg[b % 2]
                TT("pool", oa[0:P, b, :], oa[0:P, b, :], sga[0:P, b, :], ALU.mult, [oa, sga], [oa])
                TT("dve", ob[0:P, b, :], ob[0:P, b, :], sgb[0:P, b, :], ALU.mult, [ob, sgb], [ob])
                TT("pool", m_[0:P, :], oa[0:P, b, :], ob[0:P, b, :], ALU.add, [oa, ob], [m_])
                pb = PB[2 + (b % 2)]
                pv = pbf(2 + (b % 2))
                TR(pv[:, 0:P], m_[0:P, :], identb[0:P, 0:P], [m_, identb], [pb])
                ACT(mTh[:, b * P:(b + 1) * P], pv[:, 0:P], AF.Copy, [], [mTh, pb])
            DMA(mTd[h * 128:(h + 1) * 128, 0:Tq], mTh[:, :], reads=[mTh])

    for sq in seqs:
        Tq, P = sq["T"], sq["P"]
        NB = Tq // P
        isP = sq["kind"] == "p"
        x_in = I["xp"][sq["i"]] if isP else I["xs"][0]
        y_out = O["yp"][sq["i"]] if isP else O["ys"][0]
        TT_W = min(512, Tq)
        NTT = Tq // TT_W
        BPT = TT_W // P

        m0 = mark()
        gam = sb([128, D], F32, "gam")
        bet = sb([128, D], F32, "bet")
        bcast_row(gam, I["ln_in_g"])
        bcast_row(bet, I["ln_in_b"])
        xt = [sb([128, D], F32, "xt%d" % i) for i in range(2)]
        scr = [sb([128, 8], F32, "scr%d" % i) for i in range(2)]
        junk = sb([128, D], BF16, "junk")
        for b in range(NB):
            x = xt[b % 2]
            DMA(x[0:P, :], x_in[b * P:(b + 1) * P, :], writes=[x])
            layer_norm_block(P, x[0:P, :], x, x[0:P, :], x, gam, bet, scr[b % 2], junk)
            DMA(xsc[b * P:(b + 1) * P, :], x[0:P, :], reads=[x])
        S.barrier()
        release(m0)

        for l in range(L):
            last_layer = (l == L - 1)
            mlay = mark()
            hT = sb([128, 8, Tq], BF16, "hT")
            m0 = mark()
            opsc = sb([128, D], F32, "opsc")
            sh = sb([128, D], F32, "sh")
            bcast_row(sh, modd[l, sq["c"], 0:D])
            bcast_row(opsc, modd[l, sq["c"], D:2 * D])
            TS_("dve", opsc[:], opsc[:], 1.0, None, ALU.add, None, [opsc], [opsc])
            xt = [sb([128, D], F32, "xt%d" % i) for i in range(2)]
            rt = [sb([128, D], F32, "rt%d" % i) for i in range(2)]
            hb = [sb([128, D], BF16, "hb%d" % i) for i in range(2)]
            for b in range(NB):
                x = xt[b % 2]
                DMA(x[0:P, :], xsc[b * P:(b + 1) * P, :], writes=[x])
                build_hT(P, b, x, opsc, sh, rt[b % 2], hb[b % 2], hT)
            S.barrier()
            release(m0)

            m0 = mark()
            mixer(sq, l, hT)
            S.barrier()
            release(m0)

            m0 = mark()
            mT = sb([128, 8, Tq], BF16, "mT")
            DMA(mT[:], mTd[:, 0:Tq].rearrange("(k p) t -> p k t", p=128), writes=[mT])
            wo = sb([128, 8, D], BF16, "wo")
            DMA(wo[:], I["w_o"][l].rearrange("(k p) c -> p k c", p=128), writes=[wo], q="pool")
            gt = sb([128, D], F32, "gt1")
            gam = sb([128, D], F32, "gam1")
            bet = sb([128, D], F32, "bet1")
            opsc = sb([128, D], F32, "opsc2")
            sh = sb([128, D], F32, "sh2")
            bcast_row(gt, modd[l, sq["c"], 2 * D:3 * D])
            bcast_row(gam, I["ln1_g"][l])
            bcast_row(bet, I["ln1_b"][l])
            bcast_row(sh, modd[l, sq["c"], 3 * D:4 * D])
            bcast_row(opsc, modd[l, sq["c"], 4 * D:5 * D])
            TS_("dve", opsc[:], opsc[:], 1.0, None, ALU.add, None, [opsc], [opsc])
            xt = [sb([128, D], F32, "xt%d" % i) for i in range(2)]
            rt = [sb([128, D], F32, "rt%d" % i) for i in range(2)]
            hb = [sb([128, D], BF16, "hb%d" % i) for i in range(2)]
            scr = [sb([128, 8], F32, "scr%d" % i) for i in range(2)]
            junk = sb([128, D], BF16, "junk")
            for b in range(NB):
                x, r, hbb = xt[b % 2], rt[b % 2], hb[b % 2]
                DMA(x[0:P, :], xsc[b * P:(b + 1) * P, :], writes=[x])
                for half in range(2):
                    pbk = PB[half]
                    for k in range(8):
                        MM(pbk[0:P, :], mT[:, k, b * P:(b + 1) * P], wo[:, k, half * 512:(half + 1) * 512],
                           k == 0, k == 7, [mT, wo], [pbk])
                    TT("dve", r[0:P, half * 512:(half + 1) * 512], pbk[0:P, :], gt[0:P, half * 512:(half + 1) * 512],
                       ALU.mult, [gt], [r, pbk])
                STT("dve", r[0:P, :], x[0:P, :], ALPHA, r[0:P, :], ALU.mult, ALU.add, [x, r], [r])
                layer_norm_block(P, r[0:P, :], r, x[0:P, :], x, gam, bet, scr[b % 2], junk)
                DMA(xsc[b * P:(b + 1) * P, :], x[0:P, :], reads=[x])
                build_hT(P, b, x, opsc, sh, r, hbb, hT)
            S.barrier()
            release(m0)

            m0 = mark()
            wout = sb([128, 22, D], BF16, "wout")
            for j in range(22):
                DMA(wout[:, j, :], I["w_ff_out"][l][j * 128:(j + 1) * 128, :], writes=[wout], q="pool")
            gt = sb([128, D], F32, "gt2")
            gam = sb([128, D], F32, "gam2")
            bet = sb([128, D], F32, "bet2")
            bcast_row(gt, modd[l, sq["c"], 5 * D:6 * D])
            bcast_row(gam, I["ln2_g"][l])
            bcast_row(bet, I["ln2_b"][l])
            wfi = [sb([128, 8, 256], BF16, "wfi%d" % i) for i in range(3)]
            gT = sb([128, 22, TT_W], BF16, "gT")
            su = [sb([128, TT_W], BF16, "su%d" % i) for i in range(2)]
            xt = [sb([128, D], F32, "xt%d" % i) for i in range(2)]
            rt = [sb([128, D], F32, "rt%d" % i) for i in range(2)]
            scr = [sb([128, 8], F32, "scr%d" % i) for i in range(2)]
            junk = sb([128, D], BF16, "junk")
            wsrc = I["w_ff_in"][l].rearrange("(k p) c -> p k c", p=128)
            for tt in range(NTT):
                tsl = slice(tt * TT_W, (tt + 1) * TT_W)
                for j in range(22):
                    w = wfi[j % 3]
                    DMA(w[:, :, 0:128], wsrc[:, :, j * 128:(j + 1) * 128], writes=[w], q="pool")
                    DMA(w[:, :, 128:256], wsrc[:, :, DFF + j * 128:DFF + (j + 1) * 128], writes=[w], q="pool")
                    pu, pv_ = PB[4 + 2 * (j % 2)], PB[5 + 2 * (j % 2)]
                    for k in range(8):
                        MM(pu[:, 0:TT_W], w[:, k, 0:128], hT[:, k, tsl], k == 0, k == 7, [w, hT], [pu])
                    for k in range(8):
                        MM(pv_[:, 0:TT_W], w[:, k, 128:256], hT[:, k, tsl], k == 0, k == 7, [w, hT], [pv_])
                    s_ = su[j % 2]
                    ACT(s_[:, :], pu[:, 0:TT_W], AF.Silu, [], [s_, pu])
                    TT("dve", gT[:, j, :], pv_[:, 0:TT_W], s_[:, :], ALU.mult, [s_], [gT, pv_])
                for bb in range(BPT):
                    b = tt * BPT + bb
                    x, r = xt[b % 2], rt[b % 2]
                    DMA(x[0:P, :], xsc[b * P:(b + 1) * P, :], writes=[x])
                    for half in range(2):
                        pbk = PB[half]
                        for j in range(22):
                            MM(pbk[0:P, :], gT[:, j, bb * P:(bb + 1) * P], wout[:, j, half * 512:(half + 1) * 512],
                               j == 0, j == 21, [gT, wout], [pbk])
                        TT("dve", r[0:P, half * 512:(half + 1) * 512], pbk[0:P, :], gt[0:P, half * 512:(half + 1) * 512],
                           ALU.mult, [gt], [r, pbk])
                    STT("dve", r[0:P, :], x[0:P, :], ALPHA, r[0:P, :], ALU.mult, ALU.add, [x, r], [r])
                    layer_norm_block(P, r[0:P, :], r, x[0:P, :], x, gam, bet, scr[b % 2], junk)
                    if last_layer:
                        DMA(y_out[b * P:(b + 1) * P, :], x[0:P, :], reads=[x])
                    else:
                        DMA(xsc[b * P:(b + 1) * P, :], x[0:P, :], reads=[x])
            S.barrier()
            release(mlay)

    print('arena max words', amax[0], 'ops', S.nops)
    S.emit(ctx)
    return nc, consts


_CACHE = {}


def _get_program(cfg_key, cfg):
    if cfg_key not in _CACHE:
        _CACHE[cfg_key] = build_program(cfg)
    return _CACHE[cfg_key]


def kernel(x_prompt, x_sample, cache_k, cache_v, state_conv, state_delta, c_prompt, c_sample,
           ln_in_g, ln_in_b, rel_bias, w_ada, b_ada, w_in, conv_w, a_log, dt_bias, norm_a,
           lam, subln_g, w_o, ln1_g, ln1_b, w_ff_in, w_ff_out, ln2_g, ln2_b):
    NCORE = 8
    f = lambda a: np.ascontiguousarray(np.asarray(a, dtype=np.float32))
    x_prompt, x_sample, cache_k, cache_v = f(x_prompt), f(x_sample), f(cache_k), f(cache_v)
    state_conv, state_delta, c_prompt, c_sample = f(state_conv), f(state_delta), f(c_prompt), f(c_sample)
    B, T, _ = x_prompt.shape
    BS, TS, _ = x_sample.shape
    Ld = w_in.shape[0]
    PAST = cache_k.shape[2]
    NP = B // NCORE
    cfg = Cfg(depth=Ld, n_prompt=NP, T=T, with_sample=True, TS=TS, past=PAST)
    nc, consts = _get_program(("full", Ld, NP, T, TS, PAST), cfg)
    shared = {}
    for nm, arr in [("ln_in_g", ln_in_g), ("ln_in_b", ln_in_b), ("rel_bias", rel_bias), ("w_ada", w_ada), ("b_ada", b_ada),
                    ("w_in", w_in), ("conv_w", conv_w), ("a_log", a_log), ("dt_bias", dt_bias), ("norm_a", norm_a),
                    ("lam", lam), ("subln_g", subln_g), ("w_o", w_o), ("ln1_g", ln1_g), ("ln1_b", ln1_b),
                    ("w_ff_in", w_ff_in), ("w_ff_out", w_ff_out), ("ln2_g", ln2_g), ("ln2_b", ln2_b)]:
        shared[nm] = f(arr)
    for nm, arr in consts.items():
        shared["c_" + nm] = arr
    in_maps = []
    for c in range(NCORE):
        m = dict(shared)
        m["xp"] = x_prompt[c * NP:(c + 1) * NP]
        m["xs"] = x_sample[c:c + 1]
        m["cvec"] = np.ascontiguousarray(np.concatenate([c_prompt[c * NP:(c + 1) * NP], c_sample[c:c + 1]], axis=0))
        m["ck"] = np.ascontiguousarray(cache_k[:, c].reshape(Ld, PAST, D))
        m["cv"] = np.ascontiguousarray(cache_v[:, c].reshape(Ld, PAST, D))
        m["sconv"] = np.ascontiguousarray(state_conv[:, c])
        m["sdelta"] = np.ascontiguousarray(state_delta[:, c])
        in_maps.append(m)
    res = run_bass_kernel_spmd(nc, in_maps, core_ids=list(range(NCORE)))
    R = res.results
    cat = lambda key, ax: np.concatenate([np.asarray(r[key]) for r in R], axis=ax)
    y_p = cat("yp", 0)
    y_s = cat("ys", 0)
    nk_p = cat("nkp", 1).reshape(Ld, B, T, H, 128)
    nv_p = cat("nvp", 1).reshape(Ld, B, T, H, 128)
    nc_p = cat("ncp", 1)
    nd_p = cat("ndp", 1)
    nk_s = cat("nks", 1).reshape(Ld, BS, TS, H, 128)
    nv_s = cat("nvs", 1).reshape(Ld, BS, TS, H, 128)
    nc_s = cat("ncs", 1)
    nd_s = cat("nds", 1)
    return (y_p.astype(np.float32), y_s.astype(np.float32), nk_p.astype(np.float32), nv_p.astype(np.float32),
            nc_p.astype(np.float32), nd_p.astype(np.float32), nk_s.astype(np.float32), nv_s.astype(np.float32),
            nc_s.astype(np.float32), nd_s.astype(np.float32))
```

```python
import math
from contextlib import ExitStack
import numpy as np
import ml_dtypes
import concourse.bass as bass
import concourse.mybir as mybir
from concourse.bass_utils import run_bass_kernel_spmd

F32 = mybir.dt.float32
BF16 = mybir.dt.bfloat16
AF = mybir.ActivationFunctionType
ALU = mybir.AluOpType
AX = mybir.AxisListType

D = 1024
H = 8
DFF = 2816
DIN = 9232
NBK = 32
ALPHA = 8.0 ** 0.25
LN_EPS = 1e-5
NEG = -30000.0

NDMA = 8


class Res:
    __slots__ = ("name", "lw", "readers")

    def __init__(self, name=""):
        self.name = name
        self.lw = None
        self.readers = []


class Op:
    __slots__ = ("eng", "fn", "deps", "signal", "count", "is_dma", "dma_i", "idx")

    def __init__(self, eng, fn, is_dma):
        self.eng = eng
        self.fn = fn
        self.deps = []
        self.signal = False
        self.count = 0
        self.is_dma = is_dma
        self.dma_i = -1
        self.idx = -1


class TB:
    def __init__(self, t, name=""):
        self.t = t
        self.r = Res(name)

    def __getitem__(self, k):
        return self.t[k]


def _res(x):
    return x.r if isinstance(x, TB) else x


class Sched:
    ENGS = ("pe", "act", "dve", "pool", "sp")

    def __init__(self, nc):
        self.nc = nc
        self.streams = {e: [] for e in self.ENGS}
        self.ndma = {e: 0 for e in self.ENGS}
        self.all_dma = []
        self.pending_dma = []
        self.nops = 0

    def add(self, eng, fn, reads=(), writes=(), dma=False, extra=()):
        op = Op(eng, fn, dma)
        deps = list(extra)
        for r in reads:
            lw = _res(r).lw
            if lw is not None and not (lw.eng == "pe" and eng == "pe" and not lw.is_dma and not dma):
                deps.append(lw)
        for w in writes:
            w = _res(w)
            for rd in w.readers:
                if rd.is_dma or rd.eng != eng or dma:
                    deps.append(rd)
            lw = w.lw
            if lw is not None and (lw.is_dma or lw.eng != eng or dma):
                deps.append(lw)
        seen = set()
        for d in deps:
            if id(d) not in seen:
                seen.add(id(d))
                op.deps.append(d)
                d.signal = True
        for r in reads:
            _res(r).readers.append(op)
        for w in writes:
            w = _res(w)
            w.lw = op
            w.readers = []
        op.idx = len(self.streams[eng])
        self.streams[eng].append(op)
        if dma:
            op.dma_i = self.ndma[eng]
            self.ndma[eng] += 1
            op.signal = True
            self.all_dma.append(op)
            self.pending_dma.append(op)
        self.nops += 1
        return op

    def barrier(self):
        lasts = []
        for e in self.ENGS:
            for op in reversed(self.streams[e]):
                if op.fn is not None and not op.is_dma:
                    lasts.append(op)
                    break
        extra = lasts + self.pending_dma
        self.pending_dma = []
        for e in self.ENGS:
            self.add(e, None, extra=extra)

    def emit(self, ctx):
        nc = self.nc
        esem = {e: ctx.enter_context(nc.semaphore("s_" + e)) for e in self.ENGS}
        dsem = {}
        for e in self.ENGS:
            if self.ndma[e]:
                dsem[e] = [ctx.enter_context(nc.semaphore("d_%s%d" % (e, i))) for i in range(NDMA)]
        for e in self.ENGS:
            c = 0
            for op in self.streams[e]:
                if op.signal and not op.is_dma and op.fn is not None:
                    c += 1
                op.count = c

        def ev(d):
            if d.is_dma:
                return dsem[d.eng][d.dma_i % NDMA], 16 * (d.dma_i // NDMA + 1)
            return esem[d.eng], d.count

        streams = self.streams
        all_dma = self.all_dma

        def run(e, h):
            known = {}
            for op in streams[e]:
                need = {}
                for d in op.deps:
                    if d.fn is None:
                        continue
                    s, v = ev(d)
                    k = id(s)
                    if k not in need or need[k][1] < v:
                        need[k] = (s, v)
                if op.is_dma and op.dma_i >= NDMA:
                    s = dsem[e][op.dma_i % NDMA]
                    v = 16 * (op.dma_i // NDMA)
                    k = id(s)
                    if k not in need or need[k][1] < v:
                        need[k] = (s, v)
                for k, (s, v) in need.items():
                    if known.get(k, 0) < v:
                        h.wait_ge(s, v)
                        known[k] = v
                if op.fn is None:
                    continue
                ins = op.fn(h)
                if op.is_dma:
                    ins.then_inc(dsem[e][op.dma_i % NDMA], 16)
                elif op.signal:
                    ins.then_inc(esem[e], 1)
            if e == "sp":
                last = {}
                for d in all_dma:
                    s, v = ev(d)
                    k = id(s)
                    if k not in last or last[k][1] < v:
                        last[k] = (s, v)
                for k, (s, v) in last.items():
                    if known.get(k, 0) < v:
                        h.wait_ge(s, v)

        with nc.Block() as block:
            @block.sync
            def _(h):
                run("sp", h)

            @block.tensor
            def _(h):
                run("pe", h)

            @block.scalar
            def _(h):
                run("act", h)

            @block.vector
            def _(h):
                run("dve", h)

            @block.gpsimd
            def _(h):
                run("pool", h)


class Cfg:
    def __init__(self, depth=4, n_prompt=2, T=2048, with_sample=True, TS=32, past=2048,
                 do_attn=True, do_delta=True):
        self.depth = depth
        self.n_prompt = n_prompt
        self.T = T
        self.with_sample = with_sample
        self.TS = TS
        self.past = past
        self.do_attn = do_attn
        self.do_delta = do_delta


def t5_bucket_np(rel):
    import jax
    import jax.numpy as jnp
    with jax.default_device(jax.devices("cpu")[0]):
        return _t5_bucket_cpu(jnp, rel)


def _t5_bucket_cpu(jnp, rel):
    rel = jnp.asarray(rel, jnp.int32)
    nb = NBK // 2
    max_exact = nb // 2
    ret = jnp.where(rel > 0, nb, 0)
    n = jnp.abs(rel)
    nf = jnp.maximum(n, 1).astype(jnp.float32)
    large = max_exact + (jnp.log(nf / max_exact) / math.log(128 / max_exact) * (nb - max_exact)).astype(jnp.int32)
    large = jnp.minimum(large, nb - 1)
    return np.asarray(ret + jnp.where(n < max_exact, n, large))


def make_consts():
    c = {}
    c["identf"] = np.eye(128, dtype=np.float32)
    p = np.arange(128)[:, None]
    f = np.arange(128)[None, :]
    c["utri"] = (p <= f).astype(np.float32)
    c["onesf"] = np.ones((128, 128), np.float32)
    c["mincl"] = np.where(p >= f, 0.0, NEG).astype(np.float32)
    c["sstrict"] = (p > f).astype(np.float32)
    c["bmask"] = np.stack([((p // s2) == (f // s2)).astype(np.float32) for s2 in (2, 4, 8, 16, 32, 64)])
    kl = np.arange(128)[:, None]
    ql = np.arange(128)[None, :]
    bd = t5_bucket_np(kl - ql)
    bp = t5_bucket_np(kl - ql - 128)
    oh = np.zeros((2, NBK, 128, 128), np.float32)
    for b in range(NBK):
        oh[0, b] = (bd == b)
        oh[1, b] = (bp == b)
    c["bk_oh"] = oh
    vis = ((kl // 64) <= (ql // 64))
    c["amask"] = np.where(vis, 0.0, 8.0 * NEG).astype(np.float32)
    used = [sorted(set(np.unique(bd).tolist())), sorted(set(np.unique(bp).tolist()))]
    return c, used


def build_program(cfg):
    nc = bass.Bass("TRN2", target_bir_lowering=False)
    consts, used_buckets = make_consts()
    NP, T, L = cfg.n_prompt, cfg.T, cfg.depth
    NS = NP + (1 if cfg.with_sample else 0)
    TS, PAST = cfg.TS, cfg.past

    def din(name, shape, dt=F32):
        return nc.dram_tensor(name, list(shape), dt, kind="ExternalInput").ap()

    def dout(name, shape, dt=F32):
        return nc.dram_tensor(name, list(shape), dt, kind="ExternalOutput").ap()

    def dscr(name, shape, dt=F32):
        return nc.dram_tensor(name, list(shape), dt, kind="Internal").ap()

    I = {}
    I["xp"] = din("xp", [NP, T, D])
    I["cvec"] = din("cvec", [NS, D])
    if cfg.with_sample:
        I["xs"] = din("xs", [1, TS, D])
        I["ck"] = din("ck", [L, PAST, D])
        I["cv"] = din("cv", [L, PAST, D])
        I["sconv"] = din("sconv", [L, 3, 3072])
        I["sdelta"] = din("sdelta", [L, H, 128, 128])
    for nm, shp in [("ln_in_g", [D]), ("ln_in_b", [D]), ("rel_bias", [NBK, H]), ("w_ada", [L, D, 6 * D]),
                    ("b_ada", [L, 6 * D]), ("w_in", [L, D, DIN]), ("conv_w", [L, 4, 3072]), ("a_log", [L, H]),
                    ("dt_bias", [L, H]), ("norm_a", [L, 128]), ("lam", [L, 4, 64]), ("subln_g", [L, 128]),
                    ("w_o", [L, D, D]), ("ln1_g", [L, D]), ("ln1_b", [L, D]), ("w_ff_in", [L, D, 2 * DFF]),
                    ("w_ff_out", [L, DFF, D]), ("ln2_g", [L, D]), ("ln2_b", [L, D])]:
        I[nm] = din(nm, shp)
    for nm, arr in consts.items():
        I["c_" + nm] = din("c_" + nm, arr.shape)

    O = {}
    O["yp"] = dout("yp", [NP, T, D])
    O["nkp"] = dout("nkp", [L, NP, T, D])
    O["nvp"] = dout("nvp", [L, NP, T, D])
    O["ncp"] = dout("ncp", [L, NP, 3, 3072])
    O["ndp"] = dout("ndp", [L, NP, H, 128, 128])
    if cfg.with_sample:
        O["ys"] = dout("ys", [1, TS, D])
        O["nks"] = dout("nks", [L, 1, TS, D])
        O["nvs"] = dout("nvs", [L, 1, TS, D])
        O["ncs"] = dout("ncs", [L, 1, 3, 3072])
        O["nds"] = dout("nds", [L, 1, H, 128, 128])

    DBG = getattr(cfg, "dbg", False)
    if DBG:
        NCd = T // 128
        O["dbg_tk"] = dout("dbg_tk", [128, NCd * 16])
        O["dbg_raw"] = dout("dbg_raw", [3, 128, T], BF16)
        O["dbg_Q"] = dout("dbg_Q", [128, T], BF16)
        O["dbg_qkT"] = dout("dbg_qkT", [128, T], BF16)
        O["dbg_wkT"] = dout("dbg_wkT", [128, T], BF16)
        O["dbg_oa"] = dout("dbg_oa", [128, NCd * 128])
        O["dbg_ob"] = dout("dbg_ob", [128, NCd * 128])
        O["dbg_E"] = dout("dbg_E", [128, 512])
        O["dbg_X0"] = dout("dbg_X0", [128, 512], BF16)
    TMAX = max(T, TS)
    xsc = dscr("xsc", [TMAX, D])
    modd = dscr("modd", [L, NS, 6 * D])
    mTd = dscr("mTd", [D, TMAX], BF16)

    ctx = ExitStack()
    S = Sched(nc)

    AW = 53000
    arena_t = ctx.enter_context(nc.sbuf_tensor("arena", [128, AW], F32))
    aoff = [0]
    amax = [0]

    def sb(shape, dt, name):
        n = 1
        for v in shape[1:]:
            n *= v
        words = n if dt == F32 else (n + 1) // 2
        o = aoff[0]
        assert o + words <= AW, "arena overflow %s %d" % (name, o + words)
        aoff[0] = o + words
        amax[0] = max(amax[0], aoff[0])
        ap = arena_t[:, o:o + words]
        if dt != F32:
            ap = ap.bitcast(dt)[:, 0:n]
        if len(shape) == 3:
            ap = ap.rearrange("p (a b) -> p a b", b=shape[2])
        elif len(shape) == 4:
            ap = ap.rearrange("p (a b c) -> p a b c", b=shape[2], c=shape[3])
        return TB(ap, name)

    def mark():
        return aoff[0]

    def release(m):
        aoff[0] = m

    PB = [TB(ctx.enter_context(nc.psum_tensor("pb%d" % i, [128, 512], F32)), "pb%d" % i) for i in range(8)]

    def pbf(i):
        return PB[i].t[:].bitcast(BF16)

    def DMA(out, in_, reads=(), writes=(), q="sp"):
        return S.add(q, lambda h: h.dma_start(out=out, in_=in_), reads, writes, dma=True)

    def DMAS(out, in_, reads=(), writes=(), q="sp"):
        return S.add(q, lambda h: h.dma_start(out=out, in_=in_, allow_slow_non_contiguous=True), reads, writes, dma=True)

    def MM(out, lhsT, rhs, start, stop, reads, writes):
        return S.add("pe", lambda h: h.matmul(out, lhsT=lhsT, rhs=rhs, start=start, stop=stop), reads, writes)

    def TR(out, in_, ident, reads, writes):
        return S.add("pe", lambda h: h.transpose(out, in_, ident), reads, writes)

    def ACT(out, in_, func, reads, writes, bias=None, scale=None, accum=None):
        kw = {}
        if bias is not None:
            kw["bias"] = bias
        if scale is not None:
            kw["scale"] = scale
        if accum is not None:
            kw["accum_out"] = accum
        return S.add("act", lambda h: h.activation(out=out, in_=in_, func=func, **kw), reads, writes)

    def TS_(eng, out, in0, s1, s2, op0, op1, reads, writes):
        if s2 is None:
            return S.add(eng, lambda h: h.tensor_scalar(out=out, in0=in0, scalar1=s1, scalar2=None, op0=op0), reads, writes)
        return S.add(eng, lambda h: h.tensor_scalar(out=out, in0=in0, scalar1=s1, scalar2=s2, op0=op0, op1=op1), reads, writes)

    def TT(eng, out, in0, in1, op, reads, writes):
        return S.add(eng, lambda h: h.tensor_tensor(out=out, in0=in0, in1=in1, op=op), reads, writes)

    def STT(eng, out, in0, scalar, in1, op0, op1, reads, writes):
        return S.add(eng, lambda h: h.scalar_tensor_tensor(out=out, in0=in0, scalar=scalar, in1=in1, op0=op0, op1=op1), reads, writes)

    def CP(eng, out, in_, reads, writes):
        if eng == "act":
            return S.add("act", lambda h: h.activation(out=out, in_=in_, func=AF.Copy), reads, writes)
        return S.add(eng, lambda h: h.tensor_copy(out=out, in_=in_), reads, writes)

    def MEMSET(eng, ap, val, writes):
        return S.add(eng, lambda h: h.memset(ap, val), (), writes)

    def RECIP(out, in_, reads, writes):
        return S.add("dve", lambda h: h.reciprocal(out=out, in_=in_), reads, writes)

    def bcast_row(dst, row_ap):
        DMA(dst[:], row_ap.partition_broadcast(128), writes=[dst])

    identf = sb([128, 128], F32, "identf")
    identb = sb([128, 128], BF16, "identb")
    utri = sb([128, 128], F32, "utri")
    onesf = sb([128, 128], F32, "onesf")
    onesb = sb([128, 128], BF16, "onesb")
    mincl = sb([128, 512], BF16, "mincl")
    sstrict = sb([128, 512], F32, "sstrict")
    identrep = sb([128, 512], BF16, "identrep")
    cst = sb([128, 8], F32, "cst")
    cbias = sb([128, H], F32, "cbias")
    biasT = [sb([128, H, 128], BF16, "biasT%d" % t) for t in range(2)]
    bmask = sb([128, 6, 128], F32, "bmask")
    for i_ in range(6):
        DMA(bmask[:, i_, :], I["c_bmask"][i_], writes=[bmask])
    DMA(identf[:], I["c_identf"], writes=[identf])
    DMA(utri[:], I["c_utri"], writes=[utri])
    DMA(onesf[:], I["c_onesf"], writes=[onesf])
    CP("dve", identb[:], identf[:], [identf], [identb])
    CP("dve", onesb[:], onesf[:], [onesf], [onesb])
    m0 = mark()
    tmpc = sb([128, 128], F32, "tmpc")
    DMA(tmpc[:], I["c_mincl"], writes=[tmpc])
    for r4 in range(4):
        CP("dve", mincl[:, r4 * 128:(r4 + 1) * 128], tmpc[:], [tmpc], [mincl])
        CP("dve", identrep[:, r4 * 128:(r4 + 1) * 128], identf[:], [identf], [identrep])
        DMA(sstrict[:, r4 * 128:(r4 + 1) * 128], I["c_sstrict"], writes=[sstrict])
    MEMSET("dve", cst[:, 0:1], LN_EPS, [cst])
    MEMSET("dve", cst[:, 1:2], 1e-6, [cst])
    MEMSET("dve", cst[:, 2:3], 1.0, [cst])
    MEMSET("dve", cst[:, 3:4], 0.0, [cst])

    rbb = sb([128, NBK * H], F32, "rbb")
    DMA(rbb[:], I["rel_bias"].rearrange("b h -> (b h)").partition_broadcast(128), writes=[rbb])
    CP("dve", cbias[:], rbb[:, 15 * H:16 * H], [rbb], [cbias])
    rbd = sb([128, NBK * H], F32, "rbd")
    for b in range(NBK):
        TT("dve", rbd[:, b * H:(b + 1) * H], rbb[:, b * H:(b + 1) * H], cbias[:], ALU.subtract, [rbb, cbias], [rbd])
    TS_("dve", rbd[:], rbd[:], 8.0, None, ALU.mult, None, [rbd], [rbd])
    acc = sb([128, H, 128], F32, "bacc")
    ohs = [sb([128, 128], F32, "ohs%d" % i) for i in range(2)]
    cnt = 0
    for t in range(2):
        if t == 0:
            o_ = ohs[cnt % 2]
            cnt += 1
            DMA(o_[:], I["c_amask"], writes=[o_])
            for hh in range(H):
                CP("dve", acc[:, hh, :], o_[:], [o_], [acc])
        else:
            MEMSET("dve", acc[:], 0.0, [acc])
        for b in used_buckets[t]:
            o_ = ohs[cnt % 2]
            cnt += 1
            DMA(o_[:], I["c_bk_oh"][t, b], writes=[o_])
            for hh in range(H):
                STT("dve", acc[:, hh, :], o_[:], rbd[:, b * H + hh:b * H + hh + 1], acc[:, hh, :],
                    ALU.mult, ALU.add, [o_, rbd, acc], [acc])
        CP("dve", biasT[t][:], acc[:], [acc], [biasT[t]])
    S.barrier()
    release(m0)

    m0 = mark()
    cT = sb([128, 8, NS], F32, "cT")
    cTb = sb([128, 8, NS], BF16, "cTb")
    for s_i in range(NS):
        DMAS(cT[:, :, s_i], I["cvec"][s_i].rearrange("(k p) -> p k", p=128), writes=[cT])
    ACT(cTb[:], cT[:], AF.Silu, [cT], [cTb])
    wad = [sb([128, 8, 512], BF16, "wad%d" % i) for i in range(2)]
    bad = sb([128, 6 * D], F32, "bad")
    mrow = sb([128, 6 * D], F32, "mrow")
    for l in range(L):
        for s in range(NS):
            DMA(bad[s:s + 1, :], I["b_ada"][l:l + 1, :], writes=[bad])
        for ct in range(12):
            w = wad[ct % 2]
            DMA(w[:], I["w_ada"][l].rearrange("(k p) c -> p k c", p=128)[:, :, ct * 512:(ct + 1) * 512],
                writes=[w], q="pool")
            pbk = PB[ct % 2]
            for k in range(8):
                MM(pbk[0:NS, :], cTb[:, k, :], w[:, k, :], k == 0, k == 7, [cTb, w], [pbk])
            TT("dve", mrow[0:NS, ct * 512:(ct + 1) * 512], pbk[0:NS, :], bad[0:NS, ct * 512:(ct + 1) * 512], ALU.add,
               [bad], [mrow, pbk])
        DMA(modd[l], mrow[0:NS, :], reads=[mrow])
    S.barrier()
    release(m0)

    seqs = [dict(kind="p", i=i, T=T, P=128, c=i) for i in range(NP)]
    if cfg.with_sample:
        seqs.append(dict(kind="s", i=0, T=TS, P=TS, c=NP))

    def layer_norm_block(P, xin, xin_tb, xout, xout_tb, gam, bet, st, junk):
        S.add("dve", lambda h: h.reduce_sum(out=st[0:P, 0:1], in_=xin, axis=AX.X), [xin_tb], [st])
        TS_("dve", st[0:P, 1:2], st[0:P, 0:1], -1.0 / D, None, ALU.mult, None, [st], [st])
        ACT(junk[0:P, :], xin, AF.Square, [xin_tb, st], [junk, st], bias=st[0:P, 1:2], accum=st[0:P, 2:3])
        ACT(st[0:P, 3:4], st[0:P, 2:3], AF.Ln, [st, cst], [st], bias=cst[0:P, 0:1], scale=1.0 / D)
        ACT(st[0:P, 4:5], st[0:P, 3:4], AF.Exp, [st], [st], scale=-0.5)
        TS_("dve", xout, xin, st[0:P, 1:2], st[0:P, 4:5], ALU.add, ALU.mult, [xin_tb, st], [xout_tb])
        TT("pool", xout, xout, gam[0:P, :], ALU.mult, [xout_tb, gam], [xout_tb])
        TT("dve", xout, xout, bet[0:P, :], ALU.add, [xout_tb, bet], [xout_tb])

    def build_hT(P, b, x, opsc, sh, r, hbb, hT):
        TT("dve", r[0:P, :], x[0:P, :], opsc[0:P, :], ALU.mult, [x, opsc], [r])
        TT("pool", hbb[0:P, :], r[0:P, :], sh[0:P, :], ALU.add, [r, sh], [hbb])
        pb = PB[4 + (b % 2)]
        pv = pbf(4 + (b % 2)).rearrange("p (k t) -> p k t", k=8)
        for k in range(8):
            TR(pv[:, k, 0:P], hbb[0:P, k * 128:(k + 1) * 128], identb[0:P, 0:P], [hbb, identb], [pb])
        ACT(hT[:, :, b * P:(b + 1) * P], pv[:, :, 0:P], AF.Copy, [], [hT, pb])

    def mixer(sq, l, hT):
        Tq, P = sq["T"], sq["P"]
        NB = Tq // P
        isP = sq["kind"] == "p"
        si = sq["i"]
        TT_W = min(512, Tq)
        NTT = Tq // TT_W
        BPT = TT_W // P
        Lc = P
        NC = NB
        GW = min(4, NC) * Lc
        NG = (NC * Lc) // GW
        CPG = GW // Lc
        nlev = int(round(math.log2(Lc))) - 1
        lam_init = 0.8 - 0.6 * math.exp(-0.3 * l)
        KT = (PAST + TS) if not isP else Tq
        wsrc = I["w_in"][l].rearrange("(k p) c -> p k c", p=128)
        o_nk = O["nkp"][l, si] if isP else O["nks"][l, 0]
        o_nv = O["nvp"][l, si] if isP else O["nvs"][l, 0]
        o_nc = O["ncp"][l, si] if isP else O["ncs"][l, 0]
        o_nd = O["ndp"][l, si] if isP else O["nds"][l, 0]

        lamt = sb([128, 4, 64], F32, "lamt")
        DMA(lamt[:], I["lam"][l].partition_broadcast(128), writes=[lamt])
        lsc = sb([128, 8], F32, "lsc")
        ljunk = sb([128, 64], F32, "ljunk")
        TT("dve", ljunk[:], lamt[:, 0, :], lamt[:, 1, :], ALU.mult, [lamt], [ljunk])
        S.add("dve", lambda h: h.reduce_sum(out=lsc[:, 0:1], in_=ljunk[:], axis=AX.X), [ljunk], [lsc])
        TT("dve", ljunk[:], lamt[:, 2, :], lamt[:, 3, :], ALU.mult, [lamt, lsc], [ljunk])
        S.add("dve", lambda h: h.reduce_sum(out=lsc[:, 1:2], in_=ljunk[:], axis=AX.X), [ljunk], [lsc])
        ACT(lsc[:, 2:4], lsc[:, 0:2], AF.Exp, [lsc], [lsc])
        TT("dve", lsc[:, 4:5], lsc[:, 2:3], lsc[:, 3:4], ALU.subtract, [lsc], [lsc])
        TS_("dve", lsc[:, 5:6], lsc[:, 4:5], lam_init, -1.0, ALU.add, ALU.mult, [lsc], [lsc])
        neglam = lsc[:, 5:6]
        gsub = sb([128, 128], F32, "gsub")
        bcast_row(gsub, I["subln_g"][l])
        TS_("dve", gsub[:], gsub[:], 1.0 - lam_init, None, ALU.mult, None, [gsub], [gsub])
        gna = sb([128, 128], F32, "gna")
        bcast_row(gna, I["norm_a"][l])
        nea = sb([128, H], F32, "nea")
        dtb = sb([128, H], F32, "dtb")
        bcast_row(nea, I["a_log"][l])
        bcast_row(dtb, I["dt_bias"][l])
        ACT(nea[:], nea[:], AF.Exp, [nea], [nea])
        TS_("dve", nea[:], nea[:], -1.0, None, ALU.mult, None, [nea], [nea])

        beta_all = sb([128, NB, H], F32, "beta_all")
        g_all = sb([128, NB, H], F32, "g_all")
        bgw = sb([128, 8, 16], BF16, "bgw")
        DMA(bgw[:], wsrc[:, :, 4096:4112], writes=[bgw], q="pool")
        pbg = PB[0]
        pbgv = pbg.t[:, 0:NB * 16].rearrange("p (b c) -> p b c", c=16)
        for b in range(NB):
            for k in range(8):
                MM(pbg[0:P, b * 16:(b + 1) * 16], hT[:, k, b * P:(b + 1) * P], bgw[:, k, :], k == 0, k == 7, [hT, bgw], [pbg])
        ACT(beta_all[0:P, :, :], pbgv[0:P, :, 0:8], AF.Sigmoid, [], [beta_all, pbg])
        for b in range(NB):
            TT("dve", g_all[0:P, b, :], pbgv[0:P, b, 8:16], dtb[0:P, :], ALU.add, [dtb], [g_all, pbg])
        ACT(g_all[0:P, :, :], g_all[0:P, :, :], AF.Exp, [g_all], [g_all])
        ACT(g_all[0:P, :, :], g_all[0:P, :, :], AF.Ln, [g_all, cst], [g_all], bias=cst[0:P, 2:3])
        for b in range(NB):
            TT("dve", g_all[0:P, b, :], g_all[0:P, b, :], nea[0:P, :], ALU.mult, [g_all, nea], [g_all])

        NPB = PAST // 128
        if not isP:
            ckb = sb([128, NPB, D], BF16, "ckb")
            cvb = sb([128, NPB, D], BF16, "cvb")
            DMA(ckb[:], I["ck"][l].rearrange("(b p) c -> p b c", p=128), writes=[ckb], q="pool")
            DMA(cvb[:], I["cv"][l].rearrange("(b p) c -> p b c", p=128), writes=[cvb], q="pool")

        wq = [sb([128, 8, 128], BF16, "wq%d" % i) for i in range(5)]
        wtm = sb([128, 8, 512], BF16, "wtm")
        qT = sb([128, Tq], BF16, "qT")
        kT = [sb([128, KT], BF16, "kT%d" % m) for m in range(2)]
        MEMSET("pool", kT[0][64:128, :], 0.0, [kT[0]])
        MEMSET("pool", kT[1][0:64, :], 0.0, [kT[1]])
        Vh = sb([128, NB, 130], BF16, "Vh")
        MEMSET("pool", Vh[:, :, 128:130], 1.0, [Vh])
        zs = sb([128, NB, 128], BF16, "zs")
        sga = sb([128, NB, 128], BF16, "sga")
        sgb = sb([128, NB, 128], BF16, "sgb")
        ob = sb([128, NB, 128], F32, "ob")
        oa = sb([128, NB, 128], F32, "oa")
        NKG_ = ((NB if isP else (PAST // 128 + 1)) + 3) // 4
        PT = [[sb([128, NKG_, 4 * P], BF16, "PT%d%d" % (m, i)) for i in range(2)] for m in range(2)]
        rz = sb([128, 16], F32, "rz")
        t1 = [sb([128, 128], F32, "t1%d" % i) for i in range(2)]
        obj = sb([128, 128], BF16, "obj")
        sst = sb([128, 8], F32, "sst")
        nst = sb([128, 3 * NB], F32, "nst")
        nst2 = sb([128, 3 * NB], F32, "nst2")
        mg = [sb([128, 128], BF16, "mg%d" % i) for i in range(2)]
        mTh = sb([128, Tq], BF16, "mTh")
        raw = [sb([128, Tq], BF16, "raw%d" % i) for i in range(3)]
        tk = sb([128, NC, 16], F32, "tk")
        Rk = sb([128, NC, 128], BF16, "Rk")
        ktl = sb([128, NC, 128], BF16, "ktl")
        vb = sb([128, NC, 128], BF16, "vb")
        Ecore = sb([128, GW], F32, "Ecore")
        qkb = sb([128, GW], BF16, "qkb")
        Qall = sb([128, NC * Lc], BF16, "Qall")
        qkT = sb([128, NC * Lc], BF16, "qkT")
        wkT = sb([128, NC * Lc], BF16, "wkT")
        Sf = sb([128, 128], F32, "Sf")
        Sb = sb([128, 128], BF16, "Sb")
        usb = [sb([128, 128], BF16, "usb%d" % i) for i in range(2)]
        ot = [sb([128, 128], F32, "ot%d" % i) for i in range(2)]

        def load_weights(h):
            cols = head_cols(h)
            for i_, nm in enumerate(["qa", "ka", "va", "qb", "kb"]):
                DMA(wq[i_][:], wsrc[:, :, cols[nm]:cols[nm] + 128], writes=[wq[i_]], q="pool")
            for i_, nm in enumerate(["z", "vb", "ga", "gb"]):
                DMA(wtm[:, :, i_ * 128:(i_ + 1) * 128], wsrc[:, :, cols[nm]:cols[nm] + 128], writes=[wtm], q="pool")

        def head_cols(h):
            return dict(qa=h * 128, ka=1024 + h * 128, va=2048 + h * 128, z=3072 + h * 128,
                        qb=4112 + h * 128, kb=5136 + h * 128, vb=6160 + h * 128,
                        ga=7184 + h * 128, gb=8208 + h * 128)

        def proj(h):
            cols = head_cols(h)
            vo, ko, pre, cacc, sq_, ncs = PJ["vo"], PJ["ko"], PJ["pre"], PJ["cacc"], PJ["sq_"], PJ["ncs"]
            ssq_ops = []
            for b in range(NB):
                bsl = slice(b * P, (b + 1) * P)
                pbk = PB[b % 2]
                for k in range(8):
                    MM(pbk[0:P, :], hT[:, k, bsl], wtm[:, k, :], k == 0, k == 7, [hT, wtm], [pbk])
                ACT(zs[0:P, b, :], pbk[0:P, 0:128], AF.Silu, [], [zs, pbk])
                v_ = vo[b % 2]
                CP("dve", v_[0:P, :], pbk[0:P, 128:256], [], [v_, pbk])
                ACT(sga[0:P, b, :], pbk[0:P, 256:384], AF.Sigmoid, [], [sga, pbk])
                ACT(sgb[0:P, b, :], pbk[0:P, 384:512], AF.Sigmoid, [], [sgb, pbk])
                DMA(o_nv[bsl, h * 128:(h + 1) * 128], v_[0:P, :], reads=[v_])
                CP("pool", Vh[0:P, b, 0:128], v_[0:P, :], [v_], [Vh])
                pk = PB[2 + (b % 2)]
                for k in range(8):
                    MM(pk[0:P, 0:128], hT[:, k, bsl], wq[4][:, k, :], k == 0, k == 7, [hT, wq[4]], [pk])
                k_ = ko[b % 2]
                CP("dve", k_[0:P, :], pk[0:P, 0:128], [], [k_, pk])
                DMA(o_nk[bsl, h * 128:(h + 1) * 128], k_[0:P, :], reads=[k_])

            koff = 0 if isP else PAST
            for tt in range(NTT):
                tsl = slice(tt * TT_W, (tt + 1) * TT_W)
                pq = PB[4 + 2 * (tt % 2)]
                for k in range(8):
                    MM(pq[:, 0:TT_W], wq[3][:, k, :], hT[:, k, tsl], k == 0, k == 7, [wq[3], hT], [pq])
                ACT(qT[:, tsl], pq[:, 0:TT_W], AF.Copy, [], [qT, pq])
                pk = PB[5 + 2 * (tt % 2)]
                for k in range(8):
                    MM(pk[:, 0:TT_W], wq[4][:, k, :], hT[:, k, tsl], k == 0, k == 7, [wq[4], hT], [pk])
                ksl = slice(koff + tt * TT_W, koff + (tt + 1) * TT_W)
                ACT(kT[0][0:64, ksl], pk[0:64, 0:TT_W], AF.Copy, [], [kT[0], pk])
                CP("dve", kT[1][64:128, ksl], pk[64:128, 0:TT_W], [], [kT[1], pk])
            if not isP:
                for kb in range(NPB):
                    pb = PB[2 + (kb % 2)]
                    pv = pbf(2 + (kb % 2))
                    TR(pv[:, 0:128], ckb[:, kb, h * 128:(h + 1) * 128], identb[:, :], [ckb, identb], [pb])
                    ACT(kT[0][0:64, kb * 128:(kb + 1) * 128], pv[0:64, 0:128], AF.Copy, [], [kT[0], pb])
                    CP("dve", kT[1][64:128, kb * 128:(kb + 1) * 128], pv[64:128, 0:128], [], [kT[1], pb])
            if cfg.do_delta:
                for gi, nm in enumerate(["qa", "ka", "va"]):
                    c0 = cols[nm]
                    cw = PJ["cw"][gi]
                    DMAS(cw[:], I["conv_w"][l][:, c0:c0 + 128].rearrange("i c -> c i"), writes=[cw])
                    if isP:
                        MEMSET("pool", pre[:, 0:3], 0.0, [pre])
                    else:
                        DMAS(pre[:, 0:3], I["sconv"][l][:, c0:c0 + 128].rearrange("t c -> c t"), writes=[pre])
                    for tt in range(NTT):
                        tsl = slice(tt * TT_W, (tt + 1) * TT_W)
                        pq = PB[tt % 2]
                        for k in range(8):
                            MM(pq[:, 0:TT_W], wq[gi][:, k, :], hT[:, k, tsl], k == 0, k == 7, [wq[gi], hT], [pq])
                        ACT(pre[:, 3 + tt * TT_W:3 + (tt + 1) * TT_W], pq[:, 0:TT_W], AF.Copy, [], [pre, pq])
                    pn = PB[2]
                    for k in range(8):
                        MM(pn[0:3, 0:128], hT[:, k, Tq - 3:Tq], wq[gi][:, k, :], k == 0, k == 7, [hT, wq[gi]], [pn])
                    CP("dve", ncs[0:3, :], pn[0:3, 0:128], [], [ncs, pn])
                    DMA(o_nc[:, c0:c0 + 128], ncs[0:3, :], reads=[ncs])
                    TS_("dve", cacc[:, :], pre[:, 3:3 + Tq], cw[:, 3:4], None, ALU.mult, None, [pre, cw], [cacc])
                    for i_ in range(3):
                        STT("dve", cacc[:, :], pre[:, i_:i_ + Tq], cw[:, i_:i_ + 1], cacc[:, :],
                            ALU.mult, ALU.add, [pre, cw, cacc], [cacc])
                    ACT(raw[gi][:, :], cacc[:, :], AF.Silu, [cacc], [raw[gi]])
                for gi in range(2):
                    ACT(sq_[:, :], raw[gi][:, :], AF.Square, [raw[gi]], [sq_])
                    for c in range(NC):
                        MM(PB[3][0:Lc, 16 + gi * NC + c:16 + gi * NC + c + 1], sq_[:, c * Lc:(c + 1) * Lc], onesb[:, 0:1],
                           True, True, [sq_, onesb], [PB[3]])
                ACT(tk[0:Lc, :, 0], PB[3][0:Lc, 16:16 + NC], AF.Ln, [cst], [tk, PB[3]], bias=cst[0:Lc, 1:2])
                ACT(tk[0:Lc, :, 1], PB[3][0:Lc, 16 + NC:16 + 2 * NC], AF.Ln, [cst], [tk, PB[3]], bias=cst[0:Lc, 1:2])

        gcnt = [0]

        def attn_gen(h):
            if isP:
                kblocks = [dict(kp=128, col=kb * 128, v=Vh[:, kb, 0:129], vtb=Vh) for kb in range(NB)]
            else:
                kblocks = [dict(kp=128, col=kb * 128, v=cvb[:, kb, h * 128:(h + 1) * 128], vtb=cvb) for kb in range(NPB)]
                kblocks.append(dict(kp=TS, col=PAST, v=Vh[0:TS, 0, 0:128], vtb=Vh))
            NKB = len(kblocks)
            QP = P

            def qk_phase(qb):
                par = qb % 2
                nvis = (qb + 1) if isP else NKB
                nkg = (nvis + 3) // 4
                for m in range(2):
                    for kg in range(nkg):
                        sp_ = PB[4 + (gcnt[0] % 2)]
                        gcnt[0] += 1
                        kbs = list(range(kg * 4, min(nvis, kg * 4 + 4)))
                        KPg = kblocks[kbs[0]]["kp"]
                        for kb in kbs:
                            kd = kblocks[kb]
                            KP = kd["kp"]
                            assert KP == KPg
                            cl = kb - kg * 4
                            biasl = []
                            if isP:
                                if kb == qb:
                                    biasl.append((0, 128, 128))
                                if kb == qb - 1:
                                    biasl.append((1, 128, 128))
                            else:
                                if kb == NPB - 1:
                                    biasl.append((1, 128, TS))
                                if kb == NPB:
                                    biasl.append((0, TS, TS))
                            MM(sp_[0:KP, cl * QP:(cl + 1) * QP], kT[m][:, kd["col"]:kd["col"] + KP], qT[:, qb * QP:(qb + 1) * QP],
                               True, len(biasl) == 0, [kT[m], qT], [sp_])
                            for bi, (typ, kp_, qp_) in enumerate(biasl):
                                MM(sp_[0:kp_, cl * QP:cl * QP + qp_], identb[0:kp_, 0:kp_], biasT[typ][0:kp_, h, 0:qp_],
                                   False, bi == len(biasl) - 1, [identb, biasT[typ]], [sp_])
                        ACT(PT[m][par][0:KPg, kg, 0:len(kbs) * QP], sp_[0:KPg, 0:len(kbs) * QP], AF.Exp, [cbias], [PT[m][par], sp_],
                            bias=cbias[0:KPg, h:h + 1], scale=0.125)
                    yield

            def pv_phase(qb):
                par = qb % 2
                VW = 129 if isP else 128
                OBK = PB[6 + (qb % 2)] if isP else PB[6]
                nvis = (qb + 1) if isP else NKB
                for m in range(2):
                    for kb in range(nvis):
                        kd = kblocks[kb]
                        KP = kd["kp"]
                        MM(OBK[0:QP, m * 256:m * 256 + VW], PT[m][par][0:KP, kb // 4, (kb % 4) * QP:(kb % 4 + 1) * QP], kd["v"],
                           kb == 0, kb == nvis - 1, [PT[m][par], kd["vtb"]], [OBK])
                    if not isP:
                        for kb in range(nvis):
                            kd = kblocks[kb]
                            KP = kd["kp"]
                            MM(PB[7][0:QP, m:m + 1], PT[m][par][0:KP, kb // 4, (kb % 4) * QP:(kb % 4 + 1) * QP], onesb[0:KP, 0:1],
                               kb == 0, kb == nvis - 1, [PT[m][par], onesb], [PB[7]])
                    if m == 0:
                        yield
                if isP:
                    RECIP(rz[0:QP, 0:1], OBK[0:QP, 128:129], [], [rz, OBK])
                    RECIP(rz[0:QP, 1:2], OBK[0:QP, 384:385], [], [rz, OBK])
                else:
                    RECIP(rz[0:QP, 0:2], PB[7][0:QP, 0:2], [], [rz, PB[7]])
                TS_("dve", rz[0:QP, 2:4], rz[0:QP, 0:2], neglam[0:QP, :], None, ALU.mult, None, [rz, lsc], [rz])
                TS_("dve", ob[0:QP, qb, :], OBK[0:QP, 0:128], rz[0:QP, 0:1], None, ALU.mult, None, [rz], [ob, OBK])
                STT("dve", ob[0:QP, qb, :], OBK[0:QP, 256:384], rz[0:QP, 3:4], ob[0:QP, qb, :], ALU.mult, ALU.add, [rz, ob], [ob, OBK])
                ACT(obj[0:QP, :], ob[0:QP, qb, :], AF.Square, [ob], [obj, nst2], accum=nst2[0:QP, qb:qb + 1])

            yield from qk_phase(0)
            for qb in range(NB):
                if qb + 1 < NB:
                    yield from qk_phase(qb + 1)
                yield from pv_phase(qb)
                yield
            ACT(nst2[0:QP, NB:2 * NB], nst2[0:QP, 0:NB], AF.Ln, [nst2, cst], [nst2], bias=cst[0:QP, 0:1], scale=1.0 / 128)
            ACT(nst2[0:QP, 2 * NB:3 * NB], nst2[0:QP, NB:2 * NB], AF.Exp, [nst2], [nst2], scale=-0.5)
            for qb in range(NB):
                STT("dve", ob[0:QP, qb, :], ob[0:QP, qb, :], nst2[0:QP, 2 * NB + qb:2 * NB + qb + 1], gsub[0:QP, :], ALU.mult, ALU.mult,
                    [ob, nst2, gsub], [ob])
            yield

        def delta_gen(h):
            L_ = Lc
            Estr, Ering, Xb, NWT = NW["Estr"], NW["Ering"], NW["Xb"], NW["NWT"]
            ACT(tk[0:L_, :, 6], tk[0:L_, :, 0], AF.Exp, [tk], [tk], scale=-0.5)
            ACT(tk[0:L_, :, 5], tk[0:L_, :, 1], AF.Exp, [tk], [tk], scale=-0.5)
            TS_("dve", tk[0:L_, :, 6], tk[0:L_, :, 6], 128.0 ** -0.5, None, ALU.mult, None, [tk], [tk])
            CP("dve", tk[0:L_, :, 12], beta_all[0:L_, :, h], [beta_all], [tk])
            CP("dve", tk[0:L_, :, 14], g_all[0:L_, :, h], [g_all], [tk])
            pg = PB[2]
            MM(pg[0:L_, 0:NC], utri[0:L_, 0:L_], tk[0:L_, :, 14], True, True, [utri, tk], [pg])
            MM(pg[0:128, 32:32 + NC], onesf[0:L_, 0:128], tk[0:L_, :, 14], True, True, [onesf, tk], [pg])
            CP("dve", tk[0:L_, :, 2], pg[0:L_, 0:NC], [], [tk, pg])
            CP("dve", tk[:, :, 3], pg[:, 32:32 + NC], [], [tk, pg])
            STT("dve", tk[0:L_, :, 4], tk[0:L_, :, 1], -0.5, tk[0:L_, :, 2], ALU.mult, ALU.subtract, [tk], [tk])
            ACT(tk[0:L_, :, 7], tk[0:L_, :, 2], AF.Exp, [tk], [tk])
            ACT(tk[:, :, 13], tk[:, :, 3], AF.Exp, [tk], [tk])
            TT("dve", tk[0:L_, :, 8], tk[0:L_, :, 12], tk[0:L_, :, 5], ALU.mult, [tk], [tk])
            TT("dve", tk[0:L_, :, 9], tk[0:L_, :, 8], tk[0:L_, :, 7], ALU.mult, [tk], [tk])
            TT("dve", tk[0:L_, :, 10], tk[0:L_, :, 3], tk[0:L_, :, 2], ALU.subtract, [tk], [tk])
            ACT(tk[0:L_, :, 10], tk[0:L_, :, 10], AF.Exp, [tk], [tk])
            TT("dve", tk[0:L_, :, 10], tk[0:L_, :, 10], tk[0:L_, :, 5], ALU.mult, [tk], [tk])
            TT("dve", tk[0:L_, :, 11], tk[0:L_, :, 6], tk[0:L_, :, 7], ALU.mult, [tk], [tk])

            for c in range(NC):
                csl = slice(c * L_, (c + 1) * L_)
                pb = PB[c % 2]
                pv = pbf(c % 2)
                TR(pv[0:L_, 0:128], raw[1][:, csl], identb[:, :], [raw[1], identb], [pb])
                TR(pv[0:L_, 128:256], raw[2][:, csl], identb[:, :], [raw[2], identb], [pb])
                ACT(Rk[0:L_, c, :], pv[0:L_, 0:128], AF.Copy, [tk], [Rk, pb], scale=tk[0:L_, c, 9:10])
                TS_("dve", ktl[0:L_, c, :], pv[0:L_, 0:128], tk[0:L_, c, 10:11], None, ALU.mult, None, [tk], [ktl, pb])
                TS_("dve", vb[0:L_, c, :], pv[0:L_, 128:256], tk[0:L_, c, 12:13], None, ALU.mult, None, [tk], [vb, pb])
                if c % 4 == 3:
                    yield

            for g in range(NG):
                pr1, pkk, pqk = PB[0], PB[1], PB[2]
                for cl in range(CPG):
                    c = g * CPG + cl
                    csl = slice(c * L_, (c + 1) * L_)
                    lsl = slice(cl * L_, (cl + 1) * L_)
                    MM(pr1[0:L_, lsl], tk[0:L_, c, 4:5].to_broadcast([L_, L_]), identf[0:L_, 0:L_], True, False,
                       [tk, identf], [pr1])
                    MM(pr1[0:L_, lsl], identb[0:L_, 0:L_], mincl[0:L_, 0:L_], False, True, [identb, mincl], [pr1])
                    MM(pkk[0:L_, lsl], raw[1][:, csl], raw[1][:, csl], True, True, [raw[1]], [pkk])
                    MM(pqk[0:L_, lsl], raw[0][:, csl], raw[1][:, csl], True, True, [raw[0], raw[1]], [pqk])
                for cl in range(CPG):
                    c = g * CPG + cl
                    lsl = slice(cl * L_, (cl + 1) * L_)
                    ACT(Ecore[0:L_, lsl], pr1[0:L_, lsl], AF.Exp, [tk], [Ecore, pr1], bias=tk[0:L_, c, 2:3])
                for cl in range(CPG):
                    lsl = slice(cl * L_, (cl + 1) * L_)
                    TT("pool", Estr[0:L_, lsl], Ecore[0:L_, lsl], sstrict[0:L_, 0:L_], ALU.mult, [Ecore, sstrict], [Estr])
                X = Xb[0]
                for cl in range(CPG):
                    c = g * CPG + cl
                    lsl = slice(cl * L_, (cl + 1) * L_)
                    STT("dve", X[0:L_, lsl], pkk[0:L_, lsl], tk[0:L_, c, 8:9], Estr[0:L_, lsl], ALU.mult, ALU.mult,
                        [tk, Estr], [X, pkk])
                    TT("dve", qkb[0:L_, lsl], pqk[0:L_, lsl], Ecore[0:L_, lsl], ALU.mult, [Ecore], [qkb, pqk])
                pt_ = PB[3]
                ptv = pbf(3)
                for cl in range(CPG):
                    lsl = slice(cl * L_, (cl + 1) * L_)
                    TR(ptv[0:L_, cl * L_:(cl + 1) * L_], X[0:L_, lsl], identb[0:L_, 0:L_], [X, identb], [pt_])
                    TR(ptv[0:L_, 512 + cl * L_:512 + (cl + 1) * L_], qkb[0:L_, lsl], identb[0:L_, 0:L_], [qkb, identb], [pt_])
                IAt = NWT[g]["IAt"]
                TT("dve", IAt[0:L_, 0:GW], ptv[0:L_, 0:GW], identrep[0:L_, 0:GW], ALU.add, [identrep], [IAt, pt_])
                CP("act", qkT[0:L_, g * GW:(g + 1) * GW], ptv[0:L_, 512:512 + GW], [], [qkT, pt_])
                yield
            Dc = [identrep] * NG
            Dtc = [identrep] * NG
            s_ = 1
            li = 0
            while s_ < L_:
                s2 = 2 * s_
                for g in range(NG):
                    pbk = PB[g % 4]
                    IAt, P1sb = NWT[g]["IAt"], NWT[g]["P1"]
                    for cl in range(CPG):
                        lsl = slice(cl * L_, (cl + 1) * L_)
                        MM(pbk[0:L_, lsl], IAt[0:L_, lsl], Dc[g][0:L_, lsl], True, True, [IAt, Dc[g]], [pbk])
                    CP("act", P1sb[0:L_, 0:GW], pbk[0:L_, 0:GW], [], [P1sb, pbk])
                yield
                Dn_l = []
                for g in range(NG):
                    pbk = PB[g % 4]
                    P1sb = NWT[g]["P1"]
                    Dn = NWT[g]["D"][li % 2]
                    for cl in range(CPG):
                        lsl = slice(cl * L_, (cl + 1) * L_)
                        MM(pbk[0:L_, lsl], Dtc[g][0:L_, lsl], P1sb[0:L_, lsl], True, True, [Dtc[g], P1sb], [pbk])
                    if s2 < L_:
                        tmpm = NWT[g]["P1"]
                        et = Ering[g % 2]
                        TT("dve", et[0:L_, 0:GW].rearrange("p (c f) -> p c f", f=L_),
                           pbk.t[0:L_, 0:GW].rearrange("p (c f) -> p c f", f=L_),
                           bmask[0:L_, li:li + 1, 0:L_].to_broadcast([L_, CPG, L_]), ALU.mult, [bmask], [et, pbk])
                        STT("dve", Dn[0:L_, 0:GW], Dc[g][0:L_, 0:GW], 2.0, et[0:L_, 0:GW], ALU.mult, ALU.subtract,
                            [Dc[g], et], [Dn])
                    else:
                        STT("dve", Dn[0:L_, 0:GW], Dc[g][0:L_, 0:GW], 2.0, pbk[0:L_, 0:GW], ALU.mult, ALU.subtract,
                            [Dc[g]], [Dn, pbk])
                    Dn_l.append(Dn)
                yield
                for g in range(NG):
                    pbk = PB[g % 4]
                    pbv = pbf(g % 4)
                    Dn = Dn_l[g]
                    Dtn = NWT[g]["Dt"]
                    for cl in range(CPG):
                        lsl = slice(cl * L_, (cl + 1) * L_)
                        TR(pbv[0:L_, lsl], Dn[0:L_, lsl], identb[0:L_, 0:L_], [Dn, identb], [pbk])
                    CP("act", Dtn[0:L_, 0:GW], pbv[0:L_, 0:GW], [], [Dtn, pbk])
                    Dc[g] = Dn
                    Dtc[g] = Dtn
                yield
                s_ = s2
                li += 1
            for g in range(NG):
                Qf = Dtc[g]
                CP("pool", Qall[0:L_, g * GW:(g + 1) * GW], Qf[0:L_, 0:GW], [Qf], [Qall])
                pw = PB[g % 4]
                for cl in range(CPG):
                    c = g * CPG + cl
                    lsl = slice(cl * L_, (cl + 1) * L_)
                    MM(pw[:, lsl], Rk[0:L_, c, :], Qf[0:L_, lsl], True, True, [Rk, Qf], [pw])
                ACT(wkT[:, g * GW:(g + 1) * GW], pw[:, 0:GW], AF.Copy, [], [wkT, pw], scale=-1.0)
            yield

            yield "SCAN"
            if isP:
                MEMSET("dve", Sf[:], 0.0, [Sf])
                MEMSET("pool", Sb[:], 0.0, [Sb])
            else:
                DMA(Sf[:], I["sdelta"][l, h], writes=[Sf])
                CP("act", Sb[:], Sf[:], [Sf], [Sb])
            for c in range(NC):
                csl = slice(c * L_, (c + 1) * L_)
                pu, po1, po2, pds = PB[0], PB[1], PB[2], PB[3]
                MM(pu[0:L_, 0:128], Qall[0:L_, csl], vb[0:L_, c, :], True, False, [Qall, vb], [pu])
                MM(pu[0:L_, 0:128], wkT[:, csl], Sb[:, :], False, True, [wkT, Sb], [pu])
                MM(po1[0:L_, 0:128], raw[0][:, csl], Sb[:, :], True, True, [raw[0], Sb], [po1])
                u_ = usb[c % 2]
                CP("act", u_[0:L_, :], pu[0:L_, 0:128], [], [u_, pu])
                yield
                MM(po2[0:L_, 0:128], qkT[0:L_, csl], u_[0:L_, :], True, True, [qkT, u_], [po2])
                MM(pds[:, 0:128], ktl[0:L_, c, :], u_[0:L_, :], True, True, [ktl, u_], [pds])
                o_ = ot[c % 2]
                TS_("dve", o_[0:L_, :], po1[0:L_, 0:128], tk[0:L_, c, 11:12], None, ALU.mult, None, [tk], [o_, po1])
                STT("dve", oa[0:L_, c, :], po2[0:L_, 0:128], tk[0:L_, c, 6:7], o_[0:L_, :], ALU.mult, ALU.add,
                    [tk, o_], [oa, po2])
                STT("dve", Sb[:, :], Sf[:, :], tk[:, c, 13:14], pds[:, 0:128], ALU.mult, ALU.add, [Sf, tk], [Sb, pds])
                STT("dve", Sf[:, :], Sf[:, :], tk[:, c, 13:14], pds[:, 0:128], ALU.mult, ALU.add, [Sf, tk], [Sf, pds])
                yield
            DMA(o_nd[h], Sf[:, :], reads=[Sf])
            if DBG and isP and si == 0 and l == 0 and h == 0:
                DMA(O["dbg_tk"], tk[:, :, :].rearrange("p a b -> p (a b)"), reads=[tk])
                for gi_ in range(3):
                    DMA(O["dbg_raw"][gi_], raw[gi_][:, :], reads=[raw[gi_]])
                DMA(O["dbg_Q"], Qall[:, :], reads=[Qall])
                DMA(O["dbg_qkT"], qkT[:, :], reads=[qkT])
                DMA(O["dbg_wkT"], wkT[:, :], reads=[wkT])
                DMA(O["dbg_oa"], oa[:, :, :].rearrange("p a b -> p (a b)"), reads=[oa])
                DMA(O["dbg_E"][:, 0:GW], Ecore[:, :], reads=[Ecore])
            for b in range(NB):
                ACT(obj[0:P, :], oa[0:P, b, :], AF.Square, [oa], [obj, nst], accum=nst[0:P, b:b + 1])
            ACT(nst[0:P, NB:2 * NB], nst[0:P, 0:NB], AF.Ln, [nst, cst], [nst], bias=cst[0:P, 0:1], scale=1.0 / 128)
            ACT(nst[0:P, 2 * NB:3 * NB], nst[0:P, NB:2 * NB], AF.Exp, [nst], [nst], scale=-0.5)
            for b in range(NB):
                STT("dve", oa[0:P, b, :], oa[0:P, b, :], nst[0:P, 2 * NB + b:2 * NB + b + 1], gna[0:P, :], ALU.mult, ALU.mult,
                    [oa, nst, gna], [oa])
                TT("pool", oa[0:P, b, :], oa[0:P, b, :], zs[0:P, b, :], ALU.mult, [oa, zs], [oa])

        def merge(h):
            if DBG and isP and si == 0 and l == 0 and h == 0:
                DMA(O["dbg_ob"], ob[:, :, :].rearrange("p a b -> p (a b)"), reads=[ob])
            for b in range(NB):
                m_ = mg[b % 2]
                TT("pool", oa[0:P, b, :], oa[0:P, b, :], sga[0:P, b, :], ALU.mult, [oa, sga], [oa])
                TT("dve", ob[0:P, b, :], ob[0:P, b, :], sgb[0:P, b, :], ALU.mult, [ob, sgb], [ob])
                TT("pool", m_[0:P, :], oa[0:P, b, :], ob[0:P, b, :], ALU.add, [oa, ob], [m_])
                pb = PB[2 + (b % 2)]
                pv = pbf(2 + (b % 2))
                TR(pv[:, 0:P], m_[0:P, :], identb[0:P, 0:P], [m_, identb], [pb])
                ACT(mTh[:, b * P:(b + 1) * P], pv[:, 0:P], AF.Copy, [], [mTh, pb])
            DMA(mTd[h * 128:(h + 1) * 128, 0:Tq], mTh[:, :], reads=[mTh])

        def interleave(gens, ratios):
            alive = [g for g in gens if g is not None]
            rat = [r for g, r in zip(gens, ratios) if g is not None]
            while alive:
                for gi_ in range(len(alive) - 1, -1, -1):
                    try:
                        for _ in range(rat[gi_]):
                            next(alive[gi_])
                    except StopIteration:
                        alive.pop(gi_)
                        rat.pop(gi_)

        PJ = {}
        NW = {}

        load_weights(0)
        for h in range(H):
            mh = mark()
            PJ["vo"] = [sb([128, 128], F32, "vo%d" % i) for i in range(2)]
            PJ["ko"] = [sb([128, 128], F32, "ko%d" % i) for i in range(2)]
            PJ["pre"] = sb([128, 3 + Tq], F32, "pre")
            PJ["cacc"] = sb([128, Tq], F32, "cacc")
            PJ["sq_"] = sb([128, Tq], BF16, "sq_")
            PJ["cw"] = [sb([128, 4], F32, "cw%d" % i) for i in range(3)]
            PJ["ncs"] = sb([128, 128], F32, "ncs")
            proj(h)
            S.barrier()
            release(mh)
            NW["Estr"] = sb([128, GW], F32, "Estr")
            NW["Ering"] = [NW["Estr"], sb([128, GW], F32, "Estr2")]
            NW["Xb"] = [sb([128, GW], BF16, "Xb0")]
            NW["NWT"] = [dict(IAt=sb([128, GW], BF16, "IAt%d" % g), P1=sb([128, GW], BF16, "P1%d" % g),
                              D=[sb([128, GW], BF16, "Da%d" % g), sb([128, GW], BF16, "Db%d" % g)],
                              Dt=sb([128, GW], BF16, "Dt%d" % g)) for g in range(NG)]
            if h + 1 < H:
                load_weights(h + 1)
            ga = attn_gen(h) if cfg.do_attn else None
            gd = delta_gen(h) if cfg.do_delta else None
            if not cfg.do_attn:
                MEMSET("dve", ob[:], 0.0, [ob])
            if not cfg.do_delta:
                MEMSET("dve", oa[:], 0.0, [oa])
            interleave([ga, gd], [1, 1])
            merge(h)
            S.barrier()
            release(mh)

    for sq in seqs:
        Tq, P = sq["T"], sq["P"]
        NB = Tq // P
        isP = sq["kind"] == "p"
        x_in = I["xp"][sq["i"]] if isP else I["xs"][0]
        y_out = O["yp"][sq["i"]] if isP else O["ys"][0]
        TT_W = min(512, Tq)
        NTT = Tq // TT_W
        BPT = TT_W // P

        m0 = mark()
        gam = sb([128, D], F32, "gam")
        bet = sb([128, D], F32, "bet")
        bcast_row(gam, I["ln_in_g"])
        bcast_row(bet, I["ln_in_b"])
        xt = [sb([128, D], F32, "xt%d" % i) for i in range(2)]
        scr = [sb([128, 8], F32, "scr%d" % i) for i in range(2)]
        junk = sb([128, D], BF16, "junk")
        for b in range(NB):
            x = xt[b % 2]
            DMA(x[0:P, :], x_in[b * P:(b + 1) * P, :], writes=[x])
            layer_norm_block(P, x[0:P, :], x, x[0:P, :], x, gam, bet, scr[b % 2], junk)
            DMA(xsc[b * P:(b + 1) * P, :], x[0:P, :], reads=[x])
        S.barrier()
        release(m0)

        for l in range(L):
            last_layer = (l == L - 1)
            mlay = mark()
            hT = sb([128, 8, Tq], BF16, "hT")
            m0 = mark()
            opsc = sb([128, D], F32, "opsc")
            sh = sb([128, D], F32, "sh")
            bcast_row(sh, modd[l, sq["c"], 0:D])
            bcast_row(opsc, modd[l, sq["c"], D:2 * D])
            TS_("dve", opsc[:], opsc[:], 1.0, None, ALU.add, None, [opsc], [opsc])
            xt = [sb([128, D], F32, "xt%d" % i) for i in range(4)]
            rt = [sb([128, D], F32, "rt%d" % i) for i in range(4)]
            hb = [sb([128, D], BF16, "hb%d" % i) for i in range(4)]
            for b in range(NB):
                x = xt[b % 4]
                DMA(x[0:P, :], xsc[b * P:(b + 1) * P, :], writes=[x])
                build_hT(P, b, x, opsc, sh, rt[b % 4], hb[b % 4], hT)
            S.barrier()
            release(m0)

            m0 = mark()
            mixer(sq, l, hT)
            S.barrier()
            release(m0)

            m0 = mark()
            mT = sb([128, 8, Tq], BF16, "mT")
            DMA(mT[:], mTd[:, 0:Tq].rearrange("(k p) t -> p k t", p=128), writes=[mT])
            wo = sb([128, 8, D], BF16, "wo")
            DMA(wo[:], I["w_o"][l].rearrange("(k p) c -> p k c", p=128), writes=[wo], q="pool")
            gt = sb([128, D], F32, "gt1")
            gam = sb([128, D], F32, "gam1")
            bet = sb([128, D], F32, "bet1")
            opsc = sb([128, D], F32, "opsc2")
            sh = sb([128, D], F32, "sh2")
            bcast_row(gt, modd[l, sq["c"], 2 * D:3 * D])
            bcast_row(gam, I["ln1_g"][l])
            bcast_row(bet, I["ln1_b"][l])
            bcast_row(sh, modd[l, sq["c"], 3 * D:4 * D])
            bcast_row(opsc, modd[l, sq["c"], 4 * D:5 * D])
            TS_("dve", opsc[:], opsc[:], 1.0, None, ALU.add, None, [opsc], [opsc])
            xt = [sb([128, D], F32, "xt%d" % i) for i in range(4)]
            rt = [sb([128, D], F32, "rt%d" % i) for i in range(4)]
            hb = [sb([128, D], BF16, "hb%d" % i) for i in range(4)]
            scr = [sb([128, 8], F32, "scr%d" % i) for i in range(4)]
            junk = sb([128, D], BF16, "junk")
            for b in range(NB):
                x, r, hbb = xt[b % 4], rt[b % 4], hb[b % 4]
                DMA(x[0:P, :], xsc[b * P:(b + 1) * P, :], writes=[x])
                for half in range(2):
                    pbk = PB[(b % 2) * 2 + half]
                    for k in range(8):
                        MM(pbk[0:P, :], mT[:, k, b * P:(b + 1) * P], wo[:, k, half * 512:(half + 1) * 512],
                           k == 0, k == 7, [mT, wo], [pbk])
                    TT("dve", r[0:P, half * 512:(half + 1) * 512], pbk[0:P, :], gt[0:P, half * 512:(half + 1) * 512],
                       ALU.mult, [gt], [r, pbk])
                STT("dve", r[0:P, :], x[0:P, :], ALPHA, r[0:P, :], ALU.mult, ALU.add, [x, r], [r])
                layer_norm_block(P, r[0:P, :], r, x[0:P, :], x, gam, bet, scr[b % 4], junk)
                DMA(xsc[b * P:(b + 1) * P, :], x[0:P, :], reads=[x])
                build_hT(P, b, x, opsc, sh, r, hbb, hT)
            S.barrier()
            release(m0)

            m0 = mark()
            wout = sb([128, 22, D], BF16, "wout")
            for j in range(22):
                DMA(wout[:, j, :], I["w_ff_out"][l][j * 128:(j + 1) * 128, :], writes=[wout], q="pool")
            gt = sb([128, D], F32, "gt2")
            gam = sb([128, D], F32, "gam2")
            bet = sb([128, D], F32, "bet2")
            bcast_row(gt, modd[l, sq["c"], 5 * D:6 * D])
            bcast_row(gam, I["ln2_g"][l])
            bcast_row(bet, I["ln2_b"][l])
            wfi = [sb([128, 8, 256], BF16, "wfi%d" % i) for i in range(3)]
            TF = min(1024, Tq)
            NTF = Tq // TF
            NH = TF // TT_W
            BPF = TF // P
            gT = sb([128, 22, TF], BF16, "gT")
            su = [sb([128, TT_W], BF16, "su%d" % i) for i in range(2)]
            xt = [sb([128, D], F32, "xt%d" % i) for i in range(4)]
            rt = [sb([128, D], F32, "rt%d" % i) for i in range(4)]
            scr = [sb([128, 8], F32, "scr%d" % i) for i in range(4)]
            junk = sb([128, D], BF16, "junk")
            wsrc = I["w_ff_in"][l].rearrange("(k p) c -> p k c", p=128)
            fcnt = 0
            for tt in range(NTF):
                for j in range(22):
                    w = wfi[j % 3]
                    DMA(w[:, :, 0:128], wsrc[:, :, j * 128:(j + 1) * 128], writes=[w], q="pool")
                    DMA(w[:, :, 128:256], wsrc[:, :, DFF + j * 128:DFF + (j + 1) * 128], writes=[w], q="pool")
                    for hf in range(NH):
                        tsl = slice(tt * TF + hf * TT_W, tt * TF + (hf + 1) * TT_W)
                        gsl = slice(hf * TT_W, (hf + 1) * TT_W)
                        pu, pv_ = PB[4 + 2 * (fcnt % 2)], PB[5 + 2 * (fcnt % 2)]
                        for k in range(8):
                            MM(pu[:, 0:TT_W], w[:, k, 0:128], hT[:, k, tsl], k == 0, k == 7, [w, hT], [pu])
                        for k in range(8):
                            MM(pv_[:, 0:TT_W], w[:, k, 128:256], hT[:, k, tsl], k == 0, k == 7, [w, hT], [pv_])
                        s_ = su[fcnt % 2]
                        fcnt += 1
                        ACT(s_[:, :], pu[:, 0:TT_W], AF.Silu, [], [s_, pu])
                        TT("dve", gT[:, j, gsl], pv_[:, 0:TT_W], s_[:, :], ALU.mult, [s_], [gT, pv_])
                for bb in range(BPF):
                    b = tt * BPF + bb
                    x, r = xt[b % 4], rt[b % 4]
                    DMA(x[0:P, :], xsc[b * P:(b + 1) * P, :], writes=[x])
                    for half in range(2):
                        pbk = PB[(b % 2) * 2 + half]
                        for j in range(22):
                            MM(pbk[0:P, :], gT[:, j, bb * P:(bb + 1) * P], wout[:, j, half * 512:(half + 1) * 512],
                               j == 0, j == 21, [gT, wout], [pbk])
                        TT("dve", r[0:P, half * 512:(half + 1) * 512], pbk[0:P, :], gt[0:P, half * 512:(half + 1) * 512],
                           ALU.mult, [gt], [r, pbk])
                    STT("dve", r[0:P, :], x[0:P, :], ALPHA, r[0:P, :], ALU.mult, ALU.add, [x, r], [r])
                    layer_norm_block(P, r[0:P, :], r, x[0:P, :], x, gam, bet, scr[b % 4], junk)
                    if last_layer:
                        DMA(y_out[b * P:(b + 1) * P, :], x[0:P, :], reads=[x])
                    else:
                        DMA(xsc[b * P:(b + 1) * P, :], x[0:P, :], reads=[x])
            S.barrier()
            release(mlay)

    print('arena max words', amax[0], 'ops', S.nops)
    S.emit(ctx)
    return nc, consts


_CACHE = {}


def _get_program(cfg_key, cfg):
    if cfg_key not in _CACHE:
        _CACHE[cfg_key] = build_program(cfg)
    return _CACHE[cfg_key]


def kernel(x_prompt, x_sample, cache_k, cache_v, state_conv, state_delta, c_prompt, c_sample,
           ln_in_g, ln_in_b, rel_bias, w_ada, b_ada, w_in, conv_w, a_log, dt_bias, norm_a,
           lam, subln_g, w_o, ln1_g, ln1_b, w_ff_in, w_ff_out, ln2_g, ln2_b):
    NCORE = 8
    f = lambda a: np.ascontiguousarray(np.asarray(a, dtype=np.float32))
    x_prompt, x_sample, cache_k, cache_v = f(x_prompt), f(x_sample), f(cache_k), f(cache_v)
    state_conv, state_delta, c_prompt, c_sample = f(state_conv), f(state_delta), f(c_prompt), f(c_sample)
    B, T, _ = x_prompt.shape
    BS, TS, _ = x_sample.shape
    Ld = w_in.shape[0]
    PAST = cache_k.shape[2]
    NP = B // NCORE
    cfg = Cfg(depth=Ld, n_prompt=NP, T=T, with_sample=True, TS=TS, past=PAST)
    nc, consts = _get_program(("full", Ld, NP, T, TS, PAST), cfg)
    shared = {}
    for nm, arr in [("ln_in_g", ln_in_g), ("ln_in_b", ln_in_b), ("rel_bias", rel_bias), ("w_ada", w_ada), ("b_ada", b_ada),
                    ("w_in", w_in), ("conv_w", conv_w), ("a_log", a_log), ("dt_bias", dt_bias), ("norm_a", norm_a),
                    ("lam", lam), ("subln_g", subln_g), ("w_o", w_o), ("ln1_g", ln1_g), ("ln1_b", ln1_b),
                    ("w_ff_in", w_ff_in), ("w_ff_out", w_ff_out), ("ln2_g", ln2_g), ("ln2_b", ln2_b)]:
        shared[nm] = f(arr)
    for nm, arr in consts.items():
        shared["c_" + nm] = arr
    in_maps = []
    for c in range(NCORE):
        m = dict(shared)
        m["xp"] = x_prompt[c * NP:(c + 1) * NP]
        m["xs"] = x_sample[c:c + 1]
        m["cvec"] = np.ascontiguousarray(np.concatenate([c_prompt[c * NP:(c + 1) * NP], c_sample[c:c + 1]], axis=0))
        m["ck"] = np.ascontiguousarray(cache_k[:, c].reshape(Ld, PAST, D))
        m["cv"] = np.ascontiguousarray(cache_v[:, c].reshape(Ld, PAST, D))
        m["sconv"] = np.ascontiguousarray(state_conv[:, c])
        m["sdelta"] = np.ascontiguousarray(state_delta[:, c])
        in_maps.append(m)
    res = run_bass_kernel_spmd(nc, in_maps, core_ids=list(range(NCORE)))
    R = res.results
    cat = lambda key, ax: np.concatenate([np.asarray(r[key]) for r in R], axis=ax)
    y_p = cat("yp", 0)
    y_s = cat("ys", 0)
    nk_p = cat("nkp", 1).reshape(Ld, B, T, H, 128)
    nv_p = cat("nvp", 1).reshape(Ld, B, T, H, 128)
    nc_p = cat("ncp", 1)
    nd_p = cat("ndp", 1)
    nk_s = cat("nks", 1).reshape(Ld, BS, TS, H, 128)
    nv_s = cat("nvs", 1).reshape(Ld, BS, TS, H, 128)
    nc_s = cat("ncs", 1)
    nd_s = cat("nds", 1)
    return (y_p.astype(np.float32), y_s.astype(np.float32), nk_p.astype(np.float32), nv_p.astype(np.float32),
            nc_p.astype(np.float32), nd_p.astype(np.float32), nk_s.astype(np.float32), nv_s.astype(np.float32),
            nc_s.astype(np.float32), nd_s.astype(np.float32))
```

```python
import math
from contextlib import ExitStack
import numpy as np
import ml_dtypes
import concourse.bass as bass
import concourse.mybir as mybir
from concourse.bass_utils import run_bass_kernel_spmd

F32 = mybir.dt.float32
BF16 = mybir.dt.bfloat16
AF = mybir.ActivationFunctionType
ALU = mybir.AluOpType
AX = mybir.AxisListType

D = 1024
H = 8
DFF = 2816
DIN = 9232
NBK = 32
ALPHA = 8.0 ** 0.25
LN_EPS = 1e-5
NEG = -30000.0

NDMA = 8


class Res:
    __slots__ = ("name", "lw", "readers")

    def __init__(self, name=""):
        self.name = name
        self.lw = None
        self.readers = []


class Op:
    __slots__ = ("eng", "fn", "deps", "signal", "count", "is_dma", "dma_i", "idx")

    def __init__(self, eng, fn, is_dma):
        self.eng = eng
        self.fn = fn
        self.deps = []
        self.signal = False
        self.count = 0
        self.is_dma = is_dma
        self.dma_i = -1
        self.idx = -1


class TB:
    def __init__(self, t, name=""):
        self.t = t
        self.r = Res(name)

    def __getitem__(self, k):
        return self.t[k]


def _res(x):
    return x.r if isinstance(x, TB) else x


class Sched:
    ENGS = ("pe", "act", "dve", "pool", "sp")

    def __init__(self, nc):
        self.nc = nc
        self.streams = {e: [] for e in self.ENGS}
        self.ndma = {e: 0 for e in self.ENGS}
        self.all_dma = []
        self.pending_dma = []
        self.nops = 0

    def add(self, eng, fn, reads=(), writes=(), dma=False, extra=()):
        op = Op(eng, fn, dma)
        deps = list(extra)
        for r in reads:
            lw = _res(r).lw
            if lw is not None and not (lw.eng == "pe" and eng == "pe" and not lw.is_dma and not dma):
                deps.append(lw)
        for w in writes:
            w = _res(w)
            for rd in w.readers:
                if rd.is_dma or rd.eng != eng or dma:
                    deps.append(rd)
            lw = w.lw
            if lw is not None and (lw.is_dma or lw.eng != eng or dma):
                deps.append(lw)
        seen = set()
        for d in deps:
            if id(d) not in seen:
                seen.add(id(d))
                op.deps.append(d)
                d.signal = True
        for r in reads:
            _res(r).readers.append(op)
        for w in writes:
            w = _res(w)
            w.lw = op
            w.readers = []
        op.idx = len(self.streams[eng])
        self.streams[eng].append(op)
        if dma:
            op.dma_i = self.ndma[eng]
            self.ndma[eng] += 1
            op.signal = True
            self.all_dma.append(op)
            self.pending_dma.append(op)
        self.nops += 1
        return op

    def barrier(self):
        lasts = []
        for e in self.ENGS:
            for op in reversed(self.streams[e]):
                if op.fn is not None and not op.is_dma:
                    lasts.append(op)
                    break
        extra = lasts + self.pending_dma
        self.pending_dma = []
        for e in self.ENGS:
            self.add(e, None, extra=extra)

    def emit(self, ctx):
        nc = self.nc
        esem = {e: ctx.enter_context(nc.semaphore("s_" + e)) for e in self.ENGS}
        dsem = {}
        for e in self.ENGS:
            if self.ndma[e]:
                dsem[e] = [ctx.enter_context(nc.semaphore("d_%s%d" % (e, i))) for i in range(NDMA)]
        for e in self.ENGS:
            c = 0
            for op in self.streams[e]:
                if op.signal and not op.is_dma and op.fn is not None:
                    c += 1
                op.count = c

        def ev(d):
            if d.is_dma:
                return dsem[d.eng][d.dma_i % NDMA], 16 * (d.dma_i // NDMA + 1)
            return esem[d.eng], d.count

        streams = self.streams
        all_dma = self.all_dma

        def run(e, h):
            known = {}
            for op in streams[e]:
                need = {}
                for d in op.deps:
                    if d.fn is None:
                        continue
                    s, v = ev(d)
                    k = id(s)
                    if k not in need or need[k][1] < v:
                        need[k] = (s, v)
                if op.is_dma and op.dma_i >= NDMA:
                    s = dsem[e][op.dma_i % NDMA]
                    v = 16 * (op.dma_i // NDMA)
                    k = id(s)
                    if k not in need or need[k][1] < v:
                        need[k] = (s, v)
                for k, (s, v) in need.items():
                    if known.get(k, 0) < v:
                        h.wait_ge(s, v)
                        known[k] = v
                if op.fn is None:
                    continue
                ins = op.fn(h)
                if op.is_dma:
                    ins.then_inc(dsem[e][op.dma_i % NDMA], 16)
                elif op.signal:
                    ins.then_inc(esem[e], 1)
            if e == "sp":
                last = {}
                for d in all_dma:
                    s, v = ev(d)
                    k = id(s)
                    if k not in last or last[k][1] < v:
                        last[k] = (s, v)
                for k, (s, v) in last.items():
                    if known.get(k, 0) < v:
                        h.wait_ge(s, v)

        with nc.Block() as block:
            @block.sync
            def _(h):
                run("sp", h)

            @block.tensor
            def _(h):
                run("pe", h)

            @block.scalar
            def _(h):
                run("act", h)

            @block.vector
            def _(h):
                run("dve", h)

            @block.gpsimd
            def _(h):
                run("pool", h)


class Cfg:
    def __init__(self, depth=4, n_prompt=2, T=2048, with_sample=True, TS=32, past=2048,
                 do_attn=True, do_delta=True):
        self.depth = depth
        self.n_prompt = n_prompt
        self.T = T
        self.with_sample = with_sample
        self.TS = TS
        self.past = past
        self.do_attn = do_attn
        self.do_delta = do_delta


def t5_bucket_np(rel):
    import jax
    import jax.numpy as jnp
    with jax.default_device(jax.devices("cpu")[0]):
        return _t5_bucket_cpu(jnp, rel)


def _t5_bucket_cpu(jnp, rel):
    rel = jnp.asarray(rel, jnp.int32)
    nb = NBK // 2
    max_exact = nb // 2
    ret = jnp.where(rel > 0, nb, 0)
    n = jnp.abs(rel)
    nf = jnp.maximum(n, 1).astype(jnp.float32)
    large = max_exact + (jnp.log(nf / max_exact) / math.log(128 / max_exact) * (nb - max_exact)).astype(jnp.int32)
    large = jnp.minimum(large, nb - 1)
    return np.asarray(ret + jnp.where(n < max_exact, n, large))


def make_consts():
    c = {}
    c["identf"] = np.eye(128, dtype=np.float32)
    p = np.arange(128)[:, None]
    f = np.arange(128)[None, :]
    c["utri"] = (p <= f).astype(np.float32)
    c["onesf"] = np.ones((128, 128), np.float32)
    c["mincl"] = np.where(p >= f, 0.0, NEG).astype(np.float32)
    c["sstrict"] = (p > f).astype(np.float32)
    c["bmask"] = np.stack([((p // s2) == (f // s2)).astype(np.float32) for s2 in (2, 4, 8, 16, 32, 64)])
    kl = np.arange(128)[:, None]
    ql = np.arange(128)[None, :]
    bd = t5_bucket_np(kl - ql)
    bp = t5_bucket_np(kl - ql - 128)
    oh = np.zeros((2, NBK, 128, 128), np.float32)
    for b in range(NBK):
        oh[0, b] = (bd == b)
        oh[1, b] = (bp == b)
    c["bk_oh"] = oh
    vis = ((kl // 64) <= (ql // 64))
    c["amask"] = np.where(vis, 0.0, 8.0 * NEG).astype(np.float32)
    used = [sorted(set(np.unique(bd).tolist())), sorted(set(np.unique(bp).tolist()))]
    return c, used


def build_program(cfg):
    nc = bass.Bass("TRN2", target_bir_lowering=False)
    consts, used_buckets = make_consts()
    NP, T, L = cfg.n_prompt, cfg.T, cfg.depth
    NS = NP + (1 if cfg.with_sample else 0)
    TS, PAST = cfg.TS, cfg.past

    def din(name, shape, dt=F32):
        return nc.dram_tensor(name, list(shape), dt, kind="ExternalInput").ap()

    def dout(name, shape, dt=F32):
        return nc.dram_tensor(name, list(shape), dt, kind="ExternalOutput").ap()

    def dscr(name, shape, dt=F32):
        return nc.dram_tensor(name, list(shape), dt, kind="Internal").ap()

    I = {}
    I["xp"] = din("xp", [NP, T, D])
    I["cvec"] = din("cvec", [NS, D])
    if cfg.with_sample:
        I["xs"] = din("xs", [1, TS, D])
        I["ck"] = din("ck", [L, PAST, D])
        I["cv"] = din("cv", [L, PAST, D])
        I["sconv"] = din("sconv", [L, 3, 3072])
        I["sdelta"] = din("sdelta", [L, H, 128, 128])
    for nm, shp in [("ln_in_g", [D]), ("ln_in_b", [D]), ("rel_bias", [NBK, H]), ("w_ada", [L, D, 6 * D]),
                    ("b_ada", [L, 6 * D]), ("w_in", [L, D, DIN]), ("conv_w", [L, 4, 3072]), ("a_log", [L, H]),
                    ("dt_bias", [L, H]), ("norm_a", [L, 128]), ("lam", [L, 4, 64]), ("subln_g", [L, 128]),
                    ("w_o", [L, D, D]), ("ln1_g", [L, D]), ("ln1_b", [L, D]), ("w_ff_in", [L, D, 2 * DFF]),
                    ("w_ff_out", [L, DFF, D]), ("ln2_g", [L, D]), ("ln2_b", [L, D])]:
        I[nm] = din(nm, shp)
    for nm, arr in consts.items():
        I["c_" + nm] = din("c_" + nm, arr.shape)

    O = {}
    O["yp"] = dout("yp", [NP, T, D])
    O["nkp"] = dout("nkp", [L, NP, T, D])
    O["nvp"] = dout("nvp", [L, NP, T, D])
    O["ncp"] = dout("ncp", [L, NP, 3, 3072])
    O["ndp"] = dout("ndp", [L, NP, H, 128, 128])
    if cfg.with_sample:
        O["ys"] = dout("ys", [1, TS, D])
        O["nks"] = dout("nks", [L, 1, TS, D])
        O["nvs"] = dout("nvs", [L, 1, TS, D])
        O["ncs"] = dout("ncs", [L, 1, 3, 3072])
        O["nds"] = dout("nds", [L, 1, H, 128, 128])

    DBG = getattr(cfg, "dbg", False)
    if DBG:
        NCd = T // 128
        O["dbg_tk"] = dout("dbg_tk", [128, NCd * 16])
        O["dbg_raw"] = dout("dbg_raw", [3, 128, T], BF16)
        O["dbg_Q"] = dout("dbg_Q", [128, T], BF16)
        O["dbg_qkT"] = dout("dbg_qkT", [128, T], BF16)
        O["dbg_wkT"] = dout("dbg_wkT", [128, T], BF16)
        O["dbg_oa"] = dout("dbg_oa", [128, NCd * 128])
        O["dbg_ob"] = dout("dbg_ob", [128, NCd * 128])
        O["dbg_E"] = dout("dbg_E", [128, 512])
        O["dbg_X0"] = dout("dbg_X0", [128, 512], BF16)
    TMAX = max(T, TS)
    xsc = dscr("xsc", [TMAX, D])
    modd = dscr("modd", [L, NS, 6 * D])
    mTd = dscr("mTd", [D, TMAX], BF16)

    ctx = ExitStack()
    S = Sched(nc)

    AW = 53000
    arena_t = ctx.enter_context(nc.sbuf_tensor("arena", [128, AW], F32))
    aoff = [0]
    amax = [0]

    def sb(shape, dt, name):
        n = 1
        for v in shape[1:]:
            n *= v
        words = n if dt == F32 else (n + 1) // 2
        o = aoff[0]
        assert o + words <= AW, "arena overflow %s %d" % (name, o + words)
        aoff[0] = o + words
        amax[0] = max(amax[0], aoff[0])
        ap = arena_t[:, o:o + words]
        if dt != F32:
            ap = ap.bitcast(dt)[:, 0:n]
        if len(shape) == 3:
            ap = ap.rearrange("p (a b) -> p a b", b=shape[2])
        elif len(shape) == 4:
            ap = ap.rearrange("p (a b c) -> p a b c", b=shape[2], c=shape[3])
        return TB(ap, name)

    def mark():
        return aoff[0]

    def release(m):
        aoff[0] = m

    PB = [TB(ctx.enter_context(nc.psum_tensor("pb%d" % i, [128, 512], F32)), "pb%d" % i) for i in range(8)]

    def pbf(i):
        return PB[i].t[:].bitcast(BF16)

    def DMA(out, in_, reads=(), writes=(), q="sp"):
        return S.add(q, lambda h: h.dma_start(out=out, in_=in_), reads, writes, dma=True)

    def DMAS(out, in_, reads=(), writes=(), q="sp"):
        return S.add(q, lambda h: h.dma_start(out=out, in_=in_, allow_slow_non_contiguous=True), reads, writes, dma=True)

    def MM(out, lhsT, rhs, start, stop, reads, writes):
        return S.add("pe", lambda h: h.matmul(out, lhsT=lhsT, rhs=rhs, start=start, stop=stop), reads, writes)

    def TR(out, in_, ident, reads, writes):
        return S.add("pe", lambda h: h.transpose(out, in_, ident), reads, writes)

    def ACT(out, in_, func, reads, writes, bias=None, scale=None, accum=None):
        kw = {}
        if bias is not None:
            kw["bias"] = bias
        if scale is not None:
            kw["scale"] = scale
        if accum is not None:
            kw["accum_out"] = accum
        return S.add("act", lambda h: h.activation(out=out, in_=in_, func=func, **kw), reads, writes)

    def TS_(eng, out, in0, s1, s2, op0, op1, reads, writes):
        if s2 is None:
            return S.add(eng, lambda h: h.tensor_scalar(out=out, in0=in0, scalar1=s1, scalar2=None, op0=op0), reads, writes)
        return S.add(eng, lambda h: h.tensor_scalar(out=out, in0=in0, scalar1=s1, scalar2=s2, op0=op0, op1=op1), reads, writes)

    def TT(eng, out, in0, in1, op, reads, writes):
        return S.add(eng, lambda h: h.tensor_tensor(out=out, in0=in0, in1=in1, op=op), reads, writes)

    def STT(eng, out, in0, scalar, in1, op0, op1, reads, writes):
        return S.add(eng, lambda h: h.scalar_tensor_tensor(out=out, in0=in0, scalar=scalar, in1=in1, op0=op0, op1=op1), reads, writes)

    def CP(eng, out, in_, reads, writes):
        if eng == "act":
            return S.add("act", lambda h: h.activation(out=out, in_=in_, func=AF.Copy), reads, writes)
        return S.add(eng, lambda h: h.tensor_copy(out=out, in_=in_), reads, writes)

    def MEMSET(eng, ap, val, writes):
        return S.add(eng, lambda h: h.memset(ap, val), (), writes)

    def RECIP(out, in_, reads, writes):
        return S.add("dve", lambda h: h.reciprocal(out=out, in_=in_), reads, writes)

    def bcast_row(dst, row_ap):
        DMA(dst[:], row_ap.partition_broadcast(128), writes=[dst])

    identf = sb([128, 128], F32, "identf")
    identb = sb([128, 128], BF16, "identb")
    utri = sb([128, 128], F32, "utri")
    onesf = sb([128, 128], F32, "onesf")
    onesb = sb([128, 128], BF16, "onesb")
    mincl = sb([128, 512], BF16, "mincl")
    sstrict = sb([128, 512], F32, "sstrict")
    identrep = sb([128, 512], BF16, "identrep")
    cst = sb([128, 8], F32, "cst")
    cbias = sb([128, H], F32, "cbias")
    biasT = [sb([128, H, 128], BF16, "biasT%d" % t) for t in range(2)]
    bmask = sb([128, 6, 128], F32, "bmask")
    for i_ in range(6):
        DMA(bmask[:, i_, :], I["c_bmask"][i_], writes=[bmask])
    DMA(identf[:], I["c_identf"], writes=[identf])
    DMA(utri[:], I["c_utri"], writes=[utri])
    DMA(onesf[:], I["c_onesf"], writes=[onesf])
    CP("dve", identb[:], identf[:], [identf], [identb])
    CP("dve", onesb[:], onesf[:], [onesf], [onesb])
    m0 = mark()
    tmpc = sb([128, 128], F32, "tmpc")
    DMA(tmpc[:], I["c_mincl"], writes=[tmpc])
    for r4 in range(4):
        CP("dve", mincl[:, r4 * 128:(r4 + 1) * 128], tmpc[:], [tmpc], [mincl])
        CP("dve", identrep[:, r4 * 128:(r4 + 1) * 128], identf[:], [identf], [identrep])
        DMA(sstrict[:, r4 * 128:(r4 + 1) * 128], I["c_sstrict"], writes=[sstrict])
    MEMSET("dve", cst[:, 0:1], LN_EPS, [cst])
    MEMSET("dve", cst[:, 1:2], 1e-6, [cst])
    MEMSET("dve", cst[:, 2:3], 1.0, [cst])
    MEMSET("dve", cst[:, 3:4], 0.0, [cst])

    rbb = sb([128, NBK * H], F32, "rbb")
    DMA(rbb[:], I["rel_bias"].rearrange("b h -> (b h)").partition_broadcast(128), writes=[rbb])
    CP("dve", cbias[:], rbb[:, 15 * H:16 * H], [rbb], [cbias])
    rbd = sb([128, NBK * H], F32, "rbd")
    for b in range(NBK):
        TT("dve", rbd[:, b * H:(b + 1) * H], rbb[:, b * H:(b + 1) * H], cbias[:], ALU.subtract, [rbb, cbias], [rbd])
    TS_("dve", rbd[:], rbd[:], 8.0, None, ALU.mult, None, [rbd], [rbd])
    acc = sb([128, H, 128], F32, "bacc")
    ohs = [sb([128, 128], F32, "ohs%d" % i) for i in range(2)]
    cnt = 0
    for t in range(2):
        if t == 0:
            o_ = ohs[cnt % 2]
            cnt += 1
            DMA(o_[:], I["c_amask"], writes=[o_])
            for hh in range(H):
                CP("dve", acc[:, hh, :], o_[:], [o_], [acc])
        else:
            MEMSET("dve", acc[:], 0.0, [acc])
        for b in used_buckets[t]:
            o_ = ohs[cnt % 2]
            cnt += 1
            DMA(o_[:], I["c_bk_oh"][t, b], writes=[o_])
            for hh in range(H):
                STT("dve", acc[:, hh, :], o_[:], rbd[:, b * H + hh:b * H + hh + 1], acc[:, hh, :],
                    ALU.mult, ALU.add, [o_, rbd, acc], [acc])
        CP("dve", biasT[t][:], acc[:], [acc], [biasT[t]])
    S.barrier()
    release(m0)

    m0 = mark()
    cT = sb([128, 8, NS], F32, "cT")
    cTb = sb([128, 8, NS], BF16, "cTb")
    for s_i in range(NS):
        DMAS(cT[:, :, s_i], I["cvec"][s_i].rearrange("(k p) -> p k", p=128), writes=[cT])
    ACT(cTb[:], cT[:], AF.Silu, [cT], [cTb])
    wad = [sb([128, 8, 512], BF16, "wad%d" % i) for i in range(2)]
    bad = sb([128, 6 * D], F32, "bad")
    mrow = sb([128, 6 * D], F32, "mrow")
    for l in range(L):
        for s in range(NS):
            DMA(bad[s:s + 1, :], I["b_ada"][l:l + 1, :], writes=[bad])
        for ct in range(12):
            w = wad[ct % 2]
            DMA(w[:], I["w_ada"][l].rearrange("(k p) c -> p k c", p=128)[:, :, ct * 512:(ct + 1) * 512],
                writes=[w], q="pool")
            pbk = PB[ct % 2]
            for k in range(8):
                MM(pbk[0:NS, :], cTb[:, k, :], w[:, k, :], k == 0, k == 7, [cTb, w], [pbk])
            TT("dve", mrow[0:NS, ct * 512:(ct + 1) * 512], pbk[0:NS, :], bad[0:NS, ct * 512:(ct + 1) * 512], ALU.add,
               [bad], [mrow, pbk])
        DMA(modd[l], mrow[0:NS, :], reads=[mrow])
    S.barrier()
    release(m0)

    seqs = [dict(kind="p", i=i, T=T, P=128, c=i) for i in range(NP)]
    if cfg.with_sample:
        seqs.append(dict(kind="s", i=0, T=TS, P=TS, c=NP))

    def layer_norm_block(P, xin, xin_tb, xout, xout_tb, gam, bet, st, junk):
        S.add("dve", lambda h: h.reduce_sum(out=st[0:P, 0:1], in_=xin, axis=AX.X), [xin_tb], [st])
        TS_("dve", st[0:P, 1:2], st[0:P, 0:1], -1.0 / D, None, ALU.mult, None, [st], [st])
        ACT(junk[0:P, :], xin, AF.Square, [xin_tb, st], [junk, st], bias=st[0:P, 1:2], accum=st[0:P, 2:3])
        ACT(st[0:P, 3:4], st[0:P, 2:3], AF.Ln, [st, cst], [st], bias=cst[0:P, 0:1], scale=1.0 / D)
        ACT(st[0:P, 4:5], st[0:P, 3:4], AF.Exp, [st], [st], scale=-0.5)
        TS_("dve", xout, xin, st[0:P, 1:2], st[0:P, 4:5], ALU.add, ALU.mult, [xin_tb, st], [xout_tb])
        TT("pool", xout, xout, gam[0:P, :], ALU.mult, [xout_tb, gam], [xout_tb])
        TT("dve", xout, xout, bet[0:P, :], ALU.add, [xout_tb, bet], [xout_tb])

    def build_hT(P, b, x, opsc, sh, r, hbb, hT):
        TT("dve", r[0:P, :], x[0:P, :], opsc[0:P, :], ALU.mult, [x, opsc], [r])
        TT("pool", hbb[0:P, :], r[0:P, :], sh[0:P, :], ALU.add, [r, sh], [hbb])
        pb = PB[4 + (b % 2)]
        pv = pbf(4 + (b % 2)).rearrange("p (k t) -> p k t", k=8)
        for k in range(8):
            TR(pv[:, k, 0:P], hbb[0:P, k * 128:(k + 1) * 128], identb[0:P, 0:P], [hbb, identb], [pb])
        ACT(hT[:, :, b * P:(b + 1) * P], pv[:, :, 0:P], AF.Copy, [], [hT, pb])

    def mixer(sq, l, hT):
        Tq, P = sq["T"], sq["P"]
        NB = Tq // P
        isP = sq["kind"] == "p"
        si = sq["i"]
        TT_W = min(512, Tq)
        NTT = Tq // TT_W
        BPT = TT_W // P
        Lc = P
        NC = NB
        GW = min(4, NC) * Lc
        NG = (NC * Lc) // GW
        CPG = GW // Lc
        nlev = int(round(math.log2(Lc))) - 1
        lam_init = 0.8 - 0.6 * math.exp(-0.3 * l)
        KT = (PAST + TS) if not isP else Tq
        wsrc = I["w_in"][l].rearrange("(k p) c -> p k c", p=128)
        o_nk = O["nkp"][l, si] if isP else O["nks"][l, 0]
        o_nv = O["nvp"][l, si] if isP else O["nvs"][l, 0]
        o_nc = O["ncp"][l, si] if isP else O["ncs"][l, 0]
        o_nd = O["ndp"][l, si] if isP else O["nds"][l, 0]

        lamt = sb([128, 4, 64], F32, "lamt")
        DMA(lamt[:], I["lam"][l].partition_broadcast(128), writes=[lamt])
        lsc = sb([128, 8], F32, "lsc")
        ljunk = sb([128, 64], F32, "ljunk")
        TT("dve", ljunk[:], lamt[:, 0, :], lamt[:, 1, :], ALU.mult, [lamt], [ljunk])
        S.add("dve", lambda h: h.reduce_sum(out=lsc[:, 0:1], in_=ljunk[:], axis=AX.X), [ljunk], [lsc])
        TT("dve", ljunk[:], lamt[:, 2, :], lamt[:, 3, :], ALU.mult, [lamt, lsc], [ljunk])
        S.add("dve", lambda h: h.reduce_sum(out=lsc[:, 1:2], in_=ljunk[:], axis=AX.X), [ljunk], [lsc])
        ACT(lsc[:, 2:4], lsc[:, 0:2], AF.Exp, [lsc], [lsc])
        TT("dve", lsc[:, 4:5], lsc[:, 2:3], lsc[:, 3:4], ALU.subtract, [lsc], [lsc])
        TS_("dve", lsc[:, 5:6], lsc[:, 4:5], lam_init, -1.0, ALU.add, ALU.mult, [lsc], [lsc])
        neglam = lsc[:, 5:6]
        gsub = sb([128, 128], F32, "gsub")
        bcast_row(gsub, I["subln_g"][l])
        TS_("dve", gsub[:], gsub[:], 1.0 - lam_init, None, ALU.mult, None, [gsub], [gsub])
        gna = sb([128, 128], F32, "gna")
        bcast_row(gna, I["norm_a"][l])
        nea = sb([128, H], F32, "nea")
        dtb = sb([128, H], F32, "dtb")
        bcast_row(nea, I["a_log"][l])
        bcast_row(dtb, I["dt_bias"][l])
        ACT(nea[:], nea[:], AF.Exp, [nea], [nea])
        TS_("dve", nea[:], nea[:], -1.0, None, ALU.mult, None, [nea], [nea])

        beta_all = sb([128, NB, H], F32, "beta_all")
        g_all = sb([128, NB, H], F32, "g_all")
        bgw = sb([128, 8, 16], BF16, "bgw")
        DMA(bgw[:], wsrc[:, :, 4096:4112], writes=[bgw], q="pool")
        pbg = PB[0]
        pbgv = pbg.t[:, 0:NB * 16].rearrange("p (b c) -> p b c", c=16)
        for b in range(NB):
            for k in range(8):
                MM(pbg[0:P, b * 16:(b + 1) * 16], hT[:, k, b * P:(b + 1) * P], bgw[:, k, :], k == 0, k == 7, [hT, bgw], [pbg])
        ACT(beta_all[0:P, :, :], pbgv[0:P, :, 0:8], AF.Sigmoid, [], [beta_all, pbg])
        for b in range(NB):
            TT("dve", g_all[0:P, b, :], pbgv[0:P, b, 8:16], dtb[0:P, :], ALU.add, [dtb], [g_all, pbg])
        ACT(g_all[0:P, :, :], g_all[0:P, :, :], AF.Exp, [g_all], [g_all])
        ACT(g_all[0:P, :, :], g_all[0:P, :, :], AF.Ln, [g_all, cst], [g_all], bias=cst[0:P, 2:3])
        for b in range(NB):
            TT("dve", g_all[0:P, b, :], g_all[0:P, b, :], nea[0:P, :], ALU.mult, [g_all, nea], [g_all])

        NPB = PAST // 128
        if not isP:
            ckb = sb([128, NPB, D], BF16, "ckb")
            cvb = sb([128, NPB, D], BF16, "cvb")
            DMA(ckb[:], I["ck"][l].rearrange("(b p) c -> p b c", p=128), writes=[ckb], q="pool")
            DMA(cvb[:], I["cv"][l].rearrange("(b p) c -> p b c", p=128), writes=[cvb], q="pool")

        wq = [sb([128, 8, 128], BF16, "wq%d" % i) for i in range(5)]
        wtm = sb([128, 8, 512], BF16, "wtm")
        qT = sb([128, Tq], BF16, "qT")
        kT = [sb([128, KT], BF16, "kT%d" % m) for m in range(2)]
        MEMSET("pool", kT[0][64:128, :], 0.0, [kT[0]])
        MEMSET("pool", kT[1][0:64, :], 0.0, [kT[1]])
        Vh = sb([128, NB, 130], BF16, "Vh")
        MEMSET("pool", Vh[:, :, 128:130], 1.0, [Vh])
        zs = sb([128, NB, 128], BF16, "zs")
        sga = sb([128, NB, 128], BF16, "sga")
        sgb = sb([128, NB, 128], BF16, "sgb")
        ob = sb([128, NB, 128], F32, "ob")
        oa = sb([128, NB, 128], F32, "oa")
        NKG_ = ((NB if isP else (PAST // 128 + 1)) + 3) // 4
        PT = [[sb([128, NKG_, 4 * P], BF16, "PT%d%d" % (m, i)) for i in range(2)] for m in range(2)]
        rz = sb([128, 16], F32, "rz")
        t1 = [sb([128, 128], F32, "t1%d" % i) for i in range(2)]
        obj = sb([128, 128], BF16, "obj")
        sst = sb([128, 8], F32, "sst")
        nst = sb([128, 3 * NB], F32, "nst")
        nst2 = sb([128, 3 * NB], F32, "nst2")
        mgall = sb([128, NB, 128], BF16, "mgall")
        mTh = sb([128, Tq], BF16, "mTh")
        raw = [sb([128, Tq], BF16, "raw%d" % i) for i in range(3)]
        tk = sb([128, NC, 16], F32, "tk")
        Rk = sb([128, NC, 128], BF16, "Rk")
        ktl = sb([128, NC, 128], BF16, "ktl")
        vb = sb([128, NC, 128], BF16, "vb")
        Ecore = sb([128, GW], F32, "Ecore")
        qkb = sb([128, GW], BF16, "qkb")
        Qall = sb([128, NC * Lc], BF16, "Qall")
        qkT = sb([128, NC * Lc], BF16, "qkT")
        wkT = sb([128, NC * Lc], BF16, "wkT")
        Sf = sb([128, 128], F32, "Sf")
        Sb = sb([128, 128], BF16, "Sb")
        usb = [sb([128, 128], BF16, "usb%d" % i) for i in range(2)]
        ot = [sb([128, 128], F32, "ot%d" % i) for i in range(2)]

        def load_weights(h):
            cols = head_cols(h)
            for i_, nm in enumerate(["qa", "ka", "va", "qb", "kb"]):
                DMA(wq[i_][:], wsrc[:, :, cols[nm]:cols[nm] + 128], writes=[wq[i_]], q="pool")
            for i_, nm in enumerate(["z", "vb", "ga", "gb"]):
                DMA(wtm[:, :, i_ * 128:(i_ + 1) * 128], wsrc[:, :, cols[nm]:cols[nm] + 128], writes=[wtm], q="pool")

        def head_cols(h):
            return dict(qa=h * 128, ka=1024 + h * 128, va=2048 + h * 128, z=3072 + h * 128,
                        qb=4112 + h * 128, kb=5136 + h * 128, vb=6160 + h * 128,
                        ga=7184 + h * 128, gb=8208 + h * 128)

        def proj(h):
            cols = head_cols(h)
            vo, ko, pre, cacc, sq_, ncs = PJ["vo"], PJ["ko"], PJ["pre"], PJ["cacc"], PJ["sq_"], PJ["ncs"]
            ssq_ops = []
            for b in range(NB):
                bsl = slice(b * P, (b + 1) * P)
                pbk = PB[b % 2]
                for k in range(8):
                    MM(pbk[0:P, :], hT[:, k, bsl], wtm[:, k, :], k == 0, k == 7, [hT, wtm], [pbk])
                ACT(zs[0:P, b, :], pbk[0:P, 0:128], AF.Silu, [], [zs, pbk])
                v_ = vo[b % 2]
                CP("dve", v_[0:P, :], pbk[0:P, 128:256], [], [v_, pbk])
                ACT(sga[0:P, b, :], pbk[0:P, 256:384], AF.Sigmoid, [], [sga, pbk])
                ACT(sgb[0:P, b, :], pbk[0:P, 384:512], AF.Sigmoid, [], [sgb, pbk])
                DMA(o_nv[bsl, h * 128:(h + 1) * 128], v_[0:P, :], reads=[v_])
                CP("pool", Vh[0:P, b, 0:128], v_[0:P, :], [v_], [Vh])
                pk = PB[2 + (b % 2)]
                for k in range(8):
                    MM(pk[0:P, 0:128], hT[:, k, bsl], wq[4][:, k, :], k == 0, k == 7, [hT, wq[4]], [pk])
                k_ = ko[b % 2]
                CP("dve", k_[0:P, :], pk[0:P, 0:128], [], [k_, pk])
                DMA(o_nk[bsl, h * 128:(h + 1) * 128], k_[0:P, :], reads=[k_])

            koff = 0 if isP else PAST
            for tt in range(NTT):
                tsl = slice(tt * TT_W, (tt + 1) * TT_W)
                pq = PB[4 + 2 * (tt % 2)]
                for k in range(8):
                    MM(pq[:, 0:TT_W], wq[3][:, k, :], hT[:, k, tsl], k == 0, k == 7, [wq[3], hT], [pq])
                ACT(qT[:, tsl], pq[:, 0:TT_W], AF.Copy, [], [qT, pq])
                pk = PB[5 + 2 * (tt % 2)]
                for k in range(8):
                    MM(pk[:, 0:TT_W], wq[4][:, k, :], hT[:, k, tsl], k == 0, k == 7, [wq[4], hT], [pk])
                ksl = slice(koff + tt * TT_W, koff + (tt + 1) * TT_W)
                ACT(kT[0][0:64, ksl], pk[0:64, 0:TT_W], AF.Copy, [], [kT[0], pk])
                CP("dve", kT[1][64:128, ksl], pk[64:128, 0:TT_W], [], [kT[1], pk])
            if not isP:
                for kb in range(NPB):
                    pb = PB[2 + (kb % 2)]
                    pv = pbf(2 + (kb % 2))
                    TR(pv[:, 0:128], ckb[:, kb, h * 128:(h + 1) * 128], identb[:, :], [ckb, identb], [pb])
                    ACT(kT[0][0:64, kb * 128:(kb + 1) * 128], pv[0:64, 0:128], AF.Copy, [], [kT[0], pb])
                    CP("dve", kT[1][64:128, kb * 128:(kb + 1) * 128], pv[64:128, 0:128], [], [kT[1], pb])
            if cfg.do_delta:
                for gi, nm in enumerate(["qa", "ka", "va"]):
                    c0 = cols[nm]
                    cw = PJ["cw"][gi]
                    DMAS(cw[:], I["conv_w"][l][:, c0:c0 + 128].rearrange("i c -> c i"), writes=[cw])
                    if isP:
                        MEMSET("pool", pre[:, 0:3], 0.0, [pre])
                    else:
                        DMAS(pre[:, 0:3], I["sconv"][l][:, c0:c0 + 128].rearrange("t c -> c t"), writes=[pre])
                    for tt in range(NTT):
                        tsl = slice(tt * TT_W, (tt + 1) * TT_W)
                        pq = PB[tt % 2]
                        for k in range(8):
                            MM(pq[:, 0:TT_W], wq[gi][:, k, :], hT[:, k, tsl], k == 0, k == 7, [wq[gi], hT], [pq])
                        ACT(pre[:, 3 + tt * TT_W:3 + (tt + 1) * TT_W], pq[:, 0:TT_W], AF.Copy, [], [pre, pq])
                    pn = PB[2]
                    for k in range(8):
                        MM(pn[0:3, 0:128], hT[:, k, Tq - 3:Tq], wq[gi][:, k, :], k == 0, k == 7, [hT, wq[gi]], [pn])
                    CP("dve", ncs[0:3, :], pn[0:3, 0:128], [], [ncs, pn])
                    DMA(o_nc[:, c0:c0 + 128], ncs[0:3, :], reads=[ncs])
                    TS_("dve", cacc[:, :], pre[:, 3:3 + Tq], cw[:, 3:4], None, ALU.mult, None, [pre, cw], [cacc])
                    for i_ in range(3):
                        STT("dve", cacc[:, :], pre[:, i_:i_ + Tq], cw[:, i_:i_ + 1], cacc[:, :],
                            ALU.mult, ALU.add, [pre, cw, cacc], [cacc])
                    ACT(raw[gi][:, :], cacc[:, :], AF.Silu, [cacc], [raw[gi]])
                for gi in range(2):
                    ACT(sq_[:, :], raw[gi][:, :], AF.Square, [raw[gi]], [sq_])
                    for c in range(NC):
                        MM(PB[3][0:Lc, 16 + gi * NC + c:16 + gi * NC + c + 1], sq_[:, c * Lc:(c + 1) * Lc], onesb[:, 0:1],
                           True, True, [sq_, onesb], [PB[3]])
                ACT(tk[0:Lc, :, 0], PB[3][0:Lc, 16:16 + NC], AF.Ln, [cst], [tk, PB[3]], bias=cst[0:Lc, 1:2])
                ACT(tk[0:Lc, :, 1], PB[3][0:Lc, 16 + NC:16 + 2 * NC], AF.Ln, [cst], [tk, PB[3]], bias=cst[0:Lc, 1:2])

        gcnt = [0]

        def attn_gen(h):
            if isP:
                kblocks = [dict(kp=128, col=kb * 128, v=Vh[:, kb, 0:129], vtb=Vh) for kb in range(NB)]
            else:
                kblocks = [dict(kp=128, col=kb * 128, v=cvb[:, kb, h * 128:(h + 1) * 128], vtb=cvb) for kb in range(NPB)]
                kblocks.append(dict(kp=TS, col=PAST, v=Vh[0:TS, 0, 0:128], vtb=Vh))
            NKB = len(kblocks)
            QP = P

            def qk_phase(qb):
                par = qb % 2
                nvis = (qb + 1) if isP else NKB
                nkg = (nvis + 3) // 4
                for m in range(2):
                    for kg in range(nkg):
                        sp_ = PB[4 + (gcnt[0] % 2)]
                        gcnt[0] += 1
                        kbs = list(range(kg * 4, min(nvis, kg * 4 + 4)))
                        KPg = kblocks[kbs[0]]["kp"]
                        for kb in kbs:
                            kd = kblocks[kb]
                            KP = kd["kp"]
                            assert KP == KPg
                            cl = kb - kg * 4
                            biasl = []
                            if isP:
                                if kb == qb:
                                    biasl.append((0, 128, 128))
                                if kb == qb - 1:
                                    biasl.append((1, 128, 128))
                            else:
                                if kb == NPB - 1:
                                    biasl.append((1, 128, TS))
                                if kb == NPB:
                                    biasl.append((0, TS, TS))
                            MM(sp_[0:KP, cl * QP:(cl + 1) * QP], kT[m][:, kd["col"]:kd["col"] + KP], qT[:, qb * QP:(qb + 1) * QP],
                               True, len(biasl) == 0, [kT[m], qT], [sp_])
                            for bi, (typ, kp_, qp_) in enumerate(biasl):
                                MM(sp_[0:kp_, cl * QP:cl * QP + qp_], identb[0:kp_, 0:kp_], biasT[typ][0:kp_, h, 0:qp_],
                                   False, bi == len(biasl) - 1, [identb, biasT[typ]], [sp_])
                        ACT(PT[m][par][0:KPg, kg, 0:len(kbs) * QP], sp_[0:KPg, 0:len(kbs) * QP], AF.Exp, [cbias], [PT[m][par], sp_],
                            bias=cbias[0:KPg, h:h + 1], scale=0.125)
                    yield

            def pv_phase(qb):
                par = qb % 2
                VW = 129 if isP else 128
                OBK = PB[6 + (qb % 2)] if isP else PB[6]
                nvis = (qb + 1) if isP else NKB
                for m in range(2):
                    for kb in range(nvis):
                        kd = kblocks[kb]
                        KP = kd["kp"]
                        MM(OBK[0:QP, m * 256:m * 256 + VW], PT[m][par][0:KP, kb // 4, (kb % 4) * QP:(kb % 4 + 1) * QP], kd["v"],
                           kb == 0, kb == nvis - 1, [PT[m][par], kd["vtb"]], [OBK])
                    if not isP:
                        for kb in range(nvis):
                            kd = kblocks[kb]
                            KP = kd["kp"]
                            MM(PB[7][0:QP, m:m + 1], PT[m][par][0:KP, kb // 4, (kb % 4) * QP:(kb % 4 + 1) * QP], onesb[0:KP, 0:1],
                               kb == 0, kb == nvis - 1, [PT[m][par], onesb], [PB[7]])
                    if m == 0:
                        yield
                if isP:
                    RECIP(rz[0:QP, 0:1], OBK[0:QP, 128:129], [], [rz, OBK])
                    RECIP(rz[0:QP, 1:2], OBK[0:QP, 384:385], [], [rz, OBK])
                else:
                    RECIP(rz[0:QP, 0:2], PB[7][0:QP, 0:2], [], [rz, PB[7]])
                TS_("dve", rz[0:QP, 2:4], rz[0:QP, 0:2], neglam[0:QP, :], None, ALU.mult, None, [rz, lsc], [rz])
                TS_("dve", ob[0:QP, qb, :], OBK[0:QP, 0:128], rz[0:QP, 0:1], None, ALU.mult, None, [rz], [ob, OBK])
                STT("dve", ob[0:QP, qb, :], OBK[0:QP, 256:384], rz[0:QP, 3:4], ob[0:QP, qb, :], ALU.mult, ALU.add, [rz, ob], [ob, OBK])
                ACT(obj[0:QP, :], ob[0:QP, qb, :], AF.Square, [ob], [obj, nst2], accum=nst2[0:QP, qb:qb + 1])

            yield from qk_phase(0)
            for qb in range(NB):
                if qb + 1 < NB:
                    yield from qk_phase(qb + 1)
                yield from pv_phase(qb)
                yield
            ACT(nst2[0:QP, NB:2 * NB], nst2[0:QP, 0:NB], AF.Ln, [nst2, cst], [nst2], bias=cst[0:QP, 0:1], scale=1.0 / 128)
            ACT(nst2[0:QP, 2 * NB:3 * NB], nst2[0:QP, NB:2 * NB], AF.Exp, [nst2], [nst2], scale=-0.5)
            for qb in range(NB):
                STT("dve", ob[0:QP, qb, :], ob[0:QP, qb, :], nst2[0:QP, 2 * NB + qb:2 * NB + qb + 1], gsub[0:QP, :], ALU.mult, ALU.mult,
                    [ob, nst2, gsub], [ob])
            yield

        def delta_gen(h):
            L_ = Lc
            Estr, Ering, Xb, NWT = NW["Estr"], NW["Ering"], NW["Xb"], NW["NWT"]
            ACT(tk[0:L_, :, 6], tk[0:L_, :, 0], AF.Exp, [tk], [tk], scale=-0.5)
            ACT(tk[0:L_, :, 5], tk[0:L_, :, 1], AF.Exp, [tk], [tk], scale=-0.5)
            TS_("dve", tk[0:L_, :, 6], tk[0:L_, :, 6], 128.0 ** -0.5, None, ALU.mult, None, [tk], [tk])
            CP("dve", tk[0:L_, :, 12], beta_all[0:L_, :, h], [beta_all], [tk])
            CP("dve", tk[0:L_, :, 14], g_all[0:L_, :, h], [g_all], [tk])
            pg = PB[2]
            MM(pg[0:L_, 0:NC], utri[0:L_, 0:L_], tk[0:L_, :, 14], True, True, [utri, tk], [pg])
            MM(pg[0:128, 32:32 + NC], onesf[0:L_, 0:128], tk[0:L_, :, 14], True, True, [onesf, tk], [pg])
            CP("dve", tk[0:L_, :, 2], pg[0:L_, 0:NC], [], [tk, pg])
            CP("dve", tk[:, :, 3], pg[:, 32:32 + NC], [], [tk, pg])
            STT("dve", tk[0:L_, :, 4], tk[0:L_, :, 1], -0.5, tk[0:L_, :, 2], ALU.mult, ALU.subtract, [tk], [tk])
            ACT(tk[0:L_, :, 7], tk[0:L_, :, 2], AF.Exp, [tk], [tk])
            ACT(tk[:, :, 13], tk[:, :, 3], AF.Exp, [tk], [tk])
            TT("dve", tk[0:L_, :, 8], tk[0:L_, :, 12], tk[0:L_, :, 5], ALU.mult, [tk], [tk])
            TT("dve", tk[0:L_, :, 9], tk[0:L_, :, 8], tk[0:L_, :, 7], ALU.mult, [tk], [tk])
            TT("dve", tk[0:L_, :, 10], tk[0:L_, :, 3], tk[0:L_, :, 2], ALU.subtract, [tk], [tk])
            ACT(tk[0:L_, :, 10], tk[0:L_, :, 10], AF.Exp, [tk], [tk])
            TT("dve", tk[0:L_, :, 10], tk[0:L_, :, 10], tk[0:L_, :, 5], ALU.mult, [tk], [tk])
            TT("dve", tk[0:L_, :, 11], tk[0:L_, :, 6], tk[0:L_, :, 7], ALU.mult, [tk], [tk])

            for c in range(NC):
                csl = slice(c * L_, (c + 1) * L_)
                pb = PB[c % 2]
                pv = pbf(c % 2)
                TR(pv[0:L_, 0:128], raw[1][:, csl], identb[:, :], [raw[1], identb], [pb])
                TR(pv[0:L_, 128:256], raw[2][:, csl], identb[:, :], [raw[2], identb], [pb])
                ACT(Rk[0:L_, c, :], pv[0:L_, 0:128], AF.Copy, [tk], [Rk, pb], scale=tk[0:L_, c, 9:10])
                TS_("dve", ktl[0:L_, c, :], pv[0:L_, 0:128], tk[0:L_, c, 10:11], None, ALU.mult, None, [tk], [ktl, pb])
                TS_("dve", vb[0:L_, c, :], pv[0:L_, 128:256], tk[0:L_, c, 12:13], None, ALU.mult, None, [tk], [vb, pb])
                if c % 4 == 3:
                    yield

            for g in range(NG):
                pr1, pkk, pqk = PB[0], PB[1], PB[2]
                for cl in range(CPG):
                    c = g * CPG + cl
                    csl = slice(c * L_, (c + 1) * L_)
                    lsl = slice(cl * L_, (cl + 1) * L_)
                    MM(pr1[0:L_, lsl], tk[0:L_, c, 4:5].to_broadcast([L_, L_]), identf[0:L_, 0:L_], True, False,
                       [tk, identf], [pr1])
                    MM(pr1[0:L_, lsl], identb[0:L_, 0:L_], mincl[0:L_, 0:L_], False, True, [identb, mincl], [pr1])
                    MM(pkk[0:L_, lsl], raw[1][:, csl], raw[1][:, csl], True, True, [raw[1]], [pkk])
                    MM(pqk[0:L_, lsl], raw[0][:, csl], raw[1][:, csl], True, True, [raw[0], raw[1]], [pqk])
                for cl in range(CPG):
                    c = g * CPG + cl
                    lsl = slice(cl * L_, (cl + 1) * L_)
                    ACT(Ecore[0:L_, lsl], pr1[0:L_, lsl], AF.Exp, [tk], [Ecore, pr1], bias=tk[0:L_, c, 2:3])
                for cl in range(CPG):
                    lsl = slice(cl * L_, (cl + 1) * L_)
                    TT("pool", Estr[0:L_, lsl], Ecore[0:L_, lsl], sstrict[0:L_, 0:L_], ALU.mult, [Ecore, sstrict], [Estr])
                X = Xb[0]
                for cl in range(CPG):
                    c = g * CPG + cl
                    lsl = slice(cl * L_, (cl + 1) * L_)
                    STT("dve", X[0:L_, lsl], pkk[0:L_, lsl], tk[0:L_, c, 8:9], Estr[0:L_, lsl], ALU.mult, ALU.mult,
                        [tk, Estr], [X, pkk])
                    TT("dve", qkb[0:L_, lsl], pqk[0:L_, lsl], Ecore[0:L_, lsl], ALU.mult, [Ecore], [qkb, pqk])
                pt_ = PB[3]
                ptv = pbf(3)
                for cl in range(CPG):
                    lsl = slice(cl * L_, (cl + 1) * L_)
                    TR(ptv[0:L_, cl * L_:(cl + 1) * L_], X[0:L_, lsl], identb[0:L_, 0:L_], [X, identb], [pt_])
                    TR(ptv[0:L_, 512 + cl * L_:512 + (cl + 1) * L_], qkb[0:L_, lsl], identb[0:L_, 0:L_], [qkb, identb], [pt_])
                IAt = NWT[g]["IAt"]
                TT("dve", IAt[0:L_, 0:GW], ptv[0:L_, 0:GW], identrep[0:L_, 0:GW], ALU.add, [identrep], [IAt, pt_])
                CP("act", qkT[0:L_, g * GW:(g + 1) * GW], ptv[0:L_, 512:512 + GW], [], [qkT, pt_])
                yield
            Dc = [identrep] * NG
            Dtc = [identrep] * NG
            s_ = 1
            li = 0
            while s_ < L_:
                s2 = 2 * s_
                for g in range(NG):
                    pbk = PB[g % 4]
                    IAt, P1sb = NWT[g]["IAt"], NWT[g]["P1"]
                    for cl in range(CPG):
                        lsl = slice(cl * L_, (cl + 1) * L_)
                        MM(pbk[0:L_, lsl], IAt[0:L_, lsl], Dc[g][0:L_, lsl], True, True, [IAt, Dc[g]], [pbk])
                    CP("act", P1sb[0:L_, 0:GW], pbk[0:L_, 0:GW], [], [P1sb, pbk])
                yield
                Dn_l = []
                for g in range(NG):
                    pbk = PB[g % 4]
                    P1sb = NWT[g]["P1"]
                    Dn = NWT[g]["D"][li % 2]
                    for cl in range(CPG):
                        lsl = slice(cl * L_, (cl + 1) * L_)
                        MM(pbk[0:L_, lsl], Dtc[g][0:L_, lsl], P1sb[0:L_, lsl], True, True, [Dtc[g], P1sb], [pbk])
                    if s2 < L_:
                        tmpm = NWT[g]["P1"]
                        et = Ering[g % 2]
                        TT("dve", et[0:L_, 0:GW].rearrange("p (c f) -> p c f", f=L_),
                           pbk.t[0:L_, 0:GW].rearrange("p (c f) -> p c f", f=L_),
                           bmask[0:L_, li:li + 1, 0:L_].to_broadcast([L_, CPG, L_]), ALU.mult, [bmask], [et, pbk])
                        STT("dve", Dn[0:L_, 0:GW], Dc[g][0:L_, 0:GW], 2.0, et[0:L_, 0:GW], ALU.mult, ALU.subtract,
                            [Dc[g], et], [Dn])
                    else:
                        STT("dve", Dn[0:L_, 0:GW], Dc[g][0:L_, 0:GW], 2.0, pbk[0:L_, 0:GW], ALU.mult, ALU.subtract,
                            [Dc[g]], [Dn, pbk])
                    Dn_l.append(Dn)
                yield
                for g in range(NG):
                    pbk = PB[g % 4]
                    pbv = pbf(g % 4)
                    Dn = Dn_l[g]
                    Dtn = NWT[g]["Dt"]
                    for cl in range(CPG):
                        lsl = slice(cl * L_, (cl + 1) * L_)
                        TR(pbv[0:L_, lsl], Dn[0:L_, lsl], identb[0:L_, 0:L_], [Dn, identb], [pbk])
                    CP("act", Dtn[0:L_, 0:GW], pbv[0:L_, 0:GW], [], [Dtn, pbk])
                    Dc[g] = Dn
                    Dtc[g] = Dtn
                yield
                s_ = s2
                li += 1
            for g in range(NG):
                Qf = Dtc[g]
                CP("pool", Qall[0:L_, g * GW:(g + 1) * GW], Qf[0:L_, 0:GW], [Qf], [Qall])
                pw = PB[g % 4]
                for cl in range(CPG):
                    c = g * CPG + cl
                    lsl = slice(cl * L_, (cl + 1) * L_)
                    MM(pw[:, lsl], Rk[0:L_, c, :], Qf[0:L_, lsl], True, True, [Rk, Qf], [pw])
                ACT(wkT[:, g * GW:(g + 1) * GW], pw[:, 0:GW], AF.Copy, [], [wkT, pw], scale=-1.0)
            yield

            yield "SCAN"
            if isP:
                MEMSET("dve", Sf[:], 0.0, [Sf])
                MEMSET("pool", Sb[:], 0.0, [Sb])
            else:
                DMA(Sf[:], I["sdelta"][l, h], writes=[Sf])
                CP("act", Sb[:], Sf[:], [Sf], [Sb])
            for c in range(NC):
                csl = slice(c * L_, (c + 1) * L_)
                pu, po1, po2, pds = PB[0], PB[1], PB[2], PB[3]
                MM(pu[0:L_, 0:128], Qall[0:L_, csl], vb[0:L_, c, :], True, False, [Qall, vb], [pu])
                MM(pu[0:L_, 0:128], wkT[:, csl], Sb[:, :], False, True, [wkT, Sb], [pu])
                MM(po1[0:L_, 0:128], raw[0][:, csl], Sb[:, :], True, True, [raw[0], Sb], [po1])
                u_ = usb[c % 2]
                CP("act", u_[0:L_, :], pu[0:L_, 0:128], [], [u_, pu])
                yield
                MM(po2[0:L_, 0:128], qkT[0:L_, csl], u_[0:L_, :], True, True, [qkT, u_], [po2])
                MM(pds[:, 0:128], ktl[0:L_, c, :], u_[0:L_, :], True, True, [ktl, u_], [pds])
                o_ = ot[c % 2]
                TS_("dve", o_[0:L_, :], po1[0:L_, 0:128], tk[0:L_, c, 11:12], None, ALU.mult, None, [tk], [o_, po1])
                STT("dve", oa[0:L_, c, :], po2[0:L_, 0:128], tk[0:L_, c, 6:7], o_[0:L_, :], ALU.mult, ALU.add,
                    [tk, o_], [oa, po2])
                STT("dve", Sb[:, :], Sf[:, :], tk[:, c, 13:14], pds[:, 0:128], ALU.mult, ALU.add, [Sf, tk], [Sb, pds])
                STT("dve", Sf[:, :], Sf[:, :], tk[:, c, 13:14], pds[:, 0:128], ALU.mult, ALU.add, [Sf, tk], [Sf, pds])
                yield
            DMA(o_nd[h], Sf[:, :], reads=[Sf])
            if DBG and isP and si == 0 and l == 0 and h == 0:
                DMA(O["dbg_tk"], tk[:, :, :].rearrange("p a b -> p (a b)"), reads=[tk])
                for gi_ in range(3):
                    DMA(O["dbg_raw"][gi_], raw[gi_][:, :], reads=[raw[gi_]])
                DMA(O["dbg_Q"], Qall[:, :], reads=[Qall])
                DMA(O["dbg_qkT"], qkT[:, :], reads=[qkT])
                DMA(O["dbg_wkT"], wkT[:, :], reads=[wkT])
                DMA(O["dbg_oa"], oa[:, :, :].rearrange("p a b -> p (a b)"), reads=[oa])
                DMA(O["dbg_E"][:, 0:GW], Ecore[:, :], reads=[Ecore])
            for b in range(NB):
                ACT(obj[0:P, :], oa[0:P, b, :], AF.Square, [oa], [obj, nst], accum=nst[0:P, b:b + 1])
            ACT(nst[0:P, NB:2 * NB], nst[0:P, 0:NB], AF.Ln, [nst, cst], [nst], bias=cst[0:P, 0:1], scale=1.0 / 128)
            ACT(nst[0:P, 2 * NB:3 * NB], nst[0:P, NB:2 * NB], AF.Exp, [nst], [nst], scale=-0.5)
            for b in range(NB):
                STT("dve", oa[0:P, b, :], oa[0:P, b, :], nst[0:P, 2 * NB + b:2 * NB + b + 1], gna[0:P, :], ALU.mult, ALU.mult,
                    [oa, nst, gna], [oa])
                TT("pool", oa[0:P, b, :], oa[0:P, b, :], zs[0:P, b, :], ALU.mult, [oa, zs], [oa])

        def merge(h):
            if DBG and isP and si == 0 and l == 0 and h == 0:
                DMA(O["dbg_ob"], ob[:, :, :].rearrange("p a b -> p (a b)"), reads=[ob])
            TT("pool", oa[0:P, :, :], oa[0:P, :, :], sga[0:P, :, :], ALU.mult, [oa, sga], [oa])
            TT("dve", ob[0:P, :, :], ob[0:P, :, :], sgb[0:P, :, :], ALU.mult, [ob, sgb], [ob])
            TT("dve", mgall[0:P, :, :], oa[0:P, :, :], ob[0:P, :, :], ALU.add, [oa, ob], [mgall])
            BPB = 8
            for b0 in range(0, NB, BPB):
                bi = (b0 // BPB) % 2
                pb = PB[2 + bi]
                pv = pbf(2 + bi)
                nb_ = min(BPB, NB - b0)
                for bb in range(nb_):
                    TR(pv[:, bb * P:(bb + 1) * P], mgall[0:P, b0 + bb, :], identb[0:P, 0:P], [mgall, identb], [pb])
                ACT(mTh[:, b0 * P:(b0 + nb_) * P], pv[:, 0:nb_ * P], AF.Copy, [], [mTh, pb])
            DMA(mTd[h * 128:(h + 1) * 128, 0:Tq], mTh[:, :], reads=[mTh])

        def interleave(gens, ratios):
            alive = [g for g in gens if g is not None]
            rat = [r for g, r in zip(gens, ratios) if g is not None]
            while alive:
                for gi_ in range(len(alive) - 1, -1, -1):
                    try:
                        for _ in range(rat[gi_]):
                            next(alive[gi_])
                    except StopIteration:
                        alive.pop(gi_)
                        rat.pop(gi_)

        PJ = {}
        NW = {}

        load_weights(0)
        for h in range(H):
            mh = mark()
            PJ["vo"] = [sb([128, 128], F32, "vo%d" % i) for i in range(2)]
            PJ["ko"] = [sb([128, 128], F32, "ko%d" % i) for i in range(2)]
            PJ["pre"] = sb([128, 3 + Tq], F32, "pre")
            PJ["cacc"] = sb([128, Tq], F32, "cacc")
            PJ["sq_"] = sb([128, Tq], BF16, "sq_")
            PJ["cw"] = [sb([128, 4], F32, "cw%d" % i) for i in range(3)]
            PJ["ncs"] = sb([128, 128], F32, "ncs")
            proj(h)
            S.barrier()
            release(mh)
            NW["Estr"] = sb([128, GW], F32, "Estr")
            NW["Ering"] = [NW["Estr"], sb([128, GW], F32, "Estr2")]
            NW["Xb"] = [sb([128, GW], BF16, "Xb0")]
            NW["NWT"] = [dict(IAt=sb([128, GW], BF16, "IAt%d" % g), P1=sb([128, GW], BF16, "P1%d" % g),
                              D=[sb([128, GW], BF16, "Da%d" % g), sb([128, GW], BF16, "Db%d" % g)],
                              Dt=sb([128, GW], BF16, "Dt%d" % g)) for g in range(NG)]
            if h + 1 < H:
                load_weights(h + 1)
            ga = attn_gen(h) if cfg.do_attn else None
            gd = delta_gen(h) if cfg.do_delta else None
            if not cfg.do_attn:
                MEMSET("dve", ob[:], 0.0, [ob])
            if not cfg.do_delta:
                MEMSET("dve", oa[:], 0.0, [oa])
            interleave([ga, gd], [1, 1])
            merge(h)
            S.barrier()
            release(mh)

    for sq in seqs:
        Tq, P = sq["T"], sq["P"]
        NB = Tq // P
        isP = sq["kind"] == "p"
        x_in = I["xp"][sq["i"]] if isP else I["xs"][0]
        y_out = O["yp"][sq["i"]] if isP else O["ys"][0]
        TT_W = min(512, Tq)
        NTT = Tq // TT_W
        BPT = TT_W // P

        m0 = mark()
        gam = sb([128, D], F32, "gam")
        bet = sb([128, D], F32, "bet")
        bcast_row(gam, I["ln_in_g"])
        bcast_row(bet, I["ln_in_b"])
        xt = [sb([128, D], F32, "xt%d" % i) for i in range(2)]
        scr = [sb([128, 8], F32, "scr%d" % i) for i in range(2)]
        junk = sb([128, D], BF16, "junk")
        for b in range(NB):
            x = xt[b % 2]
            DMA(x[0:P, :], x_in[b * P:(b + 1) * P, :], writes=[x])
            layer_norm_block(P, x[0:P, :], x, x[0:P, :], x, gam, bet, scr[b % 2], junk)
            DMA(xsc[b * P:(b + 1) * P, :], x[0:P, :], reads=[x])
        S.barrier()
        release(m0)

        for l in range(L):
            last_layer = (l == L - 1)
            mlay = mark()
            hT = sb([128, 8, Tq], BF16, "hT")
            m0 = mark()
            opsc = sb([128, D], F32, "opsc")
            sh = sb([128, D], F32, "sh")
            bcast_row(sh, modd[l, sq["c"], 0:D])
            bcast_row(opsc, modd[l, sq["c"], D:2 * D])
            TS_("dve", opsc[:], opsc[:], 1.0, None, ALU.add, None, [opsc], [opsc])
            xt = [sb([128, D], F32, "xt%d" % i) for i in range(4)]
            rt = [sb([128, D], F32, "rt%d" % i) for i in range(4)]
            hb = [sb([128, D], BF16, "hb%d" % i) for i in range(4)]
            for b in range(NB):
                x = xt[b % 4]
                DMA(x[0:P, :], xsc[b * P:(b + 1) * P, :], writes=[x])
                build_hT(P, b, x, opsc, sh, rt[b % 4], hb[b % 4], hT)
            S.barrier()
            release(m0)

            m0 = mark()
            mixer(sq, l, hT)
            S.barrier()
            release(m0)

            m0 = mark()
            mT = sb([128, 8, Tq], BF16, "mT")
            DMA(mT[:], mTd[:, 0:Tq].rearrange("(k p) t -> p k t", p=128), writes=[mT])
            wo = sb([128, 8, D], BF16, "wo")
            DMA(wo[:], I["w_o"][l].rearrange("(k p) c -> p k c", p=128), writes=[wo], q="pool")
            gt = sb([128, D], F32, "gt1")
            gam = sb([128, D], F32, "gam1")
            bet = sb([128, D], F32, "bet1")
            opsc = sb([128, D], F32, "opsc2")
            sh = sb([128, D], F32, "sh2")
            bcast_row(gt, modd[l, sq["c"], 2 * D:3 * D])
            bcast_row(gam, I["ln1_g"][l])
            bcast_row(bet, I["ln1_b"][l])
            bcast_row(sh, modd[l, sq["c"], 3 * D:4 * D])
            bcast_row(opsc, modd[l, sq["c"], 4 * D:5 * D])
            TS_("dve", opsc[:], opsc[:], 1.0, None, ALU.add, None, [opsc], [opsc])
            xt = [sb([128, D], F32, "xt%d" % i) for i in range(4)]
            rt = [sb([128, D], F32, "rt%d" % i) for i in range(4)]
            hb = [sb([128, D], BF16, "hb%d" % i) for i in range(4)]
            scr = [sb([128, 8], F32, "scr%d" % i) for i in range(4)]
            junk = sb([128, D], BF16, "junk")
            for b in range(NB):
                x, r, hbb = xt[b % 4], rt[b % 4], hb[b % 4]
                DMA(x[0:P, :], xsc[b * P:(b + 1) * P, :], writes=[x])
                for half in range(2):
                    pbk = PB[(b % 2) * 2 + half]
                    for k in range(8):
                        MM(pbk[0:P, :], mT[:, k, b * P:(b + 1) * P], wo[:, k, half * 512:(half + 1) * 512],
                           k == 0, k == 7, [mT, wo], [pbk])
                    TT("dve", r[0:P, half * 512:(half + 1) * 512], pbk[0:P, :], gt[0:P, half * 512:(half + 1) * 512],
                       ALU.mult, [gt], [r, pbk])
                STT("dve", r[0:P, :], x[0:P, :], ALPHA, r[0:P, :], ALU.mult, ALU.add, [x, r], [r])
                layer_norm_block(P, r[0:P, :], r, x[0:P, :], x, gam, bet, scr[b % 4], junk)
                DMA(xsc[b * P:(b + 1) * P, :], x[0:P, :], reads=[x])
                build_hT(P, b, x, opsc, sh, r, hbb, hT)
            S.barrier()
            release(m0)

            m0 = mark()
            wout = sb([128, 22, D], BF16, "wout")
            for j in range(22):
                DMA(wout[:, j, :], I["w_ff_out"][l][j * 128:(j + 1) * 128, :], writes=[wout], q="pool")
            gt = sb([128, D], F32, "gt2")
            gam = sb([128, D], F32, "gam2")
            bet = sb([128, D], F32, "bet2")
            bcast_row(gt, modd[l, sq["c"], 5 * D:6 * D])
            bcast_row(gam, I["ln2_g"][l])
            bcast_row(bet, I["ln2_b"][l])
            wfi = [sb([128, 8, 256], BF16, "wfi%d" % i) for i in range(3)]
            TF = min(1024, Tq)
            NTF = Tq // TF
            NH = TF // TT_W
            BPF = TF // P
            gT = sb([128, 22, TF], BF16, "gT")
            su = [sb([128, TT_W], BF16, "su%d" % i) for i in range(2)]
            xt = [sb([128, D], F32, "xt%d" % i) for i in range(4)]
            rt = [sb([128, D], F32, "rt%d" % i) for i in range(4)]
            scr = [sb([128, 8], F32, "scr%d" % i) for i in range(4)]
            junk = sb([128, D], BF16, "junk")
            wsrc = I["w_ff_in"][l].rearrange("(k p) c -> p k c", p=128)
            fcnt = 0
            for tt in range(NTF):
                for j in range(22):
                    w = wfi[j % 3]
                    DMA(w[:, :, 0:128], wsrc[:, :, j * 128:(j + 1) * 128], writes=[w], q="pool")
                    DMA(w[:, :, 128:256], wsrc[:, :, DFF + j * 128:DFF + (j + 1) * 128], writes=[w], q="pool")
                    for hf in range(NH):
                        tsl = slice(tt * TF + hf * TT_W, tt * TF + (hf + 1) * TT_W)
                        gsl = slice(hf * TT_W, (hf + 1) * TT_W)
                        pu, pv_ = PB[4 + 2 * (fcnt % 2)], PB[5 + 2 * (fcnt % 2)]
                        for k in range(8):
                            MM(pu[:, 0:TT_W], w[:, k, 0:128], hT[:, k, tsl], k == 0, k == 7, [w, hT], [pu])
                        for k in range(8):
                            MM(pv_[:, 0:TT_W], w[:, k, 128:256], hT[:, k, tsl], k == 0, k == 7, [w, hT], [pv_])
                        s_ = su[fcnt % 2]
                        fcnt += 1
                        ACT(s_[:, :], pu[:, 0:TT_W], AF.Silu, [], [s_, pu])
                        TT("dve", gT[:, j, gsl], pv_[:, 0:TT_W], s_[:, :], ALU.mult, [s_], [gT, pv_])
                for bb in range(BPF):
                    b = tt * BPF + bb
                    x, r = xt[b % 4], rt[b % 4]
                    DMA(x[0:P, :], xsc[b * P:(b + 1) * P, :], writes=[x])
                    for half in range(2):
                        pbk = PB[(b % 2) * 2 + half]
                        for j in range(22):
                            MM(pbk[0:P, :], gT[:, j, bb * P:(bb + 1) * P], wout[:, j, half * 512:(half + 1) * 512],
                               j == 0, j == 21, [gT, wout], [pbk])
                        TT("dve", r[0:P, half * 512:(half + 1) * 512], pbk[0:P, :], gt[0:P, half * 512:(half + 1) * 512],
                           ALU.mult, [gt], [r, pbk])
                    STT("dve", r[0:P, :], x[0:P, :], ALPHA, r[0:P, :], ALU.mult, ALU.add, [x, r], [r])
                    layer_norm_block(P, r[0:P, :], r, x[0:P, :], x, gam, bet, scr[b % 4], junk)
                    if last_layer:
                        DMA(y_out[b * P:(b + 1) * P, :], x[0:P, :], reads=[x])
                    else:
                        DMA(xsc[b * P:(b + 1) * P, :], x[0:P, :], reads=[x])
            S.barrier()
            release(mlay)

    print('arena max words', amax[0], 'ops', S.nops)
    S.emit(ctx)
    return nc, consts


_CACHE = {}


def _get_program(cfg_key, cfg):
    if cfg_key not in _CACHE:
        _CACHE[cfg_key] = build_program(cfg)
    return _CACHE[cfg_key]


def kernel(x_prompt, x_sample, cache_k, cache_v, state_conv, state_delta, c_prompt, c_sample,
           ln_in_g, ln_in_b, rel_bias, w_ada, b_ada, w_in, conv_w, a_log, dt_bias, norm_a,
           lam, subln_g, w_o, ln1_g, ln1_b, w_ff_in, w_ff_out, ln2_g, ln2_b):
    NCORE = 8
    f = lambda a: np.ascontiguousarray(np.asarray(a, dtype=np.float32))
    x_prompt, x_sample, cache_k, cache_v = f(x_prompt), f(x_sample), f(cache_k), f(cache_v)
    state_conv, state_delta, c_prompt, c_sample = f(state_conv), f(state_delta), f(c_prompt), f(c_sample)
    B, T, _ = x_prompt.shape
    BS, TS, _ = x_sample.shape
    Ld = w_in.shape[0]
    PAST = cache_k.shape[2]
    NP = B // NCORE
    cfg = Cfg(depth=Ld, n_prompt=NP, T=T, with_sample=True, TS=TS, past=PAST)
    nc, consts = _get_program(("full", Ld, NP, T, TS, PAST), cfg)
    shared = {}
    for nm, arr in [("ln_in_g", ln_in_g), ("ln_in_b", ln_in_b), ("rel_bias", rel_bias), ("w_ada", w_ada), ("b_ada", b_ada),
                    ("w_in", w_in), ("conv_w", conv_w), ("a_log", a_log), ("dt_bias", dt_bias), ("norm_a", norm_a),
                    ("lam", lam), ("subln_g", subln_g), ("w_o", w_o), ("ln1_g", ln1_g), ("ln1_b", ln1_b),
                    ("w_ff_in", w_ff_in), ("w_ff_out", w_ff_out), ("ln2_g", ln2_g), ("ln2_b", ln2_b)]:
        shared[nm] = f(arr)
    for nm, arr in consts.items():
        shared["c_" + nm] = arr
    in_maps = []
    for c in range(NCORE):
        m = dict(shared)
        m["xp"] = x_prompt[c * NP:(c + 1) * NP]
        m["xs"] = x_sample[c:c + 1]
        m["cvec"] = np.ascontiguousarray(np.concatenate([c_prompt[c * NP:(c + 1) * NP], c_sample[c:c + 1]], axis=0))
        m["ck"] = np.ascontiguousarray(cache_k[:, c].reshape(Ld, PAST, D))
        m["cv"] = np.ascontiguousarray(cache_v[:, c].reshape(Ld, PAST, D))
        m["sconv"] = np.ascontiguousarray(state_conv[:, c])
        m["sdelta"] = np.ascontiguousarray(state_delta[:, c])
        in_maps.append(m)
    res = run_bass_kernel_spmd(nc, in_maps, core_ids=list(range(NCORE)))
    R = res.results
    cat = lambda key, ax: np.concatenate([np.asarray(r[key]) for r in R], axis=ax)
    y_p = cat("yp", 0)
    y_s = cat("ys", 0)
    nk_p = cat("nkp", 1).reshape(Ld, B, T, H, 128)
    nv_p = cat("nvp", 1).reshape(Ld, B, T, H, 128)
    nc_p = cat("ncp", 1)
    nd_p = cat("ndp", 1)
    nk_s = cat("nks", 1).reshape(Ld, BS, TS, H, 128)
    nv_s = cat("nvs", 1).reshape(Ld, BS, TS, H, 128)
    nc_s = cat("ncs", 1)
    nd_s = cat("nds", 1)
    return (y_p.astype(np.float32), y_s.astype(np.float32), nk_p.astype(np.float32), nv_p.astype(np.float32),
            nc_p.astype(np.float32), nd_p.astype(np.float32), nk_s.astype(np.float32), nv_s.astype(np.float32),
            nc_s.astype(np.float32), nd_s.astype(np.float32))
```
